# Optimizing a Trainium2 kernel written in Bass

```python
import math
import jax, jax.numpy as jnp
from jax import lax
import numpy as np

D_MODEL = 1024
BATCH = 8
SEQ = 2048
DEPTH = 1
DEC_BATCH = 128
DEC_SEQ = 8
PAST_LEN = 8192
PAGE_SIZE = 128

ATT_HEADS = 16
ATT_KV_HEADS = 4
ATT_GROUP = ATT_HEADS // ATT_KV_HEADS
HEAD_DIM = 64
ATT_WIDTH = ATT_HEADS * HEAD_DIM
KV_WIDTH = ATT_KV_HEADS * HEAD_DIM
WINDOW = 128
CACHE_WIN = min(WINDOW, PAST_LEN)
ROT_DIM = HEAD_DIM // 4
ROPE_THETA = 500000.0
SSM_HEADS = 16
SSM_HEAD_DIM = 64
SSM_WIDTH = SSM_HEADS * SSM_HEAD_DIM
SSM_GROUPS = 2
SSM_HPG = SSM_HEADS // SSM_GROUPS
SSM_GROUP_W = SSM_WIDTH // SSM_GROUPS
D_STATE = 128
CONV_W = 4
SSM_CHUNK = 128
CONV_DIM = SSM_WIDTH + 2 * SSM_GROUPS * D_STATE
MIX_WIDTH = ATT_WIDTH + SSM_WIDTH
IN_PROJ = ATT_WIDTH + 2 * KV_WIDTH + SSM_WIDTH + CONV_DIM + SSM_HEADS
SPLIT_IDX = [ATT_WIDTH, ATT_WIDTH + KV_WIDTH, ATT_WIDTH + 2 * KV_WIDTH,
             ATT_WIDTH + 2 * KV_WIDTH + SSM_WIDTH,
             ATT_WIDTH + 2 * KV_WIDTH + SSM_WIDTH + CONV_DIM]
FF_RAW = -(-8 * D_MODEL // 3)
D_FF = -(-FF_RAW // 256) * 256
EPS = 1e-6

kernel_name = "hymba_swa_sink_ssd_decode_step"

F32 = jnp.float32


def rmsnorm(x, g):
    xf = x.astype(F32)
    y = xf * lax.rsqrt(jnp.mean(xf * xf, axis=-1, keepdims=True) + EPS)
    return (y * g.astype(F32)).astype(x.dtype)


def rope_partial(x, pos):
    half = ROT_DIM // 2
    inv = ROPE_THETA ** (-jnp.arange(half, dtype=F32) * 2.0 / ROT_DIM)
    ang = pos.astype(F32)[:, None] * inv[None, :]
    cos = jnp.cos(ang)[None, :, None, :]
    sin = jnp.sin(ang)[None, :, None, :]
    xr = x[..., :ROT_DIM].astype(F32)
    x1, x2 = xr[..., :half], xr[..., half:]
    rot = jnp.concatenate([x1 * cos - x2 * sin, x2 * cos + x1 * sin], axis=-1)
    return jnp.concatenate([rot.astype(x.dtype), x[..., ROT_DIM:]], axis=-1)


def sink_attention(q, k, v, valid, sinks):
    s = jnp.einsum('bnqkgd,bnskd->bnkgqs', q.astype(F32), k.astype(F32)) * (HEAD_DIM ** -0.5)
    s = jnp.where(valid[None, :, None, None], s, -jnp.inf)
    sink = sinks.astype(F32).reshape(1, 1, ATT_KV_HEADS, ATT_GROUP, 1, 1)
    m = jnp.maximum(jnp.max(s, axis=-1, keepdims=True), sink)
    p = jnp.exp(s - m)
    den = jnp.sum(p, axis=-1, keepdims=True) + jnp.exp(sink - m)
    o = jnp.einsum('bnkgqs,bnskd->bnqkgd', p / den, v.astype(F32))
    return o.astype(q.dtype)


def swa_prompt(q, k, v, sinks):
    b, L = q.shape[:2]
    nb = L // WINDOW
    qb = q.reshape(b, nb, WINDOW, ATT_KV_HEADS, ATT_GROUP, HEAD_DIM)
    kb = k.reshape(b, nb, WINDOW, ATT_KV_HEADS, HEAD_DIM)
    vb = v.reshape(b, nb, WINDOW, ATT_KV_HEADS, HEAD_DIM)
    pad = ((0, 0), (1, 0), (0, 0), (0, 0), (0, 0))
    kk = jnp.concatenate([jnp.pad(kb[:, :-1], pad), kb], axis=2)
    vv = jnp.concatenate([jnp.pad(vb[:, :-1], pad), vb], axis=2)
    i = jnp.arange(WINDOW)[:, None]
    j = jnp.arange(2 * WINDOW)[None, :]
    d = WINDOW + i - j
    valid_rel = (d >= 0) & (d <= WINDOW)
    valid = valid_rel[None] & ((jnp.arange(nb)[:, None, None] > 0) | (j >= WINDOW)[None])
    o = sink_attention(qb, kk, vv, valid, sinks)
    return o.reshape(b, L, ATT_WIDTH)


def swa_sample(q, k, v, k_cache, v_cache, sinks):
    b, t = q.shape[:2]
    kk = jnp.concatenate([k_cache.astype(k.dtype), k], axis=1)
    vv = jnp.concatenate([v_cache.astype(v.dtype), v], axis=1)
    qpos = jnp.arange(t)[:, None]
    kpos = jnp.concatenate([jnp.arange(CACHE_WIN) - CACHE_WIN, jnp.arange(t)])[None, :]
    d = qpos - kpos
    valid = ((d >= 0) & (d <= WINDOW))[None]
    qb = q.reshape(b, 1, t, ATT_KV_HEADS, ATT_GROUP, HEAD_DIM)
    o = sink_attention(qb, kk[:, None], vv[:, None], valid, sinks)
    return o.reshape(b, t, ATT_WIDTH), kk[:, -CACHE_WIN:], vv[:, -CACHE_WIN:]


def causal_conv_silu(xpad, w, bias):
    L = xpad.shape[1] - (CONV_W - 1)
    acc = bias + xpad[:, 0:L] * w[0]
    for tap in range(1, CONV_W):
        acc = acc + xpad[:, tap:tap + L] * w[tap]
    return jax.nn.silu(acc)


def ssd_scan(x, dt, a, bm, cm, h0, chunk):
    b, L = x.shape[:2]
    nc = L // chunk

    def to_chunks(t):
        return t.reshape((b, nc, chunk) + t.shape[2:]).swapaxes(0, 1)

    xs = to_chunks(x.reshape(b, L, SSM_GROUPS, SSM_HPG, SSM_HEAD_DIM))
    dts = to_chunks(dt.reshape(b, L, SSM_GROUPS, SSM_HPG))
    bs = to_chunks(bm)
    cs = to_chunks(cm)
    ag = a.reshape(SSM_GROUPS, SSM_HPG)
    causal = jnp.tril(jnp.ones((chunk, chunk), dtype=bool))

    def step(h, inp):
        xc, dtc, bc, cc = inp
        acum = jnp.cumsum(dtc * ag, axis=1)
        seg = acum[:, :, None] - acum[:, None, :]
        decay = jnp.exp(jnp.where(causal[None, :, :, None, None], seg, -jnp.inf))
        cb = jnp.einsum('btgn,bsgn->btsg', cc, bc)
        wts = cb[..., None] * decay * dtc[:, None]
        y = jnp.einsum('btsge,bsgep->btgep', wts, xc)
        hg = h.reshape(b, SSM_GROUPS, SSM_HPG, SSM_HEAD_DIM, D_STATE)
        y = y + jnp.einsum('btgn,bgepn->btgep', cc, hg) * jnp.exp(acum)[..., None]
        tail = jnp.exp(acum[:, -1:] - acum) * dtc
        h_new = hg * jnp.exp(acum[:, -1])[..., None, None] + jnp.einsum('bsge,bsgep,bsgn->bgepn', tail, xc, bc)
        return h_new.reshape(b, SSM_HEADS, SSM_HEAD_DIM, D_STATE), y

    h, ys = lax.scan(step, h0, (xs, dts, bs, cs))
    y = ys.swapaxes(0, 1).reshape(b, L, SSM_HEADS, SSM_HEAD_DIM)
    return y, h


def decoder_layer(x, pos, k_cache, v_cache, conv_state, ssm_state, chunk,
                  g_pre_mix, w_in, sinks, conv_w, conv_b, dt_bias, a_log, d_skip, g_ssm_out,
                  w_out, g_post_mix, g_pre_ffn, w_gate, w_up, w_down, g_post_ffn):
    b, L, _ = x.shape
    h = rmsnorm(x, g_pre_mix)
    q, k, v, z, xbc, dt_raw = jnp.split(h @ w_in, SPLIT_IDX, axis=-1)
    q = rope_partial(q.reshape(b, L, ATT_HEADS, HEAD_DIM), pos)
    k = rope_partial(k.reshape(b, L, ATT_KV_HEADS, HEAD_DIM), pos)
    v = v.reshape(b, L, ATT_KV_HEADS, HEAD_DIM)
    if k_cache is None:
        o_att = swa_prompt(q, k, v, sinks)
        new_k, new_v = k[:, -CACHE_WIN:], v[:, -CACHE_WIN:]
    else:
        o_att, new_k, new_v = swa_sample(q, k, v, k_cache, v_cache, sinks)
    xpad = jnp.concatenate([conv_state.astype(xbc.dtype), xbc], axis=1)
    new_conv = xpad[:, -(CONV_W - 1):]
    xbc = causal_conv_silu(xpad, conv_w, conv_b)
    xs, bmat, cmat = jnp.split(xbc, [SSM_WIDTH, SSM_WIDTH + SSM_GROUPS * D_STATE], axis=-1)
    dt = jax.nn.softplus(dt_raw.astype(F32) + dt_bias.astype(F32))
    a = -jnp.exp(a_log.astype(F32))
    xh = xs.reshape(b, L, SSM_HEADS, SSM_HEAD_DIM).astype(F32)
    y, new_ssm = ssd_scan(xh, dt, a,
                          bmat.reshape(b, L, SSM_GROUPS, D_STATE).astype(F32),
                          cmat.reshape(b, L, SSM_GROUPS, D_STATE).astype(F32),
                          ssm_state.astype(F32), chunk)
    y = y + d_skip.astype(F32)[:, None] * xh
    gated = y.reshape(b, L, SSM_GROUPS, SSM_GROUP_W) * jax.nn.silu(z.astype(F32)).reshape(b, L, SSM_GROUPS, SSM_GROUP_W)
    gated = gated * lax.rsqrt(jnp.mean(gated * gated, axis=-1, keepdims=True) + EPS)
    o_ssm = (gated.reshape(b, L, SSM_WIDTH) * g_ssm_out.astype(F32)).astype(x.dtype)
    mix = jnp.concatenate([o_att.astype(x.dtype), o_ssm], axis=-1) @ w_out
    x = x + rmsnorm(mix, g_post_mix)
    f = rmsnorm(x, g_pre_ffn)
    f = (jax.nn.silu(f @ w_gate) * (f @ w_up)) @ w_down
    x = x + rmsnorm(f, g_post_ffn)
    return x, new_k, new_v, new_conv, new_ssm


def setup_inputs(seed: int = 0) -> dict:
    key = jax.random.key(seed)
    ks = jax.random.split(key, 24)

    def nrm(k, shape, scale):
        return jax.random.normal(k, shape, F32) * scale

    dt0 = jnp.exp(jax.random.uniform(ks[10], (DEPTH, SSM_HEADS), F32, math.log(1e-3), math.log(1e-1)))
    return {
        "x_prompt": nrm(ks[0], (BATCH, SEQ, D_MODEL), 1.0),
        "x_sample": nrm(ks[1], (DEC_BATCH, DEC_SEQ, D_MODEL), 1.0),
        "cache_k_win": nrm(ks[2], (DEPTH, DEC_BATCH, CACHE_WIN, ATT_KV_HEADS, HEAD_DIM), 1.0),
        "cache_v_win": nrm(ks[3], (DEPTH, DEC_BATCH, CACHE_WIN, ATT_KV_HEADS, HEAD_DIM), 1.0),
        "state_conv": nrm(ks[4], (DEPTH, DEC_BATCH, CONV_W - 1, CONV_DIM), 1.0),
        "state_ssm": nrm(ks[5], (DEPTH, DEC_BATCH, SSM_HEADS, SSM_HEAD_DIM, D_STATE), 0.1),
        "g_pre_mix": 1.0 + nrm(ks[6], (DEPTH, D_MODEL), 0.02),
        "w_in": nrm(ks[7], (DEPTH, D_MODEL, IN_PROJ), D_MODEL ** -0.5),
        "attn_sinks": nrm(ks[8], (DEPTH, ATT_HEADS), 1.0),
        "conv_w": nrm(ks[9], (DEPTH, CONV_W, CONV_DIM), CONV_W ** -0.5),
        "conv_b": nrm(ks[11], (DEPTH, CONV_DIM), 0.02),
        "dt_bias": dt0 + jnp.log(-jnp.expm1(-dt0)),
        "a_log": jnp.log(jax.random.uniform(ks[12], (DEPTH, SSM_HEADS), F32, 1.0, 16.0)),
        "d_skip": 1.0 + nrm(ks[13], (DEPTH, SSM_HEADS), 0.02),
        "g_ssm_out": 1.0 + nrm(ks[14], (DEPTH, SSM_WIDTH), 0.02),
        "w_out": nrm(ks[15], (DEPTH, MIX_WIDTH, D_MODEL), MIX_WIDTH ** -0.5),
        "g_post_mix": 1.0 + nrm(ks[16], (DEPTH, D_MODEL), 0.02),
        "g_pre_ffn": 1.0 + nrm(ks[17], (DEPTH, D_MODEL), 0.02),
        "w_gate": nrm(ks[18], (DEPTH, D_MODEL, D_FF), D_MODEL ** -0.5),
        "w_up": nrm(ks[19], (DEPTH, D_MODEL, D_FF), D_MODEL ** -0.5),
        "w_down": nrm(ks[20], (DEPTH, D_FF, D_MODEL), D_FF ** -0.5),
        "g_post_ffn": 1.0 + nrm(ks[21], (DEPTH, D_MODEL), 0.02),
    }


def reference(x_prompt, x_sample, cache_k_win, cache_v_win, state_conv, state_ssm,
              g_pre_mix, w_in, attn_sinks, conv_w, conv_b, dt_bias, a_log, d_skip, g_ssm_out,
              w_out, g_post_mix, g_pre_ffn, w_gate, w_up, w_down, g_post_ffn):
    bp, lp = x_prompt.shape[:2]
    ts = x_sample.shape[1]
    pos_p = jnp.arange(lp, dtype=jnp.int32)
    pos_s = PAST_LEN + jnp.arange(ts, dtype=jnp.int32)
    hp, hs = x_prompt, x_sample
    kp_l, vp_l, cp_l, sp_l = [], [], [], []
    ks_l, vs_l, cs_l, ss_l = [], [], [], []
    for l in range(DEPTH):
        w = (g_pre_mix[l], w_in[l], attn_sinks[l], conv_w[l], conv_b[l], dt_bias[l], a_log[l],
             d_skip[l], g_ssm_out[l], w_out[l], g_post_mix[l], g_pre_ffn[l], w_gate[l], w_up[l],
             w_down[l], g_post_ffn[l])
        conv0 = jnp.zeros((bp, CONV_W - 1, CONV_DIM), x_prompt.dtype)
        ssm0 = jnp.zeros((bp, SSM_HEADS, SSM_HEAD_DIM, D_STATE), F32)
        hp, kp, vp, cp, sp = decoder_layer(hp, pos_p, None, None, conv0, ssm0, SSM_CHUNK, *w)
        hs, ksm, vsm, csm, ssm = decoder_layer(hs, pos_s, cache_k_win[l], cache_v_win[l],
                                               state_conv[l], state_ssm[l], ts, *w)
        kp_l.append(kp); vp_l.append(vp); cp_l.append(cp); sp_l.append(sp)
        ks_l.append(ksm); vs_l.append(vsm); cs_l.append(csm); ss_l.append(ssm)
    return (hp, hs,
            jnp.stack(kp_l), jnp.stack(vp_l), jnp.stack(cp_l), jnp.stack(sp_l),
            jnp.stack(ks_l), jnp.stack(vs_l), jnp.stack(cs_l), jnp.stack(ss_l))
```

```python
import numpy as np
import contextlib
import os
import concourse.bass as bass
import concourse.mybir as mybir
from concourse.bass_utils import run_bass_kernel_spmd

F32 = mybir.dt.float32
BF16 = mybir.dt.bfloat16
AF = mybir.ActivationFunctionType
ALU = mybir.AluOpType
AX = mybir.AxisListType

NEG = -30000.0
EPS = 1e-6
D = 1024
DFF = 2816
NIN = 4112
SB_BASE = 16512
SB_LIMIT = 229344


class Reg:
    __slots__ = ("w", "r", "name", "rng")

    def __init__(self, name, rng=None):
        self.w = None
        self.r = []
        self.name = name
        self.rng = rng


class Sched:
    ENG = ("pe", "act", "dve", "pool", "sp")

    def __init__(self, nc, stack):
        self.nc = nc
        self.h = {"pe": nc.tensor, "act": nc.scalar, "dve": nc.vector, "pool": nc.gpsimd, "sp": nc.sync}
        self.esem = {e: stack.enter_context(nc.semaphore("sem_" + e)) for e in self.ENG}
        self.ecnt = {e: 0 for e in self.ENG}
        self.known = {e: {} for e in self.ENG}
        self.prog = {e: [] for e in self.ENG}
        self.dsem = {}
        self.stack = stack
        self.semobj = {}
        self.ranged = []
        self.use_barriers = bool(os.environ.get("KBAR"))
        for e in self.ENG:
            self.semobj["sem_" + e] = self.esem[e]

    def dma_sem(self, key):
        if key not in self.dsem:
            s = self.stack.enter_context(self.nc.semaphore("dq_" + key))
            self.dsem[key] = [s, 0]
            self.semobj["dq_" + key] = s
        return self.dsem[key]

    def _waits(self, e, deps):
        waits = []
        kn = self.known[e]
        for k, v in deps.items():
            if kn.get(k, 0) < v:
                kn[k] = v
                waits.append((self.semobj[k], v))
        return waits

    def task(self, e, fn, reads=(), writes=(), dma=None, ndma=1):
        deps = {}

        def add(d):
            if d is not None and deps.get(d[0], 0) < d[1]:
                deps[d[0]] = d[1]
        for R in reads:
            add(R.w)
        for R in writes:
            add(R.w)
            for d in R.r:
                add(d)
            if R.rng:
                for Q in self.ranged:
                    if Q is R or (Q.w is None and not Q.r):
                        continue
                    hit = False
                    for (a0, a1) in R.rng:
                        for (b0, b1) in Q.rng:
                            if a0 < b1 and b0 < a1:
                                hit = True
                    if hit:
                        add(Q.w)
                        for d in Q.r:
                            add(d)
        waits = self._waits(e, deps)
        if dma is not None:
            ds = self.dma_sem(dma)
            ds[1] += 16 * ndma
            my = ("dq_" + dma, ds[1])
            sem = ds[0]
        else:
            self.ecnt[e] += 1
            my = ("sem_" + e, self.ecnt[e])
            sem = self.esem[e]
        for R in reads:
            R.r.append(my)
        for R in writes:
            R.w = my
            R.r = []
        self.prog[e].append((waits, fn, sem, dma is not None))

    def chain(self, e, fns, reads=(), writes=()):
        ch = Reg("chain")
        for f in fns:
            self.task(e, f, reads=list(reads), writes=list(writes) + [ch])

    def barrier(self, force=False):
        if not (force or self.use_barriers):
            return
        deps = {}
        for e in self.ENG:
            if self.ecnt[e]:
                deps["sem_" + e] = self.ecnt[e]
        for k, (s, c) in self.dsem.items():
            if c:
                deps["dq_" + k] = c
        for e in self.ENG:
            w = self._waits(e, dict(deps))
            if w:
                self.prog[e].append((w, None, None, False))

    def emit(self):
        nc = self.nc
        with nc.Block() as block:
            def run(e):
                def body(eng):
                    for waits, fn, sem, isdma in self.prog[e]:
                        for s, v in waits:
                            eng.wait_ge(s, v)
                        if fn is None:
                            continue
                        if isdma:
                            fn(eng, sem)
                        else:
                            last = fn(eng)
                            last.then_inc(sem, 1)
                return body
            block.tensor(run("pe"))
            block.scalar(run("act"))
            block.vector(run("dve"))
            block.gpsimd(run("pool"))
            block.sync(run("sp"))


def _host_consts():
    i = np.arange(128)
    seq = i // 8
    tri_p = (i[:, None] <= i[None, :]).astype(np.float32)
    same = (seq[:, None] == seq[None, :])
    tri_s = tri_p * same
    U = (i[:, None] > i[None, :]).astype(np.float32)
    ident = np.eye(128, dtype=np.float32)
    lastmask = (i[None, :] == (8 * seq[:, None] + 7)).astype(np.float32)
    oh = (seq[:, None] == np.arange(16)[None, :]).astype(np.float32)
    ones = np.ones((128, 128), np.float32)
    row0 = np.zeros((128, 1), np.float32)
    row0[0, 0] = 1.0
    cf = np.concatenate([ident, tri_p, tri_s, U, lastmask, oh, ones, row0], axis=1)
    mprev = np.where(i[:, None] >= i[None, :], 0.0, NEG)
    mcur = np.where(i[:, None] <= i[None, :], 0.0, NEG)
    mcur_s = np.where(same & (i[:, None] <= i[None, :]), 0.0, NEG)
    t8 = np.arange(8)
    mc = np.where(i[:, None] >= t8[None, :], 0.0, NEG)
    prot = np.zeros((128, 128), np.float32)
    for m in range(128):
        d = m % 64
        base = m - d
        if d < 8:
            prot[base + d + 8, m] = 1.0
        elif d < 16:
            prot[base + d - 8, m] = 1.0
    sel16 = np.zeros((128, 16, 128), np.float32)
    for h in range(16):
        sel16[h, h, :] = 1.0
    selpair = np.zeros((128, 8, 128), np.float32)
    for pr in range(8):
        selpair[2 * pr, pr, 0:64] = 1.0
        selpair[2 * pr + 1, pr, 64:128] = 1.0
    sinkE = np.zeros((128, 128), np.float32)
    sinkE[0, 64:128] = 1.0
    sinkO = np.zeros((128, 128), np.float32)
    sinkO[0, 0:64] = 1.0
    cb = np.concatenate([ident, prot, np.tile(mprev, (1, 4)), np.tile(mcur, (1, 4)), np.tile(mcur_s, (1, 4)),
                         np.tile(mc, (1, 64)), sel16.reshape(128, -1), selpair.reshape(128, -1), sinkE, sinkO,
                         ones, U], axis=1).astype(np.float32)
    pos = np.concatenate([np.arange(2048), 8192 + (np.arange(128) % 8)]).astype(np.float32)
    half = 8
    inv = (500000.0 ** (-np.arange(half, dtype=np.float32) * 2.0 / 16)).astype(np.float32)
    ang = pos[None, :] * inv[:, None]
    cosv = np.cos(ang).astype(np.float32)
    sinv = np.sin(ang).astype(np.float32)
    cos_t = np.ones((128, pos.size), np.float32)
    sin_t = np.zeros((128, pos.size), np.float32)
    for p in range(128):
        d = p % 64
        if d < 8:
            cos_t[p] = cosv[d]
            sin_t[p] = -sinv[d]
        elif d < 16:
            cos_t[p] = cosv[d - 8]
            sin_t[p] = sinv[d - 8]
    return cf, cb, cos_t, sin_t


CF_OFF = {}
_o = 0
for _n, _w in [("ident", 128), ("tri_p", 128), ("tri_s", 128), ("U", 128), ("lastmask", 128), ("oh", 16),
               ("ones", 128), ("row0", 1)]:
    CF_OFF[_n] = (_o, _o + _w)
    _o += _w
CF_W = _o
CB_OFF = {}
_o = 0
for _n, _w in [("ident", 128), ("prot", 128), ("mprev", 512), ("mcur", 512), ("mcur_s", 512), ("mc", 512),
               ("sel16", 2048), ("selpair", 1024), ("sinkE", 128), ("sinkO", 128), ("ones", 128), ("U", 128)]:
    CB_OFF[_n] = (_o, _o + _w)
    _o += _w
CB_W = _o


def build_nc():
    nc = bass.Bass("TRN2", target_bir_lowering=False)
    stack = contextlib.ExitStack()
    S = Sched(nc, stack)

    def din(name, shape):
        return nc.dram_tensor(name, list(shape), F32, kind="ExternalInput").ap()

    def dout(name, shape):
        return nc.dram_tensor(name, list(shape), F32, kind="ExternalOutput").ap()

    xp = din("xp", [2048, D]); xs = din("xs", [128, D])
    ck = din("ck", [16, 128, 4, 64]); cv = din("cv", [16, 128, 4, 64])
    sconv = din("sconv", [48, 1536]); sssm = din("sssm", [16, 1024, 128])
    w_in = din("w_in", [D, NIN]); w_out = din("w_out", [2048, D])
    w_gate = din("w_gate", [D, DFF]); w_up = din("w_up", [D, DFF]); w_down = din("w_down", [DFF, D])
    gtiles = din("gtiles", [128, 4 * D])
    vecs = din("vecs", [128, 48 + 60 + 16 + 16])
    cf_d = din("cf", [128, CF_W]); cb_d = din("cb", [128, CB_W])
    cos_d = din("cos_t", [128, 2176]); sin_d = din("sin_t", [128, 2176])

    y_p = dout("y_p", [2048, D]); y_s = dout("y_s", [128, D])
    nk_p = dout("nk_p", [128, 256]); nv_p = dout("nv_p", [128, 256])
    ncv_p = dout("ncv_p", [3, 1536]); nssm_p = dout("nssm_p", [1024, 128])
    nk_s = dout("nk_s", [16, 128, 256]); nv_s = dout("nv_s", [16, 128, 256])
    ncv_s = dout("ncv_s", [48, 1536]); nssm_s = dout("nssm_s", [16, 1024, 128])
    KDBG = os.environ.get('KDBG', '')
    if KDBG:
        dbg = nc.dram_tensor("dbg", [128, 16, 128], BF16, kind="ExternalOutput").ap()

    cur = [SB_BASE]
    bufrange = {}

    def sb(name, shape, dt, at=None):
        nbytes = int(np.prod(shape[1:])) * (4 if dt == F32 else 2)
        nbytes = (nbytes + 31) // 32 * 32
        if at is None:
            off = cur[0]
            cur[0] += nbytes
        else:
            off = at[0]
            at[0] += nbytes
        assert off + nbytes <= SB_LIMIT, (name, off, nbytes)
        if at is not None:
            bufrange[name] = (off, off + nbytes)
        return nc.alloc_sbuf_tensor_at(name, list(shape), dt, offset=off)

    G = 640
    cf = sb("cf", [128, CF_W], F32)
    cb = sb("cb", [128, CB_W], BF16)
    gt = sb("gt", [128, 4 * D], F32)
    vc = sb("vc", [128, 140], F32)
    hT = sb("hT", [128, 8, G], BF16)
    catT = sb("catT", [128, 16, G], BF16, cur)
    ovc = [bufrange["catT"][0]]
    wpx = [sb("wpx%d" % i, [128, 8, 128], BF16, ovc) for i in range(8)]
    assert ovc[0] <= bufrange["catT"][1]
    wp = [sb("wp%d" % i, [128, 8, 128], BF16, cur) for i in range(4)]
    wpb = []
    for i_, nm_ in enumerate(["wp0", "wp2", "wpx0", "wpx2", "wpx4", "wpx6"]):
        wpb.append(sb("wpb%d" % i_, [128, 8, 256], BF16, [bufrange[nm_][0]]))
    wv = sb("wv", [128, 8, 272], BF16)
    xin = [sb("xin%d" % i, [128, D], F32) for i in range(2)]
    hb = sb("hb", [128, D], BF16)
    stat = sb("stat", [128, 64], F32)
    ccar = sb("ccar", [128, 12, 3], F32)
    HT = sb("HT", [128, 1024], F32)
    HTb = sb("HTb", [128, 1024], BF16)
    kE0 = sb("kE0", [128, 4, 128], BF16); kO0 = sb("kO0", [128, 4, 128], BF16)
    va0 = sb("va0", [128, 4, 192], BF16)
    dtt = sb("dtt", [128, 5, 16], F32)
    dtA = sb("dtA", [128, 5, 16], F32)
    abc = sb("abc", [128, 16], F32)
    esr = sb("esr", [128, 16, 128], BF16)
    es = sb("es", [128, 16], F32)
    kfp = sb("kfp", [128, 4, 128], F32)
    vfp = sb("vfp", [128, 256], F32)
    otok = sb("otok", [128, 256], F32)
    otok2 = sb("otok2", [128, 256], F32)
    ARENA = cur[0]
    a = [ARENA]
    szT = sb("szT", [128, 8, G], BF16, a)
    xbcT = sb("xbcT", [128, 12, G], BF16, a)
    stg = sb("stg", [128, 3 + 512], F32, a)
    sstg = sb("sstg", [128, 16, 11], F32, a)
    acc = sb("acc", [128, 512], F32, a)
    cstT = sb("cstT", [128, 12, 48], F32, a)
    ncs = sb("ncs", [128, 12, 48], F32, a)
    sctok = sb("sctok", [48, 1536], F32, a)
    X = sb("X", [128, 16, 128], F32, a)
    Xb = nc.alloc_sbuf_tensor_at("Xb", [128, 16, 128], BF16, offset=bufrange["X"][0])
    dec0 = sb("dec0", [128, 16, 128], BF16, a)
    eac = sb("eac", [128, 16, 128], BF16, a)
    CdT0 = sb("CdT0", [128, 16, 128], BF16, a)
    xdt0 = sb("xdt0", [128, 16, 64], BF16, a)
    xtail0 = sb("xtail0", [128, 16, 64], BF16, a)
    Btok0 = sb("Btok0", [128, 2, 128], BF16, a)
    cbm = sb("cbm", [128, 2, 128], BF16, a)
    acT = sb("acT", [128, 128], F32, a)
    achi = sb("achi", [128, 128], BF16, a)
    aclo = sb("aclo", [128, 128], BF16, a)
    eal0 = sb("eal0", [128, 16], F32, a)
    tailc = sb("tailc", [128, 16], F32, a)
    gated = sb("gated", [128, 8, 128], F32, a)
    gsq = sb("gsq", [128, 8, 128], BF16, a)
    rs = sb("rs", [128, 2, 128], F32, a)
    h0n = sb("h0n", [128, 8, 128], F32, a)
    h0T0 = sb("h0T0", [128, 1024], BF16, a)
    h0T1 = sb("h0T1", [128, 1024], BF16, a)
    h0TB = [h0T0, h0T1]
    Bm = sb("Bm", [128, 16, 256], BF16, a)
    decs = sb("decs", [128, 8, 16], F32, a)
    hout = sb("hout", [128, 8, 128], F32, a)
    tmpd = sb("tmpd", [128, 16, 128], BF16, a)
    h0n2 = sb("h0n2", [128, 8, 128], F32, a)
    dsk = sb("dsk", [128, 8, 128], BF16, a)
    h0n3 = sb("h0n3", [128, 8, 128], F32, [bufrange["tmpd"][0]])
    hout2 = sb("hout2", [128, 8, 128], F32, [bufrange["X"][0]])
    stg2 = sb("stg2", [128, 3 + 512], F32, [bufrange["X"][0] + 4096])
    acc2 = sb("acc2", [128, 512], F32, [bufrange["dec0"][0]])
    stgB, accB = [stg, stg2], [acc, acc2]
    SSD_END = a[0]
    ov = [bufrange["Bm"][0]]
    dec1 = sb("dec1", [128, 16, 128], BF16, ov)
    CdT1 = sb("CdT1", [128, 16, 128], BF16, ov)
    assert ov[0] <= bufrange["Bm"][1]
    ov = [bufrange["tmpd"][0]]
    xdt1 = sb("xdt1", [128, 16, 64], BF16, ov)
    xtail1 = sb("xtail1", [128, 16, 64], BF16, ov)
    assert ov[0] <= bufrange["tmpd"][1]
    ov = [bufrange["h0n"][0]]
    Btok1 = sb("Btok1", [128, 2, 128], BF16, ov)
    eal1 = sb("eal1", [128, 16], F32, ov)
    assert ov[0] <= bufrange["h0n"][1]
    decB, CdTB, xdtB, xtailB, BtokB, ealB = [dec0, dec1], [CdT0, CdT1], [xdt0, xdt1], [xtail0, xtail1], [Btok0, Btok1], [eal0, eal1]
    a = [ARENA]
    qT = sb("qT", [128, 8, G], BF16, a)
    kE = sb("kE", [128, 4, G], BF16, a)
    kO = sb("kO", [128, 4, G], BF16, a)
    vaug = sb("vaug", [128, 5, 4, 192], BF16, a)
    cosT = sb("cosT", [128, G], F32, a)
    sinT = sb("sinT", [128, G], F32, a)
    qb = sb("qb", [128, 512], BF16, a)
    t1 = sb("t1", [128, 512], F32, a)
    pT0 = sb("pT0", [128, 2, 512], BF16, a)
    pT1 = sb("pT1", [128, 2, 512], BF16, a)
    pTB = [pT0, pT1]
    rden = sb("rden", [128, 512], F32, a)
    kcn = sb("kcn", [128, 16, 128], BF16, a)
    kcE = sb("kcE", [128, 16, 128], BF16, a)
    kcO = sb("kcO", [128, 16, 128], BF16, a)
    vca = sb("vca", [128, 16, 192], BF16, a)
    pTc = sb("pTc", [128, 512], BF16, a)
    kcn2 = sb("kcn2", [128, 16, 128], BF16, a)
    kcnB = [kcn, kcn2]
    ATT_END = a[0]
    a = [ARENA]
    wd = sb("wd", [128, 22, D], BF16, a)
    xres = sb("xres", [128, 5, D], F32, a)
    mixs = sb("mixs", [128, D], F32, a)
    dns = mixs
    assert a[0] >= ATT_END, (a[0], ATT_END)
    a2 = [a[0]]
    wo = sb("wo", [128, 16, D], BF16, a)
    hff = sb("hff", [128, 22, G], BF16, a2)
    sg = sb("sg", [128, 512], BF16, a)
    hb1 = sb("hb1", [128, D], BF16, a)
    hbB = [hb, hb1]
    E_END = a[0]
    F_END = a2[0]
    assert max(SSD_END, ATT_END, E_END, F_END) <= SB_LIMIT, (SSD_END, ATT_END, E_END, F_END)

    banks = [stack.enter_context(nc.psum_tensor("bank%d" % i, [128, 512], F32)) for i in range(8)]
    bankR = [Reg("bank%d" % i) for i in range(8)]
    bctr = [0]

    reserved = set()

    def nb():
        while True:
            i = bctr[0] % 8
            bctr[0] += 1
            if i not in reserved:
                return banks[i], bankR[i]

    regs = {}

    special = {"ac": ["acT", "achi", "aclo"], "rope": ["cosT", "sinT"], "vca1": ["vca"]}

    def R(name):
        if name not in regs:
            rng = None
            if name in bufrange:
                rng = [bufrange[name]]
            elif name in special:
                rng = [bufrange[b] for b in special[name]]
            elif name.startswith("xres") and name[4:].isdigit():
                lo = bufrange["xres"][0] + int(name[4:]) * D * 4
                rng = [(lo, lo + D * 4)]
            regs[name] = Reg(name, rng)
            if rng:
                S.ranged.append(regs[name])
        return regs[name]

    def C(n, pack=cf, off=CF_OFF):
        a0, a1 = off[n]
        return pack[:, a0:a1]

    def CB(n):
        a0, a1 = CB_OFF[n]
        return cb[:, a0:a1]

    def dma_load(eng, out_ap, in_ap, key, writes, reads=()):
        def fn(e, sem):
            e.dma_start(out=out_ap, in_=in_ap).then_inc(sem, 16)
        S.task(eng, fn, reads=reads, writes=writes, dma=key)

    def dma_multi(eng, pairs, key, writes, reads=(), slow=False):
        def fn(e, sem):
            for o, i_ in pairs:
                if slow:
                    e.dma_start(out=o, in_=i_, allow_slow_non_contiguous=True).then_inc(sem, 16)
                else:
                    e.dma_start(out=o, in_=i_).then_inc(sem, 16)
        S.task(eng, fn, reads=reads, writes=writes, dma=key, ndma=len(pairs))

    Rc = R("consts")
    dma_load("sp", cf[:, :], cf_d[:, :], "c0", [R("c0")])
    dma_load("sp", gt[:, :], gtiles[:, :], "c1", [R("c1")])
    dma_load("sp", vc[:, :], vecs[:, :], "c2", [R("c2")])
    dma_load("pool", cb[:, :], cb_d[:, :], "c3", [R("c3")])
    S.task("dve", lambda dve: dve.memset(stat[:, 60:64], 0.0), reads=[R("c0"), R("c1"), R("c2"), R("c3")], writes=[Rc])
    gpre, gpost, gffn, gpff = (gt[:, i * D:(i + 1) * D] for i in range(4))
    dtb, alog, sinks = vc[:, 0:16], vc[:, 16:32], vc[:, 32:48]
    convw = vc[:, 48:96]
    convb = vc[:, 96:108]
    gssm = vc[:, 108:116]
    dskipc = vc[:, 116:124]

    def t_init(act):
        act.activation(out=abc[:, :], in_=alog, func=AF.Exp)
        return act.activation(out=es[:, :], in_=sinks, func=AF.Exp)
    S.task("act", t_init, reads=[Rc], writes=[R("abc"), R("es")])

    def t_init2(dve):
        dve.tensor_scalar(out=abc[:, :], in0=abc[:, :], scalar1=-1.0, scalar2=None, op0=ALU.mult)
        dve.memset(ccar[:, :, :], 0.0)
        dve.memset(HT[:, :], 0.0)
        dve.memset(HTb[:, :], 0.0)
        dve.memset(stat[:, :], 0.0)
        return dve.tensor_scalar(out=esr[:, :, :], in0=es[:, :].unsqueeze(2).to_broadcast([128, 16, 128]),
                                 scalar1=C("row0"), scalar2=None, op0=ALU.mult)
    S.task("dve", t_init2, reads=[Rc, R("abc"), R("es")], writes=[R("abc"), R("esr"), R("ccar"), R("HT"), R("HTb"), R("stat")])

    wslot = [0]

    fslot = [0]
    bslot = [0]

    def load_big(dram_w, c0):
        j = bslot[0] % 6
        bslot[0] += 1
        t, nm = wpb[j], "wpb%d" % j
        Rw = R(nm)
        src = dram_w.rearrange("(kc p) n -> p kc n", p=128)
        dma_multi("pool", [(t[:, :, :], src[:, :, c0:c0 + 256])], nm, [Rw])
        return t, Rw

    def load_panel(dram_w, c0, ncols=128, dup=False, deep=False):
        if deep:
            j = fslot[0] % 12
            fslot[0] += 1
            if j < 4:
                t, nm = wp[j], "wp%d" % j
            else:
                t, nm = wpx[j - 4], "wpx%d" % (j - 4)
            Rw = R(nm)
            src = dram_w.rearrange("(kc p) n -> p kc n", p=128)
            dma_multi("pool", [(t[:, :, 0:ncols], src[:, :, c0:c0 + ncols])], nm, [Rw])
            return t, Rw
        i = wslot[0] % 4
        wslot[0] += 1
        t = wp[i]
        Rw = R("wp%d" % i)
        src = dram_w.rearrange("(kc p) n -> p kc n", p=128)
        if dup:
            pairs = [(t[:, :, 0:64], src[:, :, c0:c0 + 64]), (t[:, :, 64:128], src[:, :, c0:c0 + 64])]
        else:
            pairs = [(t[:, :, 0:ncols], src[:, :, c0:c0 + ncols])]
        dma_multi("pool", pairs, "wp%d" % i, [Rw])
        return t, Rw

    KSTOP = os.environ.get('KSTOP', '')
    KBARS = set(os.environ.get('KBARS', '').split(','))

    class _Stop(Exception):
        pass

    def chk(tag, g):
        if KSTOP == tag + str(g):
            raise _Stop()
    try:
      for g in range(4):
          has_s = (g == 3)
          ntile = 5 if has_s else 4
          NP = 512
          ranges = [(0, 512)] + ([(512, 640)] if has_s else [])
          Rh = [R("hT%d" % t) for t in range(5)]

          def a_tile(g, lt):
              Rh = [R("hT%d" % t) for t in range(5)]
              if True:
                  xi = xin[lt % 2]
                  Rx = R("xin%d" % (lt % 2))
                  src = xs[:, :] if lt == 4 else xp[(4 * g + lt) * 128:(4 * g + lt + 1) * 128, :]
                  dma_load("sp", xi[:, :], src, "xin%d" % (lt % 2), [Rx])
                  Rst = R("stat")

                  S.task("dve", lambda dve: dve.memset(stat[:, 0:1], 0.0), writes=[Rst])
                  S.chain("act", [lambda act, xi=xi: act.activation(out=hb[:, :], in_=xi[:, :], func=AF.Square, accum_out=stat[:, 0:1]),
                                  lambda act: act.activation(out=stat[:, 1:2], in_=stat[:, 0:1], func=AF.Ln, scale=1.0 / D, bias=EPS),
                                  lambda act: act.activation(out=stat[:, 2:3], in_=stat[:, 1:2], func=AF.Exp, scale=-0.5)],
                          reads=[Rx], writes=[Rst, R("hb0")])
                  S.chain("dve", [lambda dve, xi=xi: dve.scalar_tensor_tensor(out=hb[:, :], in0=xi[:, :], scalar=stat[:, 2:3], in1=gpre,
                                                                             op0=ALU.mult, op1=ALU.mult)],
                          reads=[Rx, Rst, Rc], writes=[Rst, R("hb0")])
                  yield
                  bk, bR = nb()

                  def tA3(pe, bk=bk):
                      for kc in range(8):
                          last = pe.transpose(bk[:, kc * 64:(kc + 1) * 64].bitcast(BF16), hb[:, kc * 128:(kc + 1) * 128], CB("ident"))
                      return last
                  S.task("pe", tA3, reads=[R("hb0"), Rc], writes=[bR])

                  def tA4(act, bk=bk, lt=lt):
                      return act.activation(out=hT[:, :, lt * 128:(lt + 1) * 128],
                                            in_=bk[:, :].bitcast(BF16).rearrange("p (c t) -> p c t", c=8), func=AF.Copy)
                  S.task("act", tA4, reads=[bR], writes=[Rh[lt]])
          def phaseA(g):
              for lt_ in range(5 if g == 3 else 4):
                  for _ in a_tile(g, lt_):
                      pass
          if g == 0:
              phaseA(0)
          S.barrier(force=('A' in KBARS))
          chk('A', g)

          srcw = w_in.rearrange("(kc p) n -> p kc n", p=128)
          dma_multi("pool", [(wv[:, :, 0:256], srcw[:, :, 1280:1536]), (wv[:, :, 256:272], srcw[:, :, 4096:4112])], "wv", [R("wv")])
          if has_s and True:
              dma_load("sp", sctok[:, :], sconv[:, :], "sct", [R("sctok")])
              for c in range(12):
                  bk, bR = nb()

                  def tcs(pe, bk=bk, c=c):
                      return pe.transpose(bk[:, 0:48], sctok[0:48, c * 128:(c + 1) * 128], C("ident")[0:48, 0:48])
                  S.task("pe", tcs, reads=[R("sctok"), Rc], writes=[bR])

                  def tcs2(act, bk=bk, c=c):
                      return act.activation(out=cstT[:, c, :], in_=bk[:, 0:48], func=AF.Copy)
                  S.task("act", tcs2, reads=[bR], writes=[R("cstT")])
          for c in range(12):
              wt, Rw = load_panel(w_in, 1536 + 1024 + c * 128, deep=True)
              for (r0, r1) in ranges:
                  bk, bR = nb()

                  def tm(pe, bk=bk, wt=wt, r0=r0, r1=r1):
                      for kc in range(8):
                          last = pe.matmul(bk[:, 0:r1 - r0], wt[:, kc, :], hT[:, kc, r0:r1], start=(kc == 0), stop=(kc == 7))
                      return last
                  S.task("pe", tm, reads=[Rw] + Rh, writes=[bR])
                  if r0 == 0:
                      sg_, an_ = stgB[c % 2], accB[c % 2]
                      sgn, ann = ("stg", "acc") if c % 2 == 0 else ("stg2", "acc2")
                      S.chain("act", [lambda act, c=c, sg_=sg_: act.activation(out=sg_[:, 0:3], in_=ccar[:, c, :], func=AF.Copy),
                                      lambda act, bk=bk, sg_=sg_: act.activation(out=sg_[:, 3:515], in_=bk[:, 0:512], func=AF.Copy),
                                      lambda act, c=c, sg_=sg_: act.activation(out=ccar[:, c, :], in_=sg_[:, 512:515], func=AF.Copy)],
                              reads=[bR, R("ccar")], writes=[R(sgn), R("ccar")])
                      fl = [lambda dve, c=c, sg_=sg_, an_=an_: dve.tensor_scalar(out=an_[:, :], in0=sg_[:, 0:512], scalar1=convw[:, c * 4:c * 4 + 1], scalar2=None, op0=ALU.mult)]
                      for tap in range(1, 4):
                          fl.append(lambda dve, c=c, tap=tap, sg_=sg_, an_=an_: dve.scalar_tensor_tensor(out=an_[:, :], in0=sg_[:, tap:tap + 512], scalar=convw[:, c * 4 + tap:c * 4 + tap + 1],
                                                                                                       in1=an_[:, :], op0=ALU.mult, op1=ALU.add))
                      S.chain("dve", fl, reads=[R(sgn), Rc], writes=[R(ann)])

                      def tc3(act, c=c, an_=an_):
                          return act.activation(out=xbcT[:, c, 0:512], in_=an_[:, :], func=AF.Silu, bias=convb[:, c:c + 1], scale=1.0)
                      S.task("act", tc3, reads=[R(ann), Rc], writes=[R("xbcT")])
                  else:
                      S.chain("act", [lambda act, c=c: act.activation(out=sstg[:, :, 0:3], in_=cstT[:, c, :].rearrange("p (b t) -> p b t", t=3), func=AF.Copy),
                                      lambda act, bk=bk: act.activation(out=sstg[:, :, 3:11], in_=bk[:, 0:128].rearrange("p (b t) -> p b t", t=8), func=AF.Copy),
                                      lambda act, c=c: act.activation(out=ncs[:, c, :].rearrange("p (b t) -> p b t", t=3), in_=sstg[:, :, 8:11], func=AF.Copy)],
                              reads=[bR, R("cstT")], writes=[R("sstg"), R("ncs")])
                      av = acc[:, 0:128].rearrange("p (b t) -> p b t", t=8)
                      fl = [lambda dve, c=c, av=av: dve.tensor_scalar(out=av, in0=sstg[:, :, 0:8], scalar1=convw[:, c * 4:c * 4 + 1], scalar2=None, op0=ALU.mult)]
                      for tap in range(1, 4):
                          fl.append(lambda dve, c=c, tap=tap, av=av: dve.scalar_tensor_tensor(out=av, in0=sstg[:, :, tap:tap + 8], scalar=convw[:, c * 4 + tap:c * 4 + tap + 1],
                                                                                              in1=av, op0=ALU.mult, op1=ALU.add))
                      S.chain("dve", fl, reads=[R("sstg"), Rc], writes=[R("acc")])

                      def ts3(act, c=c):
                          return act.activation(out=xbcT[:, c, 512:640], in_=acc[:, 0:128], func=AF.Silu, bias=convb[:, c:c + 1], scale=1.0)
                      S.task("act", ts3, reads=[R("acc"), Rc], writes=[R("xbcT")])
          for c in range(8):
              wt, Rw = load_panel(w_in, 1536 + c * 128, deep=True)
              for (r0, r1) in ranges:
                  bk, bR = nb()

                  def tm(pe, bk=bk, wt=wt, r0=r0, r1=r1):
                      for kc in range(8):
                          last = pe.matmul(bk[:, 0:r1 - r0], wt[:, kc, :], hT[:, kc, r0:r1], start=(kc == 0), stop=(kc == 7))
                      return last
                  S.task("pe", tm, reads=[Rw] + Rh, writes=[bR])

                  def tz(act, bk=bk, c=c, r0=r0, r1=r1):
                      return act.activation(out=szT[:, c, r0:r1], in_=bk[:, 0:r1 - r0], func=AF.Silu)
                  S.task("act", tz, reads=[bR], writes=[R("szT")])
          for lt in range(ntile):
              bk, bR = nb()

              def tdt(pe, bk=bk, lt=lt):
                  for kc in range(8):
                      last = pe.matmul(bk[:, 0:16], hT[:, kc, lt * 128:(lt + 1) * 128], wv[:, kc, 256:272], start=(kc == 0), stop=(kc == 7))
                  return last
              S.task("pe", tdt, reads=[R("wv")] + Rh, writes=[bR])

              def tdt2(dve, bk=bk, lt=lt):
                  return dve.tensor_tensor(out=dtt[:, lt, :], in0=bk[:, 0:16], in1=dtb, op=ALU.add)
              S.task("dve", tdt2, reads=[bR, Rc], writes=[R("dtt")])
          S.chain("act", [lambda act, ntile=ntile: act.activation(out=dtt[:, 0:ntile, :], in_=dtt[:, 0:ntile, :], func=AF.Exp),
                          lambda act, ntile=ntile: act.activation(out=dtt[:, 0:ntile, :], in_=dtt[:, 0:ntile, :], func=AF.Ln, bias=1.0, scale=1.0)],
                  reads=[R("dtt")], writes=[R("dtt")])

          def tdt4(dve, ntile=ntile):
              return dve.tensor_tensor(out=dtA[:, 0:ntile, :], in0=dtt[:, 0:ntile, :], in1=abc[:, :].unsqueeze(1).to_broadcast([128, ntile, 16]), op=ALU.mult)
          S.task("dve", tdt4, reads=[R("dtt"), R("abc")], writes=[R("dtA")])

          S.barrier(force=('B1' in KBARS))
          chk('B1', g)
          def ssd_front(lt):
              par = (lt % 2) if lt < 4 else 0
              dec, CdT, xdt, xtail, Btok, eal = decB[par], CdTB[par], xdtB[par], xtailB[par], BtokB[par], ealB[par]
              samp = (lt == 4)
              ci = 4 * g + lt
              cs = slice(lt * 128, (lt + 1) * 128)
              tri = C("tri_s") if samp else C("tri_p")
              RS = R("ssdtmp")
              bk, bR = nb()

              def tac(pe, bk=bk, lt=lt, tri=tri):
                  pe.matmul(bk[0:16, 0:128], dtA[:, lt, :], tri, start=True, stop=True)
                  return pe.matmul(bk[:, 128:144], C("ones"), dtA[:, lt, :], start=True, stop=True)
              S.task("pe", tac, reads=[R("dtA"), Rc], writes=[bR])

              S.chain("dve", [lambda dve, bk=bk: dve.tensor_copy(out=acT[0:16, :], in_=bk[0:16, 0:128]),
                              lambda dve: dve.tensor_copy(out=achi[0:16, :], in_=acT[0:16, :]),
                              lambda dve: dve.tensor_tensor(out=aclo[0:16, :], in0=acT[0:16, :], in1=achi[0:16, :], op=ALU.subtract)],
                      reads=[bR], writes=[R("ac"), bR])

              def tac3(act, bk=bk):
                  return act.activation(out=eal[:, :], in_=bk[:, 128:144], func=AF.Exp)
              S.task("act", tac3, reads=[bR], writes=[R("eal%d" % par), bR])
              def tX(dve, lt=lt, tri=tri):
                  return dve.tensor_tensor(out=Xb[:, :, :], in0=tri.unsqueeze(1).to_broadcast([128, 16, 128]),
                                           in1=dtA[:, lt, :].unsqueeze(2).to_broadcast([128, 16, 128]), op=ALU.mult)
              S.task("dve", tX, reads=[R("dtA"), Rc], writes=[R("X")])
              yield
              sb_ = [nb() for _ in range(4)]

              def tseg(pe, sb_=sb_):
                  for q4 in range(4):
                      last = pe.matmul(sb_[q4][0][:, :], CB("U"), Xb[:, q4 * 4:(q4 + 1) * 4, :], start=True, stop=True)
                  return last
              S.task("pe", tseg, reads=[R("X"), Rc], writes=[b[1] for b in sb_])

              def tdec(act, sb_=sb_):
                  for q4 in range(4):
                      last = act.activation(out=dec[:, q4 * 4:(q4 + 1) * 4, :], in_=sb_[q4][0][:, :].rearrange("p (h t) -> p h t", h=4), func=AF.Exp)
                  return last
              S.task("act", tdec, reads=[b[1] for b in sb_], writes=[R("dec%d" % par)])
              yield
              bk, bR = nb()

              def tcb(pe, bk=bk, cs=cs):
                  for gg in range(2):
                      last = pe.matmul(bk[:, gg * 128:(gg + 1) * 128], xbcT[:, 8 + gg, cs], xbcT[:, 10 + gg, cs], start=True, stop=True)
                  return last
              S.task("pe", tcb, reads=[R("xbcT")], writes=[bR])

              def tcb2(dve, bk=bk, tri=tri):
                  return dve.tensor_tensor(out=cbm[:, :, :], in0=bk[:, 0:256].rearrange("p (g t) -> p g t", g=2),
                                           in1=tri.unsqueeze(1).to_broadcast([128, 2, 128]), op=ALU.mult)
              S.task("dve", tcb2, reads=[bR, Rc], writes=[R("cbm")])
              if samp:
                  S.chain("dve", [lambda dve: dve.tensor_tensor(out=tmpd[:, :, :], in0=dec[:, :, :], in1=C("lastmask").unsqueeze(1).to_broadcast([128, 16, 128]), op=ALU.mult),
                                  lambda dve: dve.tensor_reduce(out=tailc[:, :], in_=tmpd[:, :, :], axis=AX.X, op=ALU.add)],
                          reads=[R("dec%d" % par), Rc], writes=[R("tailc"), R("tmpd")])
              else:
                  def ttl(dve):
                      return dve.tensor_copy(out=tailc[:, :], in_=dec[:, :, 127])
                  S.task("dve", ttl, reads=[R("dec%d" % par)], writes=[R("tailc")])
              bk, bR = nb()
              bk2, bR2 = nb()

              def ttr(pe, bk=bk, bk2=bk2, cs=cs):
                  for c in range(8):
                      pe.transpose(bk[:, c * 64:(c + 1) * 64].bitcast(BF16), xbcT[:, c, cs], CB("ident"))
                  for gg in range(2):
                      last = pe.transpose(bk2[:, gg * 64:(gg + 1) * 64].bitcast(BF16), xbcT[:, 8 + gg, cs], CB("ident"))
                  return last
              S.task("pe", ttr, reads=[R("xbcT"), Rc], writes=[bR, bR2])

              S.chain("dve", [lambda dve, bk=bk, lt=lt: dve.tensor_tensor(out=xdt[:, :, :], in0=bk[:, :].bitcast(BF16).rearrange("p (h d) -> p h d", h=16),
                                                                          in1=dtt[:, lt, :].unsqueeze(2).to_broadcast([128, 16, 64]), op=ALU.mult),
                              lambda dve: dve.tensor_tensor(out=xtail[:, :, :], in0=xdt[:, :, :], in1=tailc[:, :].unsqueeze(2).to_broadcast([128, 16, 64]), op=ALU.mult),
                              lambda dve, bk2=bk2: dve.tensor_copy(out=Btok[:, :, :], in_=bk2[:, 0:128].bitcast(BF16).rearrange("p (g n) -> p g n", g=2))],
                      reads=[bR, bR2, R("dtt"), R("tailc")], writes=[R("xdt%d" % par), R("xtail%d" % par), R("Btok%d" % par)])
              yield
              eb = [nb() for _ in range(4)]

              def teac(pe, eb=eb):
                  for h in range(16):
                      o = eb[h // 4][0][:, (h % 4) * 128:(h % 4 + 1) * 128]
                      a0 = CB_OFF["sel16"][0] + h * 128
                      pe.matmul(o, cb[0:16, a0:a0 + 128], achi[0:16, :], start=True, stop=False)
                      last = pe.matmul(o, cb[0:16, a0:a0 + 128], aclo[0:16, :], start=False, stop=True)
                  return last
              S.task("pe", teac, reads=[R("ac"), Rc], writes=[b[1] for b in eb])

              def teac2(act, eb=eb):
                  for q4 in range(4):
                      last = act.activation(out=eac[:, q4 * 4:(q4 + 1) * 4, :], in_=eb[q4][0][:, :].rearrange("p (h t) -> p h t", h=4), func=AF.Exp)
                  return last
              S.task("act", teac2, reads=[b[1] for b in eb], writes=[R("eac")])
              def twt(dve, cs=cs):
                  return dve.tensor_tensor(out=dec[:, :, :].rearrange("p (g e) t -> p g e t", g=2), in0=dec[:, :, :].rearrange("p (g e) t -> p g e t", g=2),
                                           in1=cbm[:, :, :].unsqueeze(2).to_broadcast([128, 2, 8, 128]), op=ALU.mult)
              S.task("dve", twt, reads=[R("dec%d" % par), R("cbm"), R("tailc")], writes=[R("dec%d" % par)])

              def twt2(dve, cs=cs):
                  return dve.tensor_tensor(out=CdT[:, :, :].rearrange("p (g e) t -> p g e t", g=2), in0=eac[:, :, :].rearrange("p (g e) t -> p g e t", g=2),
                                            in1=xbcT[:, 10:12, cs].unsqueeze(2).to_broadcast([128, 2, 8, 128]), op=ALU.mult)
              S.task("dve", twt2, reads=[R("eac"), R("xbcT")], writes=[R("CdT%d" % par)])
          def ssd_back(lt):
              par = (lt % 2) if lt < 4 else 0
              dec, CdT, xdt, xtail, Btok, eal = decB[par], CdTB[par], xdtB[par], xtailB[par], BtokB[par], ealB[par]
              samp = (lt == 4)
              ci = 4 * g + lt
              cs = slice(lt * 128, (lt + 1) * 128)
              tri = C("tri_s") if samp else C("tri_p")
              yb = [nb(), nb()]
              if samp:
                  reserved.update(banks.index(yb[0][0]), ) if False else None
                  for _b in yb:
                      reserved.add([id(x) for x in banks].index(id(_b[0])))
              first_chunk = (ci == 0 and not samp)
              if samp:
                  def tBm(dve):
                      return dve.tensor_tensor(out=Bm[:, :, :], in0=Btok[:, :, :].rearrange("p g n -> p (g n)").unsqueeze(1).to_broadcast([128, 16, 256]),
                                               in1=C("oh").unsqueeze(2).to_broadcast([128, 16, 256]), op=ALU.mult)
                  S.task("dve", tBm, reads=[R("Btok%d" % par), Rc], writes=[R("Bm")])
                  bk3, bR3 = nb()

                  def tds(pe, bk3=bk3):
                      for pr in range(8):
                          a0 = CB_OFF["selpair"][0] + pr * 128
                          o = bk3[:, pr * 16:(pr + 1) * 16]
                          pe.matmul(o, cb[0:16, a0:a0 + 128], achi[0:16, 7:128:8], start=True, stop=False)
                          last = pe.matmul(o, cb[0:16, a0:a0 + 128], aclo[0:16, 7:128:8], start=False, stop=True)
                      return last
                  S.task("pe", tds, reads=[R("ac"), Rc], writes=[bR3])

                  def tds2(act, bk3=bk3):
                      return act.activation(out=decs[:, :, :], in_=bk3[:, 0:128].rearrange("p (r b) -> p r b", r=8), func=AF.Exp)
                  S.task("act", tds2, reads=[bR3], writes=[R("decs")])

              def tyi(pe, yb=yb, first_chunk=first_chunk, samp=samp, cs=cs):
                  for h in range(16):
                      pr = h // 2
                      o = yb[pr // 4][0][64 * (h % 2):64 * (h % 2) + 64, (pr % 4) * 128:(pr % 4 + 1) * 128]
                      last = pe.matmul(o, xdt[:, h, :], dec[:, h, :], start=((pr % 4 == 0) if samp else True), stop=first_chunk, tile_position=(0, 64 * (h % 2)), skip_group_check=samp)
                      if not first_chunk and not samp:
                          last = pe.matmul(o, HTb[:, h * 64:(h + 1) * 64], CdT[:, h, :], start=False, stop=True, tile_position=(0, 64 * (h % 2)))
                      if h % 2 == 1:
                          last = pe.matmul(yb[pr // 4][0][:, (pr % 4) * 128:(pr % 4 + 1) * 128], dsk[:, pr, :], xbcT[:, pr, cs], start=False, stop=True,
                                           skip_group_check=True)
                  return last
              S.task("pe", tyi, reads=[R("xdt%d" % par), R("dec%d" % par), R("CdT%d" % par), R("HTb"), R("dsk"), R("xbcT")], writes=[yb[0][1], yb[1][1]])
              if samp:
                  seqctx = {}
                  def seq_s1(b):
                      h0n_, hn_ = [(h0n, "h0n"), (h0n2, "h0n2"), (h0n3, "h0n3")][b % 3]
                      hout_, ho_ = (hout, "hout") if b % 2 == 0 else (hout2, "hout2")
                      dma_load("pool", h0n_[:, :, :], sssm[b].rearrange("(r q) n -> q r n", q=128), hn_, [R(hn_)])
                      tb = [nb(), nb()]

                      def th0(pe, tb=tb, h0n_=h0n_):
                          for pr in range(8):
                              last = pe.transpose(tb[pr // 4][0][:, (pr % 4) * 128:(pr % 4 + 1) * 128], h0n_[:, pr, :], C("ident"))
                          return last
                      S.task("pe", th0, reads=[R(hn_), Rc], writes=[tb[0][1], tb[1][1]])

                      def th1(act, tb=tb, h0T=h0TB[b % 2]):
                          act.activation(out=h0T[:, 0:512], in_=tb[0][0][:, :], func=AF.Copy)
                          return act.activation(out=h0T[:, 512:1024], in_=tb[1][0][:, :], func=AF.Copy)
                      S.task("act", th1, reads=[tb[0][1], tb[1][1]], writes=[R("h0T%d" % (b % 2))])

                      seqctx[b] = tb
                  def seq_s2(b):
                      h0n_, hn_ = [(h0n, "h0n"), (h0n2, "h0n2"), (h0n3, "h0n3")][b % 3]
                      hout_, ho_ = (hout, "hout") if b % 2 == 0 else (hout2, "hout2")
                      tb = seqctx.pop(b)
                      def th2(pe, yb=yb, b=b, h0T=h0TB[b % 2]):
                          for h in range(16):
                              pr = h // 2
                              o = yb[pr // 4][0][64 * (h % 2):64 * (h % 2) + 64, (pr % 4) * 128 + 8 * b:(pr % 4) * 128 + 8 * b + 8]
                              last = pe.matmul(o, h0T[:, h * 64:(h + 1) * 64], CdT[:, h, 8 * b:8 * b + 8], start=False, stop=(b == 15), skip_group_check=True,
                                               tile_position=(0, 64 * (h % 2)))
                          return last
                      S.task("pe", th2, reads=[R("h0T%d" % (b % 2)), R("CdT%d" % par)], writes=[yb[0][1], yb[1][1]])
                      hb2 = [nb(), nb()]

                      def th3(pe, hb2=hb2, b=b):
                          for pr in range(8):
                              gg = pr // 4
                              last = pe.matmul(hb2[pr // 4][0][:, (pr % 4) * 128:(pr % 4 + 1) * 128], xtail[:, :, :].rearrange("p h d -> p (h d)")[:, pr * 128:(pr + 1) * 128],
                                               Bm[:, b, gg * 128:(gg + 1) * 128], start=True, stop=True)
                          return last
                      S.task("pe", th3, reads=[R("xtail%d" % par), R("Bm")], writes=[hb2[0][1], hb2[1][1]])

                      def th4(dve, hb2=hb2, b=b, h0n_=h0n_, hout_=hout_):
                          for pr in range(8):
                              last = dve.scalar_tensor_tensor(out=hout_[:, pr, :], in0=h0n_[:, pr, :], scalar=decs[:, pr, b:b + 1],
                                                              in1=hb2[pr // 4][0][:, (pr % 4) * 128:(pr % 4 + 1) * 128], op0=ALU.mult, op1=ALU.add)
                          return last
                      S.task("dve", th4, reads=[hb2[0][1], hb2[1][1], R(hn_), R("decs")], writes=[R(ho_)])
                      dma_load("sp", nssm_s[b].rearrange("(r q) n -> q r n", q=128), hout_[:, :, :], ho_, [], reads=[R(ho_)])

                  for b in range(16):
                      seq_s1(b)
                      seq_s2(b)
              else:
                  hb2 = [nb(), nb()]

                  def tst(pe, hb2=hb2):
                      for gg in range(2):
                          last = pe.matmul(hb2[gg][0][:, :], Btok[:, gg, :], xtail[:, :, :].rearrange("p h d -> p (h d)")[:, gg * 512:(gg + 1) * 512], start=True, stop=True)
                      return last
                  S.task("pe", tst, reads=[R("Btok%d" % par), R("xtail%d" % par)], writes=[hb2[0][1], hb2[1][1]])

                  hv = HT[:, :].rearrange("p (h d) -> p h d", h=16)
                  S.chain("dve", [lambda dve, hv=hv: dve.tensor_tensor(out=hv, in0=hv, in1=eal[:, :].unsqueeze(2).to_broadcast([128, 16, 64]), op=ALU.mult),
                                  lambda dve, hb2=hb2: dve.tensor_tensor(out=HT[:, 0:512], in0=HT[:, 0:512], in1=hb2[0][0][:, :], op=ALU.add),
                                  lambda dve, hb2=hb2: dve.tensor_tensor(out=HT[:, 512:1024], in0=HT[:, 512:1024], in1=hb2[1][0][:, :], op=ALU.add)],
                          reads=[hb2[0][1], hb2[1][1], R("eal%d" % par)], writes=[R("HT")])
              reserved.clear()
              def tg1(dve, yb=yb, cs=cs):
                  for hf in range(2):
                      last = dve.tensor_tensor(out=gated[:, hf * 4:hf * 4 + 4, :], in0=yb[hf][0][:, :].rearrange("p (r t) -> p r t", r=4),
                                               in1=szT[:, hf * 4:hf * 4 + 4, cs], op=ALU.mult)
                  return last
              S.task("dve", tg1, reads=[yb[0][1], yb[1][1], R("szT")], writes=[R("gated")])
              yield
              if not samp:
                  def tst3(act):
                      return act.activation(out=HTb[:, :], in_=HT[:, :], func=AF.Copy)
                  S.task("act", tst3, reads=[R("HT")], writes=[R("HTb")])

              def tg2(act):
                  return act.activation(out=gsq[:, :, :], in_=gated[:, :, :], func=AF.Square)
              S.task("act", tg2, reads=[R("gated")], writes=[R("gsq")])
              bk, bR = nb()

              def tg3(pe, bk=bk):
                  for pr in range(8):
                      gg = pr // 4
                      last = pe.matmul(bk[:, gg * 128:(gg + 1) * 128], CB("ones"), gsq[:, pr, :], start=(pr % 4 == 0), stop=(pr % 4 == 3))
                  return last
              S.task("pe", tg3, reads=[R("gsq"), Rc], writes=[bR])

              S.chain("act", [lambda act, bk=bk: act.activation(out=rs[:, :, :], in_=bk[:, 0:256].rearrange("p (g t) -> p g t", g=2), func=AF.Ln, scale=1.0 / 512, bias=EPS),
                              lambda act: act.activation(out=rs[:, :, :], in_=rs[:, :, :], func=AF.Exp, scale=-0.5)],
                      reads=[bR], writes=[R("rs")])
              yield

              def tg5(dve, cs=cs):
                  for pr in range(8):
                      last = dve.scalar_tensor_tensor(out=catT[:, 8 + pr, cs], in0=gated[:, pr, :], scalar=gssm[:, pr:pr + 1],
                                                      in1=rs[:, pr // 4, :], op0=ALU.mult, op1=ALU.mult)
                  return last
              S.task("dve", tg5, reads=[R("rs"), R("gated"), Rc], writes=[R("catT")])
              if ci == 15 and not samp:
                  tb = [nb(), nb()]

                  def tfin(pe, tb=tb):
                      for pr in range(8):
                          last = pe.transpose(tb[pr // 4][0][:, (pr % 4) * 128:(pr % 4 + 1) * 128], HT[:, pr * 128:(pr + 1) * 128], C("ident"))
                      return last
                  S.task("pe", tfin, reads=[R("HT"), Rc], writes=[tb[0][1], tb[1][1]])

                  def tfin2(act, tb=tb):
                      act.activation(out=hout[:, 0:4, :], in_=tb[0][0][:, :].rearrange("p (r n) -> p r n", r=4), func=AF.Copy)
                      return act.activation(out=hout[:, 4:8, :], in_=tb[1][0][:, :].rearrange("p (r n) -> p r n", r=4), func=AF.Copy)
                  S.task("act", tfin2, reads=[tb[0][1], tb[1][1]], writes=[R("hout")])
                  dma_load("sp", nssm_p.rearrange("(r q) n -> q r n", q=128), hout[:, :, :], "hout", [], reads=[R("hout")])
          def run_il(gens):
              gens = list(gens)
              while gens:
                  for g_ in list(gens):
                      try:
                          next(g_)
                      except StopIteration:
                          gens.remove(g_)
          def tdsk(dve):
              for pr in range(8):
                  last = dve.tensor_scalar(out=dsk[:, pr, :], in0=CB("ident"), scalar1=dskipc[:, pr:pr + 1], scalar2=None, op0=ALU.mult)
              return last
          S.task("dve", tdsk, reads=[Rc], writes=[R("dsk")])
          run_il([ssd_front(0)])
          for lt in range(4):
              if lt + 1 < 4:
                  run_il([ssd_front(lt + 1), ssd_back(lt)])
              else:
                  run_il([ssd_back(lt)])
          if has_s:
              run_il([ssd_front(4)])
              run_il([ssd_back(4)])
          if has_s:
              for c in range(12):
                  bk, bR = nb()

                  def tco(pe, bk=bk, c=c):
                      pe.transpose(bk[0:48, 0:128], ncs[:, c, :], C("ident"))
                      return pe.transpose(bk[0:3, 128:256], ccar[:, c, :], C("ident"))
                  S.task("pe", tco, reads=[R("ncs"), R("ccar"), Rc], writes=[bR])

                  def tco2(act, bk=bk, c=c):
                      act.activation(out=sctok[0:48, c * 128:(c + 1) * 128], in_=bk[0:48, 0:128], func=AF.Copy)
                      return act.activation(out=stg[0:3, 0:128], in_=bk[0:3, 128:256], func=AF.Copy)
                  S.task("act", tco2, reads=[bR], writes=[R("sctok"), R("stg")])
                  dma_load("sp", ncv_p[:, c * 128:(c + 1) * 128], stg[0:3, 0:128], "ncv", [], reads=[R("stg")])
                  R("stg").r.append(("dq_ncv", S.dsem["ncv"][1]))
              dma_load("sp", ncv_s[:, :], sctok[0:48, :], "ncvs", [], reads=[R("sctok")])
          S.barrier(force=('C' in KBARS))
          chk('C', g)

          dma_multi("sp", [(cosT[:, 0:512], cos_d[:, g * 512:(g + 1) * 512]), (sinT[:, 0:512], sin_d[:, g * 512:(g + 1) * 512])], "rope", [R("rope")])
          if has_s:
              dma_multi("sp", [(cosT[:, 512:640], cos_d[:, 2048:2176]), (sinT[:, 512:640], sin_d[:, 2048:2176])], "rope", [R("rope")])

          def tz0(dve):
              dve.memset(kE[64:128, :, :], 0.0)
              dve.memset(kO[0:64, :, :], 0.0)
              return dve.memset(vaug[:, :, :, 64:128], 1.0)
          S.task("dve", tz0, writes=[R("kE"), R("kO"), R("vaug")])
          if KSTOP == 'B2a%d' % g:
              S.barrier()
              chk('B2a', g)
          for c in range(12):
              isk = c >= 8
              if c == 8 and KSTOP == 'B2b%d' % g:
                  S.barrier()
                  chk('B2b', g)
              if isk:
                  wt, Rw = load_panel(w_in, 1024 + (c - 8) * 64, dup=True)
              else:
                  wt, Rw = load_panel(w_in, c * 128)
              for (r0, r1) in ranges:
                  n = r1 - r0
                  bk, bR = nb()

                  def tm(pe, bk=bk, wt=wt, r0=r0, r1=r1):
                      for kc in range(8):
                          last = pe.matmul(bk[:, 0:r1 - r0], wt[:, kc, :], hT[:, kc, r0:r1], start=(kc == 0), stop=(kc == 7))
                      return last
                  S.task("pe", tm, reads=[Rw] + Rh, writes=[bR])

                  def tq1(act, bk=bk, n=n):
                      return act.activation(out=qb[:, 0:n], in_=bk[:, 0:n], func=AF.Copy)
                  S.task("act", tq1, reads=[bR], writes=[R("qb"), bR])

                  def tq2(dve, bk=bk, r0=r0, r1=r1, n=n):
                      return dve.tensor_tensor(out=t1[:, 0:n], in0=bk[:, 0:n], in1=cosT[:, r0:r1], op=ALU.mult)
                  S.task("dve", tq2, reads=[bR, R("rope")], writes=[R("t1"), bR])
                  bk2, bR2 = nb()

                  def tq3(pe, bk2=bk2, n=n):
                      return pe.matmul(bk2[:, 0:n], CB("prot"), qb[:, 0:n], start=True, stop=True)
                  S.task("pe", tq3, reads=[R("qb"), Rc], writes=[bR2])
                  if not isk:
                      S.chain("dve", [lambda dve, bk2=bk2, r0=r0, r1=r1, n=n: dve.tensor_tensor(out=rden[:, 0:n], in0=bk2[:, 0:n], in1=sinT[:, r0:r1], op=ALU.mult),
                                      lambda dve, c=c, r0=r0, r1=r1, n=n: dve.tensor_tensor(out=qT[:, c, r0:r1], in0=rden[:, 0:n], in1=t1[:, 0:n], op=ALU.add)],
                              reads=[bR2, R("t1"), R("rope")], writes=[R("qT"), R("rden")])
                  else:
                      kv = c - 8

                      def tk4c(dve, kv=kv, r0=r0, r1=r1, n=n, g=g):
                          dve.tensor_copy(out=kE[0:64, kv, r0:r1], in_=rden[0:64, 0:n])
                          last = dve.tensor_copy(out=kO[64:128, kv, r0:r1], in_=rden[64:128, 0:n])
                          if r0 == 512:
                              last = dve.tensor_copy(out=kfp[:, kv, :], in_=rden[:, 0:128])
                          elif g == 3:
                              last = dve.tensor_copy(out=kfp[:, kv, :], in_=rden[:, 384:512])
                          return last
                      S.chain("dve", [lambda dve, bk2=bk2, r0=r0, r1=r1, n=n: dve.tensor_tensor(out=rden[:, 0:n], in0=bk2[:, 0:n], in1=sinT[:, r0:r1], op=ALU.mult),
                                      lambda dve, n=n: dve.tensor_tensor(out=rden[:, 0:n], in0=rden[:, 0:n], in1=t1[:, 0:n], op=ALU.add),
                                      tk4c],
                              reads=[bR2, R("t1"), R("rope")], writes=[R("kE"), R("kO"), R("rden"), R("kfp")])
                      if g == 3:
                          bk3, bR3 = nb()

                          def tko(pe, bk3=bk3, kv=kv):
                              return pe.transpose(bk3[:, 0:128], kfp[:, kv, :], C("ident"))
                          S.task("pe", tko, reads=[R("kfp"), Rc], writes=[bR3])

                          ot = otok if r0 == 0 else otok2
                          otn = "otok" if r0 == 0 else "otok2"

                          def tko2(act, bk3=bk3, kv=kv, ot=ot):
                              return act.activation(out=ot[:, kv * 64:(kv + 1) * 64], in_=bk3[:, 0:64], func=AF.Copy)
                          S.task("act", tko2, reads=[bR3], writes=[R(otn)])
                          if kv == 3:
                              if r0 == 0:
                                  dma_load("sp", nk_p[:, :], ot[:, :], otn, [], reads=[R(otn)])
                              else:
                                  dma_multi("sp", [(nk_s[b, 120:128, :], ot[8 * b:8 * b + 8, :]) for b in range(16)], otn, [], reads=[R(otn)])
          if KSTOP == 'B2c%d' % g:
              S.barrier()
              chk('B2c', g)
          for lt in range(ntile):
              bk, bR = nb()

              def tv(pe, bk=bk, lt=lt):
                  for kc in range(8):
                      last = pe.matmul(bk[:, 0:256], hT[:, kc, lt * 128:(lt + 1) * 128], wv[:, kc, 0:256], start=(kc == 0), stop=(kc == 7))
                  return last
              S.task("pe", tv, reads=[R("wv")] + Rh, writes=[bR])

              def tv2(act, bk=bk, lt=lt):
                  vv = bk[:, 0:256].rearrange("p (k d) -> p k d", k=4)
                  act.activation(out=vaug[:, lt, :, 0:64], in_=vv, func=AF.Copy)
                  return act.activation(out=vaug[:, lt, :, 128:192], in_=vv, func=AF.Copy)
              S.task("act", tv2, reads=[bR], writes=[R("vaug"), bR])
              if g == 3 and lt >= 3:
                  def tv3(dve, bk=bk):
                      return dve.tensor_copy(out=vfp[:, :], in_=bk[:, 0:256])
                  S.task("dve", tv3, reads=[bR], writes=[R("vfp"), bR])
                  if lt == 3:
                      dma_load("sp", nv_p[:, :], vfp[:, :], "vfp", [], reads=[R("vfp")])
                  else:
                      dma_multi("sp", [(nv_s[b, 120:128, :], vfp[8 * b:8 * b + 8, :]) for b in range(16)], "vfp", [], reads=[R("vfp")])
                  R("vfp").r.append(("dq_vfp", S.dsem["vfp"][1]))
          if has_s:
              dma_multi("sp", [(nk_s[:, 0:120, :], ck[:, 8:128, :, :].rearrange("b s k d -> b s (k d)")),
                               (nv_s[:, 0:120, :], cv[:, 8:128, :, :].rearrange("b s k d -> b s (k d)"))], "cshift", [])

          S.barrier(force=('B2' in KBARS))
          chk('B2', g)
          srco = w_out.rearrange("(kc p) n -> p kc n", p=128)
          dma_multi("pool", [(wo[:, 4 * i:4 * i + 4, :], srco[:, 4 * i:4 * i + 4, :]) for i in range(4)], "wo", [R("wo")])
          attn_ctx = {}
          def att_s1(u, lt, kv):
              samp = (lt == 4)
              ci = 4 * g + lt
              cs = slice(lt * 128, (lt + 1) * 128)
              has_prev = (not samp) and ci > 0
              pp = u % 2
              pT = pTB[pp]
              if samp:
                  kcn_ = kcnB[kv % 2]
                  kcnn = "kcn" if kv % 2 == 0 else "kcn2"
                  dma_multi("pool", [(kcn_[:, :, 0:64], ck[:, :, kv, :].rearrange("b s d -> s b d")),
                                     (kcn_[:, :, 64:128], ck[:, :, kv, :].rearrange("b s d -> s b d"))], kcnn, [R(kcnn)])
                  dma_multi("pool", [(vca[:, :, 0:64], cv[:, :, kv, :].rearrange("b s d -> s b d")),
                                     (vca[:, :, 128:192], cv[:, :, kv, :].rearrange("b s d -> s b d"))], "vcaL", [R("vca")])

                  if kv == 0:
                      def tkc0(dve):
                          dve.memset(kcE[64:128, :, :], 0.0)
                          return dve.memset(kcO[0:64, :, :], 0.0)
                      S.task("dve", tkc0, reads=[], writes=[R("kcE"), R("kcO")])
                  S.task("dve", lambda dve: dve.memset(vca[:, :, 64:128], 1.0), reads=[], writes=[R("vca"), R("vca1")])
                  for b4 in range(4):
                      bk, bR = nb()

                      def tkc(pe, bk=bk, b4=b4, kcn_=kcn_):
                          for j in range(4):
                              last = pe.transpose(bk[:, j * 64:(j + 1) * 64].bitcast(BF16), kcn_[:, b4 * 4 + j, :], CB("ident"))
                          return last
                      S.task("pe", tkc, reads=[R(kcnn), Rc], writes=[bR])

                      def tkc2(act, bk=bk, b4=b4):
                          vv = bk[:, 0:256].bitcast(BF16).rearrange("p (j s) -> p j s", j=4)
                          act.activation(out=kcE[0:64, b4 * 4:b4 * 4 + 4, :], in_=vv[0:64], func=AF.Copy)
                          return act.activation(out=kcO[64:128, b4 * 4:b4 * 4 + 4, :], in_=vv[64:128], func=AF.Copy)
                      S.task("act", tkc2, reads=[bR], writes=[R("kcE"), R("kcO")])
                  bkc, bRc = nb()

                  def tsc(pe, bkc=bkc, kv=kv):
                      pe.matmul(bkc[:, :], CB("ident"), CB("mc"), start=True, stop=False)
                      for b in range(16):
                          for gq in range(4):
                              c = 2 * kv + gq // 2
                              kk = kcE if gq % 2 == 0 else kcO
                              last = pe.matmul(bkc[:, b * 32 + gq * 8:b * 32 + gq * 8 + 8], kk[:, b, :], qT[:, c, 512 + 8 * b:512 + 8 * b + 8],
                                               start=False, stop=(b == 15 and gq == 3))
                      return last
                  S.task("pe", tsc, reads=[R("kcE"), R("kcO"), R("qT"), Rc], writes=[bRc])

                  def tsc2(act, bkc=bkc):
                      return act.activation(out=pTc[:, :], in_=bkc[:, :], func=AF.Exp, scale=0.125)
                  S.task("act", tsc2, reads=[bRc], writes=[R("pTc")])
              sbk = []
              for j in ([0, 1] if has_prev else [1]):
                  bk, bR = nb()
                  sbk.append((j, bk, bR))

              def tsc_(pe, sbk=sbk, kv=kv, lt=lt, cs=cs, samp=samp):
                  for j, bk, bR in sbk:
                      mk = CB("mcur_s") if samp else (CB("mcur") if j == 1 else CB("mprev"))
                      pe.matmul(bk[:, :], CB("ident"), mk, start=True, stop=False)
                      for gq in range(4):
                          c = 2 * kv + gq // 2
                          if j == 1:
                              kk = (kE if gq % 2 == 0 else kO)[:, kv, cs]
                          elif lt == 0:
                              kk = (kE0 if gq % 2 == 0 else kO0)[:, kv, :]
                          else:
                              kk = (kE if gq % 2 == 0 else kO)[:, kv, (lt - 1) * 128:lt * 128]
                          last = pe.matmul(bk[:, gq * 128:(gq + 1) * 128], kk, qT[:, c, cs], start=False, stop=(gq == 3))
                  return last
              S.task("pe", tsc_, reads=[R("kE"), R("kO"), R("qT"), R("k0"), Rc], writes=[x[2] for x in sbk])

              def tex(act, sbk=sbk):
                  for j, bk, bR in sbk:
                      last = act.activation(out=pT[:, j, :], in_=bk[:, :], func=AF.Exp, scale=0.125)
                  return last
              S.task("act", tex, reads=[x[2] for x in sbk], writes=[R("pT%d" % pp)])
              attn_ctx[u] = sbk
          def att_s2(u, lt, kv):
              samp = (lt == 4)
              ci = 4 * g + lt
              cs = slice(lt * 128, (lt + 1) * 128)
              has_prev = (not samp) and ci > 0
              pp = u % 2
              pT = pTB[pp]
              sbk = attn_ctx.pop(u)
              pv, pvR = nb()

              def tpv(pe, pv=pv, sbk=sbk, kv=kv, lt=lt, samp=samp):
                  for eo in range(2):
                      o = pv[:, eo * 256:(eo + 1) * 256]
                      first = True
                      for j, bk, bR in sbk:
                          if j == 1:
                              va = vaug[:, lt, kv, eo * 64:eo * 64 + 128]
                          elif lt == 0:
                              va = va0[:, kv, eo * 64:eo * 64 + 128]
                          else:
                              va = vaug[:, lt - 1, kv, eo * 64:eo * 64 + 128]
                          pe.matmul(o, va, pT[:, j, :].rearrange("p (g q) -> p g q", g=4)[:, eo::2, :], start=first, stop=False)
                          first = False
                      if samp:
                          for b in range(16):
                              for g2 in range(2):
                                  pe.matmul(o[:, g2 * 128 + 8 * b:g2 * 128 + 8 * b + 8], vca[:, b, eo * 64:eo * 64 + 128],
                                            pTc[:, b * 32 + (2 * g2 + eo) * 8:b * 32 + (2 * g2 + eo) * 8 + 8], start=False, stop=False)
                      sk = CB("sinkE") if eo == 0 else CB("sinkO")
                      last = pe.matmul(o, sk, esr[:, 4 * kv + eo:4 * kv + 4:2, :], start=False, stop=True)
                  return last
              S.task("pe", tpv, reads=[R("pT%d" % pp), R("pTc"), R("vaug"), R("vca"), R("vca1"), R("k0"), R("esr"), Rc], writes=[pvR])

              def tno0(act, pv=pv):
                  act.activation(out=rden[64:128, 0:256], in_=pv[64:128, 0:256], func=AF.Ln)
                  return act.activation(out=rden[0:64, 256:512], in_=pv[0:64, 256:512], func=AF.Ln)

              def tno1(act):
                  act.activation(out=rden[64:128, 0:256], in_=rden[64:128, 0:256], func=AF.Exp, scale=-1.0)
                  return act.activation(out=rden[0:64, 256:512], in_=rden[0:64, 256:512], func=AF.Exp, scale=-1.0)

              def tno(dve, pv=pv, kv=kv, cs=cs):
                  dve.tensor_tensor(out=catT[0:64, 2 * kv:2 * kv + 2, cs], in0=pv[0:64, 0:256].rearrange("p (g q) -> p g q", g=2),
                                    in1=rden[64:128, 0:256].rearrange("p (g q) -> p g q", g=2), op=ALU.mult)
                  return dve.tensor_tensor(out=catT[64:128, 2 * kv:2 * kv + 2, cs], in0=pv[64:128, 256:512].rearrange("p (g q) -> p g q", g=2),
                                           in1=rden[0:64, 256:512].rearrange("p (g q) -> p g q", g=2), op=ALU.mult)
              S.chain("act", [tno0, tno1], reads=[pvR], writes=[R("rden"), pvR])
              S.task("dve", tno, reads=[pvR, R("rden")], writes=[R("catT"), pvR])
          units = [(lt, kv) for lt in range(4) for kv in range(4)]
          att_s1(0, *units[0])
          for u in range(len(units)):
              if u + 1 < len(units):
                  att_s1(u + 1, *units[u + 1])
              att_s2(u, *units[u])
          if has_s:
              for kv in range(4):
                  att_s1(16 + kv, 4, kv)
                  att_s2(16 + kv, 4, kv)
          def tcar(act):
              act.activation(out=kE0[:, :, :], in_=kE[:, :, 384:512], func=AF.Copy)
              act.activation(out=kO0[:, :, :], in_=kO[:, :, 384:512], func=AF.Copy)
              return act.activation(out=va0[:, :, :], in_=vaug[:, 3, :, :], func=AF.Copy)
          S.task("act", tcar, reads=[R("kE"), R("kO"), R("vaug")], writes=[R("k0")])
          S.barrier(force=('D' in KBARS))
          if KDBG and g == 3:
              dma_load("sp", dbg[:, :, :], catT[:, :, 512:640], "dbg", [], reads=[R("catT")])
              S.barrier()
          chk('D', g)

          srcd = w_down.rearrange("(kc p) n -> p kc n", p=128)
          dma_multi("pool", [(wd[:, 0:11, :], srcd[:, 0:11, :]), (wd[:, 11:22, :], srcd[:, 11:22, :])], "wd", [R("wd")])
          def e_s1(lt):
              hb = hbB[lt % 2]
              src = xs[:, :] if lt == 4 else xp[(4 * g + lt) * 128:(4 * g + lt + 1) * 128, :]
              dma_load("sp", xres[:, lt, :], src, "xres%d" % lt, [R("xres%d" % lt)])
              mb = [nb(), nb()]

              def tmo(pe, mb=mb, lt=lt):
                  for hf in range(2):
                      for kc in range(16):
                          last = pe.matmul(mb[hf][0][:, :], catT[:, kc, lt * 128:(lt + 1) * 128], wo[:, kc, hf * 512:(hf + 1) * 512],
                                           start=(kc == 0), stop=(kc == 15))
                  return last
              S.task("pe", tmo, reads=[R("catT"), R("wo")], writes=[mb[0][1], mb[1][1]])

              S.task("dve", lambda dve: dve.memset(stat[:, 4:12], 0.0), writes=[R("stat")])

              def tmo2(act, mb=mb):
                  act.activation(out=mixs[:, 0:512], in_=mb[0][0][:, :], func=AF.Copy)
                  return act.activation(out=mixs[:, 512:1024], in_=mb[1][0][:, :], func=AF.Copy)
              S.chain("act", [tmo2, lambda act: act.activation(out=hb[:, :], in_=mixs[:, :], func=AF.Square, accum_out=stat[:, 4:5]),
                              lambda act: act.activation(out=stat[:, 5:6], in_=stat[:, 4:5], func=AF.Ln, scale=1.0 / D, bias=EPS),
                              lambda act: act.activation(out=stat[:, 6:7], in_=stat[:, 5:6], func=AF.Exp, scale=-0.5)],
                      reads=[mb[0][1], mb[1][1]], writes=[R("mixs"), R("hb%d" % (lt % 2)), R("stat")])
              S.chain("dve", [lambda dve: dve.scalar_tensor_tensor(out=mixs[:, :], in0=mixs[:, :], scalar=stat[:, 6:7], in1=gpost, op0=ALU.mult, op1=ALU.mult),
                              lambda dve, lt=lt: dve.tensor_tensor(out=xres[:, lt, :], in0=xres[:, lt, :], in1=mixs[:, :], op=ALU.add)],
                      reads=[R("mixs"), R("stat"), Rc], writes=[R("mixs"), R("stat"), R("xres%d" % lt)])
              S.chain("act", [lambda act, lt=lt: act.activation(out=hb[:, :], in_=xres[:, lt, :], func=AF.Square, accum_out=stat[:, 8:9]),
                              lambda act: act.activation(out=stat[:, 9:10], in_=stat[:, 8:9], func=AF.Ln, scale=1.0 / D, bias=EPS),
                              lambda act: act.activation(out=stat[:, 10:11], in_=stat[:, 9:10], func=AF.Exp, scale=-0.5)],
                      reads=[R("xres%d" % lt)], writes=[R("hb%d" % (lt % 2)), R("stat")])
              S.chain("dve", [lambda dve, lt=lt: dve.scalar_tensor_tensor(out=hb[:, :], in0=xres[:, lt, :], scalar=stat[:, 10:11], in1=gffn, op0=ALU.mult, op1=ALU.mult)],
                      reads=[R("xres%d" % lt), R("stat"), Rc], writes=[R("stat"), R("hb%d" % (lt % 2))])
          def e_s2(lt):
              hb = hbB[lt % 2]
              bk, bR = nb()

              def tmo6(pe, bk=bk):
                  for kc in range(8):
                      last = pe.transpose(bk[:, kc * 64:(kc + 1) * 64].bitcast(BF16), hb[:, kc * 128:(kc + 1) * 128], CB("ident"))
                  return last
              S.task("pe", tmo6, reads=[R("hb%d" % (lt % 2)), Rc], writes=[bR])

              def tmo7(act, bk=bk, lt=lt):
                  return act.activation(out=hT[:, :, lt * 128:(lt + 1) * 128],
                                        in_=bk[:, :].bitcast(BF16).rearrange("p (c t) -> p c t", c=8), func=AF.Copy)
              S.task("act", tmo7, reads=[bR], writes=[Rh[lt]])
          e_s1(0)
          for lt in range(ntile):
              if lt + 1 < ntile:
                  e_s1(lt + 1)
              e_s2(lt)
          S.barrier(force=('E' in KBARS))
          chk('E', g)

          for m in range(22):
              if m % 2 == 0:
                  wgb_, Rg = load_big(w_gate, m * 128)
                  wub_, Ru = load_big(w_up, m * 128)
              wg_ = wgb_[:, :, (m % 2) * 128:(m % 2) * 128 + 128]
              wu_ = wub_[:, :, (m % 2) * 128:(m % 2) * 128 + 128]
              for (r0, r1) in ranges:
                  n = r1 - r0
                  bg, bgR = nb()
                  bu, buR = nb()

                  def tf(pe, bg=bg, bu=bu, wg_=wg_, wu_=wu_, r0=r0, r1=r1, n=n):
                      for kc in range(8):
                          pe.matmul(bg[:, 0:n], wg_[:, kc, :], hT[:, kc, r0:r1], start=(kc == 0), stop=(kc == 7))
                      for kc in range(8):
                          last = pe.matmul(bu[:, 0:n], wu_[:, kc, :], hT[:, kc, r0:r1], start=(kc == 0), stop=(kc == 7))
                      return last
                  S.task("pe", tf, reads=[Rg, Ru] + Rh, writes=[bgR, buR])

                  def tf2(act, bg=bg, n=n):
                      return act.activation(out=sg[:, 0:n], in_=bg[:, 0:n], func=AF.Silu)
                  S.task("act", tf2, reads=[bgR], writes=[R("sg")])

                  def tf3(dve, bu=bu, m=m, r0=r0, r1=r1, n=n):
                      return dve.tensor_tensor(out=hff[:, m, r0:r1], in0=sg[:, 0:n], in1=bu[:, 0:n], op=ALU.mult)
                  S.task("dve", tf3, reads=[buR, R("sg")], writes=[R("hff")])
          ga = {}
          ntn = 0 if g == 3 else (5 if g + 1 == 3 else 4)
          for lt in range(ntile):
              if lt < ntn:
                  ga[lt] = a_tile(g + 1, lt)
                  next(ga[lt])
              mb = [nb(), nb()]

              def td(pe, mb=mb, lt=lt):
                  for hf in range(2):
                      for kc in range(22):
                          last = pe.matmul(mb[hf][0][:, :], hff[:, kc, lt * 128:(lt + 1) * 128], wd[:, kc, hf * 512:(hf + 1) * 512],
                                           start=(kc == 0), stop=(kc == 21))
                  return last
              S.task("pe", td, reads=[R("hff"), R("wd")], writes=[mb[0][1], mb[1][1]])

              S.task("dve", lambda dve: dve.memset(stat[:, 12:13], 0.0), writes=[R("stat")])

              def td2(act, mb=mb):
                  act.activation(out=dns[:, 0:512], in_=mb[0][0][:, :], func=AF.Copy)
                  return act.activation(out=dns[:, 512:1024], in_=mb[1][0][:, :], func=AF.Copy)
              S.chain("act", [td2, lambda act, lt=lt: act.activation(out=hbB[1][:, :], in_=dns[:, :], func=AF.Square, accum_out=stat[:, 12:13]),
                              lambda act: act.activation(out=stat[:, 13:14], in_=stat[:, 12:13], func=AF.Ln, scale=1.0 / D, bias=EPS),
                              lambda act: act.activation(out=stat[:, 14:15], in_=stat[:, 13:14], func=AF.Exp, scale=-0.5)],
                      reads=[mb[0][1], mb[1][1]], writes=[R("mixs"), R("hb1"), R("stat")])
              S.chain("dve", [lambda dve: dve.scalar_tensor_tensor(out=dns[:, :], in0=dns[:, :], scalar=stat[:, 14:15], in1=gpff, op0=ALU.mult, op1=ALU.mult),
                              lambda dve, lt=lt: dve.tensor_tensor(out=xres[:, lt, :], in0=xres[:, lt, :], in1=dns[:, :], op=ALU.add)],
                      reads=[R("mixs"), R("stat"), Rc], writes=[R("mixs"), R("stat"), R("xres%d" % lt)])
              dst = y_s[:, :] if lt == 4 else y_p[(4 * g + lt) * 128:(4 * g + lt + 1) * 128, :]
              dma_load("sp", dst, xres[:, lt, :], "yout%d" % lt, [], reads=[R("xres%d" % lt)])
              if lt in ga:
                  for _ in ga[lt]:
                      pass
          for lt_ in range(ntile, ntn):
              for _ in a_tile(g + 1, lt_):
                  pass
          S.barrier(force=('F' in KBARS))

    except _Stop:
        pass
    S.barrier(force=True)
    S.emit()
    return nc


def _vecs(inp):
    v = np.zeros((128, 140), np.float32)
    v[:, 0:16] = inp["dt_bias"][0][None, :]
    v[:, 16:32] = inp["a_log"][0][None, :]
    v[:, 32:48] = inp["attn_sinks"][0][None, :]
    cw = inp["conv_w"][0]
    v[:, 48:96] = cw.reshape(4, 12, 128).transpose(2, 1, 0).reshape(128, 48)
    v[:, 96:108] = inp["conv_b"][0].reshape(12, 128).T
    v[:, 108:116] = inp["g_ssm_out"][0].reshape(8, 128).T
    v[:, 116:124] = np.repeat(inp["d_skip"][0].reshape(8, 2, 1), 64, axis=2).reshape(8, 128).T
    return v


_NC_CACHE = {}


def kernel(**inp):
    inp = {k: np.asarray(v) for k, v in inp.items()}
    if "nc" not in _NC_CACHE:
        _NC_CACHE["nc"] = build_nc()
    nc = _NC_CACHE["nc"]
    cf, cb, cos_t, sin_t = _host_consts()
    gt = np.concatenate([np.broadcast_to(inp[k][0][None, :], (128, D)) for k in ("g_pre_mix", "g_post_mix", "g_pre_ffn", "g_post_ffn")], axis=1)
    gt = np.ascontiguousarray(gt, dtype=np.float32)
    vecs = _vecs(inp)
    shared = {"w_in": np.ascontiguousarray(inp["w_in"][0]), "w_out": np.ascontiguousarray(inp["w_out"][0]),
              "w_gate": np.ascontiguousarray(inp["w_gate"][0]), "w_up": np.ascontiguousarray(inp["w_up"][0]),
              "w_down": np.ascontiguousarray(inp["w_down"][0]), "gtiles": gt, "vecs": vecs, "cf": cf, "cb": cb,
              "cos_t": cos_t, "sin_t": sin_t}
    in_maps = []
    for c in range(8):
        sl = slice(16 * c, 16 * c + 16)
        m = dict(shared)
        m["xp"] = np.ascontiguousarray(inp["x_prompt"][c])
        m["xs"] = np.ascontiguousarray(inp["x_sample"][sl].reshape(128, D))
        m["ck"] = np.ascontiguousarray(inp["cache_k_win"][0, sl])
        m["cv"] = np.ascontiguousarray(inp["cache_v_win"][0, sl])
        m["sconv"] = np.ascontiguousarray(inp["state_conv"][0, sl].reshape(48, 1536))
        m["sssm"] = np.ascontiguousarray(inp["state_ssm"][0, sl].reshape(16, 1024, 128))
        in_maps.append(m)
    res = run_bass_kernel_spmd(nc, in_maps, core_ids=list(range(8)))
    r = res.results

    def cat(name, shape):
        return np.stack([np.asarray(r[c][name]) for c in range(8)], 0).reshape(shape).astype(np.float32)
    y_p = cat("y_p", (8, 2048, D))
    y_s = cat("y_s", (128, 8, D))
    nk_p = cat("nk_p", (1, 8, 128, 4, 64))
    nv_p = cat("nv_p", (1, 8, 128, 4, 64))
    ncv_p = cat("ncv_p", (1, 8, 3, 1536))
    nssm_p = cat("nssm_p", (1, 8, 16, 64, 128))
    nk_s = cat("nk_s", (1, 128, 128, 4, 64))
    nv_s = cat("nv_s", (1, 128, 128, 4, 64))
    ncv_s = cat("ncv_s", (1, 128, 3, 1536))
    nssm_s = cat("nssm_s", (1, 128, 16, 64, 128))
    return (y_p, y_s, nk_p, nv_p, ncv_p, nssm_p, nk_s, nv_s, ncv_s, nssm_s)
```

```python
import numpy as np
import contextlib
import os
import concourse.bass as bass
import concourse.mybir as mybir
from concourse.bass_utils import run_bass_kernel_spmd

F32 = mybir.dt.float32
BF16 = mybir.dt.bfloat16
AF = mybir.ActivationFunctionType
ALU = mybir.AluOpType
AX = mybir.AxisListType

NEG = -30000.0
EPS = 1e-6
D = 1024
DFF = 2816
NIN = 4112
SB_BASE = 16512
SB_LIMIT = 229344


class Reg:
    __slots__ = ("w", "r", "name", "rng")

    def __init__(self, name, rng=None):
        self.w = None
        self.r = []
        self.name = name
        self.rng = rng


class Sched:
    ENG = ("pe", "act", "dve", "pool", "sp")

    def __init__(self, nc, stack):
        self.nc = nc
        self.h = {"pe": nc.tensor, "act": nc.scalar, "dve": nc.vector, "pool": nc.gpsimd, "sp": nc.sync}
        self.esem = {e: stack.enter_context(nc.semaphore("sem_" + e)) for e in self.ENG}
        self.ecnt = {e: 0 for e in self.ENG}
        self.known = {e: {} for e in self.ENG}
        self.prog = {e: [] for e in self.ENG}
        self.dsem = {}
        self.stack = stack
        self.semobj = {}
        self.ranged = []
        self.use_barriers = bool(os.environ.get("KBAR"))
        for e in self.ENG:
            self.semobj["sem_" + e] = self.esem[e]

    def dma_sem(self, key):
        if key not in self.dsem:
            s = self.stack.enter_context(self.nc.semaphore("dq_" + key))
            self.dsem[key] = [s, 0]
            self.semobj["dq_" + key] = s
        return self.dsem[key]

    def _waits(self, e, deps):
        waits = []
        kn = self.known[e]
        for k, v in deps.items():
            if kn.get(k, 0) < v:
                kn[k] = v
                waits.append((self.semobj[k], v))
        return waits

    def task(self, e, fn, reads=(), writes=(), dma=None, ndma=1):
        deps = {}

        def add(d):
            if d is not None and deps.get(d[0], 0) < d[1]:
                deps[d[0]] = d[1]
        for R in reads:
            add(R.w)
        for R in writes:
            add(R.w)
            for d in R.r:
                add(d)
            if R.rng:
                for Q in self.ranged:
                    if Q is R or (Q.w is None and not Q.r):
                        continue
                    hit = False
                    for (a0, a1) in R.rng:
                        for (b0, b1) in Q.rng:
                            if a0 < b1 and b0 < a1:
                                hit = True
                    if hit:
                        add(Q.w)
                        for d in Q.r:
                            add(d)
        waits = self._waits(e, deps)
        if dma is not None:
            ds = self.dma_sem(dma)
            ds[1] += 16 * ndma
            my = ("dq_" + dma, ds[1])
            sem = ds[0]
        else:
            self.ecnt[e] += 1
            my = ("sem_" + e, self.ecnt[e])
            sem = self.esem[e]
        for R in reads:
            R.r.append(my)
        for R in writes:
            R.w = my
            R.r = []
        self.prog[e].append((waits, fn, sem, dma is not None))

    def chain(self, e, fns, reads=(), writes=()):
        ch = Reg("chain")
        for f in fns:
            self.task(e, f, reads=list(reads), writes=list(writes) + [ch])

    def barrier(self, force=False):
        if not (force or self.use_barriers):
            return
        deps = {}
        for e in self.ENG:
            if self.ecnt[e]:
                deps["sem_" + e] = self.ecnt[e]
        for k, (s, c) in self.dsem.items():
            if c:
                deps["dq_" + k] = c
        for e in self.ENG:
            w = self._waits(e, dict(deps))
            if w:
                self.prog[e].append((w, None, None, False))

    def emit(self):
        nc = self.nc
        with nc.Block() as block:
            def run(e):
                def body(eng):
                    for waits, fn, sem, isdma in self.prog[e]:
                        for s, v in waits:
                            eng.wait_ge(s, v)
                        if fn is None:
                            continue
                        if isdma:
                            fn(eng, sem)
                        else:
                            last = fn(eng)
                            last.then_inc(sem, 1)
                return body
            block.tensor(run("pe"))
            block.scalar(run("act"))
            block.vector(run("dve"))
            block.gpsimd(run("pool"))
            block.sync(run("sp"))


def _host_consts():
    i = np.arange(128)
    seq = i // 8
    tri_p = (i[:, None] <= i[None, :]).astype(np.float32)
    same = (seq[:, None] == seq[None, :])
    tri_s = tri_p * same
    U = (i[:, None] > i[None, :]).astype(np.float32)
    ident = np.eye(128, dtype=np.float32)
    lastmask = (i[None, :] == (8 * seq[:, None] + 7)).astype(np.float32)
    oh = (seq[:, None] == np.arange(16)[None, :]).astype(np.float32)
    ones = np.ones((128, 128), np.float32)
    row0 = np.zeros((128, 1), np.float32)
    row0[0, 0] = 1.0
    cf = np.concatenate([ident, tri_p, tri_s, U, lastmask, oh, ones, row0], axis=1)
    mprev = np.where(i[:, None] >= i[None, :], 0.0, NEG)
    mcur = np.where(i[:, None] <= i[None, :], 0.0, NEG)
    mcur_s = np.where(same & (i[:, None] <= i[None, :]), 0.0, NEG)
    t8 = np.arange(8)
    mc = np.where(i[:, None] >= t8[None, :], 0.0, NEG)
    prot = np.zeros((128, 128), np.float32)
    for m in range(128):
        d = m % 64
        base = m - d
        if d < 8:
            prot[base + d + 8, m] = 1.0
        elif d < 16:
            prot[base + d - 8, m] = 1.0
    sel16 = np.zeros((128, 16, 128), np.float32)
    for h in range(16):
        sel16[h, h, :] = 1.0
    selpair = np.zeros((128, 8, 128), np.float32)
    for pr in range(8):
        selpair[2 * pr, pr, 0:64] = 1.0
        selpair[2 * pr + 1, pr, 64:128] = 1.0
    sinkE = np.zeros((128, 128), np.float32)
    sinkE[0, 64:128] = 1.0
    sinkO = np.zeros((128, 128), np.float32)
    sinkO[0, 0:64] = 1.0
    cb = np.concatenate([ident, prot, np.tile(mprev, (1, 4)), np.tile(mcur, (1, 4)), np.tile(mcur_s, (1, 4)),
                         np.tile(mc, (1, 64)), sel16.reshape(128, -1), selpair.reshape(128, -1), sinkE, sinkO,
                         ones, U], axis=1).astype(np.float32)
    pos = np.concatenate([np.arange(2048), 8192 + (np.arange(128) % 8)]).astype(np.float32)
    half = 8
    inv = (500000.0 ** (-np.arange(half, dtype=np.float32) * 2.0 / 16)).astype(np.float32)
    ang = pos[None, :] * inv[:, None]
    cosv = np.cos(ang).astype(np.float32)
    sinv = np.sin(ang).astype(np.float32)
    cos_t = np.ones((128, pos.size), np.float32)
    sin_t = np.zeros((128, pos.size), np.float32)
    for p in range(128):
        d = p % 64
        if d < 8:
            cos_t[p] = cosv[d]
            sin_t[p] = -sinv[d]
        elif d < 16:
            cos_t[p] = cosv[d - 8]
            sin_t[p] = sinv[d - 8]
    return cf, cb, cos_t, sin_t


CF_OFF = {}
_o = 0
for _n, _w in [("ident", 128), ("tri_p", 128), ("tri_s", 128), ("U", 128), ("lastmask", 128), ("oh", 16),
               ("ones", 128), ("row0", 1)]:
    CF_OFF[_n] = (_o, _o + _w)
    _o += _w
CF_W = _o
CB_OFF = {}
_o = 0
for _n, _w in [("ident", 128), ("prot", 128), ("mprev", 512), ("mcur", 512), ("mcur_s", 512), ("mc", 512),
               ("sel16", 2048), ("selpair", 1024), ("sinkE", 128), ("sinkO", 128), ("ones", 128), ("U", 128)]:
    CB_OFF[_n] = (_o, _o + _w)
    _o += _w
CB_W = _o


def build_nc():
    nc = bass.Bass("TRN2", target_bir_lowering=False)
    stack = contextlib.ExitStack()
    S = Sched(nc, stack)

    def din(name, shape):
        return nc.dram_tensor(name, list(shape), F32, kind="ExternalInput").ap()

    def dout(name, shape):
        return nc.dram_tensor(name, list(shape), F32, kind="ExternalOutput").ap()

    xp = din("xp", [2048, D]); xs = din("xs", [128, D])
    ck = din("ck", [16, 128, 4, 64]); cv = din("cv", [16, 128, 4, 64])
    sconv = din("sconv", [48, 1536]); sssm = din("sssm", [16, 1024, 128])
    w_in = din("w_in", [D, NIN]); w_out = din("w_out", [2048, D])
    w_gate = din("w_gate", [D, DFF]); w_up = din("w_up", [D, DFF]); w_down = din("w_down", [DFF, D])
    gtiles = din("gtiles", [128, 4 * D])
    vecs = din("vecs", [128, 48 + 60 + 16 + 16])
    cf_d = din("cf", [128, CF_W]); cb_d = din("cb", [128, CB_W])
    cos_d = din("cos_t", [128, 2176]); sin_d = din("sin_t", [128, 2176])

    y_p = dout("y_p", [2048, D]); y_s = dout("y_s", [128, D])
    nk_p = dout("nk_p", [128, 256]); nv_p = dout("nv_p", [128, 256])
    ncv_p = dout("ncv_p", [3, 1536]); nssm_p = dout("nssm_p", [1024, 128])
    nk_s = dout("nk_s", [16, 128, 256]); nv_s = dout("nv_s", [16, 128, 256])
    ncv_s = dout("ncv_s", [48, 1536]); nssm_s = dout("nssm_s", [16, 1024, 128])
    KDBG = os.environ.get('KDBG', '')
    if KDBG:
        dbg = nc.dram_tensor("dbg", [128, 16, 128], BF16, kind="ExternalOutput").ap()

    cur = [SB_BASE]
    bufrange = {}

    def sb(name, shape, dt, at=None):
        nbytes = int(np.prod(shape[1:])) * (4 if dt == F32 else 2)
        nbytes = (nbytes + 31) // 32 * 32
        if at is None:
            off = cur[0]
            cur[0] += nbytes
        else:
            off = at[0]
            at[0] += nbytes
        assert off + nbytes <= SB_LIMIT, (name, off, nbytes)
        if at is not None:
            bufrange[name] = (off, off + nbytes)
        return nc.alloc_sbuf_tensor_at(name, list(shape), dt, offset=off)

    G = 640
    cf = sb("cf", [128, CF_W], F32)
    cb = sb("cb", [128, CB_W], BF16)
    gt = sb("gt", [128, 4 * D], F32)
    vc = sb("vc", [128, 140], F32)
    hT = sb("hT", [128, 8, G], BF16)
    catT = sb("catT", [128, 16, G], BF16, cur)
    ovc = [bufrange["catT"][0]]
    wpx = [sb("wpx%d" % i, [128, 8, 128], BF16, ovc) for i in range(8)]
    assert ovc[0] <= bufrange["catT"][1]
    wp = [sb("wp%d" % i, [128, 8, 128], BF16, cur) for i in range(4)]
    wpb = []
    for i_, nm_ in enumerate(["wp0", "wp2", "wpx0", "wpx2", "wpx4", "wpx6"]):
        wpb.append(sb("wpb%d" % i_, [128, 8, 256], BF16, [bufrange[nm_][0]]))
    wv = sb("wv", [128, 8, 272], BF16)
    xin = [sb("xin%d" % i, [128, D], F32) for i in range(2)]
    hb = sb("hb", [128, D], BF16)
    stat = sb("stat", [128, 64], F32)
    ccar = sb("ccar", [128, 12, 3], F32)
    HT = sb("HT", [128, 1024], F32)
    HTb = sb("HTb", [128, 1024], BF16)
    kE0 = sb("kE0", [128, 4, 128], BF16); kO0 = sb("kO0", [128, 4, 128], BF16)
    va0 = sb("va0", [128, 4, 192], BF16)
    dtt = sb("dtt", [128, 5, 16], F32)
    dtA = sb("dtA", [128, 5, 16], F32)
    abc = sb("abc", [128, 16], F32)
    esr = sb("esr", [128, 16, 128], BF16)
    es = sb("es", [128, 16], F32)
    kfp = sb("kfp", [128, 4, 128], F32)
    vfp = sb("vfp", [128, 256], F32)
    otok = sb("otok", [128, 256], F32)
    otok2 = sb("otok2", [128, 256], F32)
    ARENA = cur[0]
    a = [ARENA]
    szT = sb("szT", [128, 8, G], BF16, a)
    xbcT = sb("xbcT", [128, 12, G], BF16, a)
    stg = sb("stg", [128, 3 + 512], F32, a)
    sstg = sb("sstg", [128, 16, 11], F32, a)
    acc = sb("acc", [128, 512], F32, a)
    cstT = sb("cstT", [128, 12, 48], F32, a)
    ncs = sb("ncs", [128, 12, 48], F32, a)
    sctok = sb("sctok", [48, 1536], F32, a)
    X = sb("X", [128, 16, 128], F32, a)
    Xb = nc.alloc_sbuf_tensor_at("Xb", [128, 16, 128], BF16, offset=bufrange["X"][0])
    dec0 = sb("dec0", [128, 16, 128], BF16, a)
    eac = sb("eac", [128, 16, 128], BF16, a)
    CdT0 = sb("CdT0", [128, 16, 128], BF16, a)
    xdt0 = sb("xdt0", [128, 16, 64], BF16, a)
    xtail0 = sb("xtail0", [128, 16, 64], BF16, a)
    Btok0 = sb("Btok0", [128, 2, 128], BF16, a)
    cbm = sb("cbm", [128, 2, 128], BF16, a)
    acT = sb("acT", [128, 128], F32, a)
    achi = sb("achi", [128, 128], BF16, a)
    aclo = sb("aclo", [128, 128], BF16, a)
    eal0 = sb("eal0", [128, 16], F32, a)
    tailc = sb("tailc", [128, 16], F32, a)
    gated = sb("gated", [128, 8, 128], F32, a)
    gsq = sb("gsq", [128, 8, 128], BF16, a)
    rs = sb("rs", [128, 2, 128], F32, a)
    h0n = sb("h0n", [128, 8, 128], F32, a)
    h0T0 = sb("h0T0", [128, 1024], BF16, a)
    h0T1 = sb("h0T1", [128, 1024], BF16, a)
    h0TB = [h0T0, h0T1]
    Bm = sb("Bm", [128, 16, 256], BF16, a)
    decs = sb("decs", [128, 8, 16], F32, a)
    hout = sb("hout", [128, 8, 128], F32, a)
    tmpd = sb("tmpd", [128, 16, 128], BF16, a)
    h0n2 = sb("h0n2", [128, 8, 128], F32, a)
    h0n3 = sb("h0n3", [128, 8, 128], F32, [bufrange["tmpd"][0]])
    hout2 = sb("hout2", [128, 8, 128], F32, [bufrange["X"][0]])
    stg2 = sb("stg2", [128, 3 + 512], F32, [bufrange["X"][0] + 4096])
    acc2 = sb("acc2", [128, 512], F32, [bufrange["dec0"][0]])
    stgB, accB = [stg, stg2], [acc, acc2]
    SSD_END = a[0]
    ov = [bufrange["Bm"][0]]
    dec1 = sb("dec1", [128, 16, 128], BF16, ov)
    CdT1 = sb("CdT1", [128, 16, 128], BF16, ov)
    assert ov[0] <= bufrange["Bm"][1]
    ov = [bufrange["tmpd"][0]]
    xdt1 = sb("xdt1", [128, 16, 64], BF16, ov)
    xtail1 = sb("xtail1", [128, 16, 64], BF16, ov)
    assert ov[0] <= bufrange["tmpd"][1]
    ov = [bufrange["h0n"][0]]
    Btok1 = sb("Btok1", [128, 2, 128], BF16, ov)
    eal1 = sb("eal1", [128, 16], F32, ov)
    assert ov[0] <= bufrange["h0n"][1]
    decB, CdTB, xdtB, xtailB, BtokB, ealB = [dec0, dec1], [CdT0, CdT1], [xdt0, xdt1], [xtail0, xtail1], [Btok0, Btok1], [eal0, eal1]
    a = [ARENA]
    qT = sb("qT", [128, 8, G], BF16, a)
    kE = sb("kE", [128, 4, G], BF16, a)
    kO = sb("kO", [128, 4, G], BF16, a)
    vaug = sb("vaug", [128, 5, 4, 192], BF16, a)
    cosT = sb("cosT", [128, G], F32, a)
    sinT = sb("sinT", [128, G], F32, a)
    qb = sb("qb", [128, 512], BF16, a)
    t1 = sb("t1", [128, 512], F32, a)
    pT0 = sb("pT0", [128, 2, 512], BF16, a)
    pT1 = sb("pT1", [128, 2, 512], BF16, a)
    pTB = [pT0, pT1]
    rden = sb("rden", [128, 512], F32, a)
    kcn = sb("kcn", [128, 16, 128], BF16, a)
    kcE = sb("kcE", [128, 16, 128], BF16, a)
    kcO = sb("kcO", [128, 16, 128], BF16, a)
    vca = sb("vca", [128, 16, 192], BF16, a)
    pTc = sb("pTc", [128, 512], BF16, a)
    kcn2 = sb("kcn2", [128, 16, 128], BF16, a)
    kcnB = [kcn, kcn2]
    ATT_END = a[0]
    a = [ARENA]
    wd = sb("wd", [128, 22, D], BF16, a)
    xres = sb("xres", [128, 5, D], F32, a)
    mixs = sb("mixs", [128, D], F32, a)
    dns = mixs
    assert a[0] >= ATT_END, (a[0], ATT_END)
    a2 = [a[0]]
    wo = sb("wo", [128, 16, D], BF16, a)
    hff = sb("hff", [128, 22, G], BF16, a2)
    sg = sb("sg", [128, 512], BF16, a)
    hb1 = sb("hb1", [128, D], BF16, a)
    hbB = [hb, hb1]
    E_END = a[0]
    F_END = a2[0]
    assert max(SSD_END, ATT_END, E_END, F_END) <= SB_LIMIT, (SSD_END, ATT_END, E_END, F_END)

    banks = [stack.enter_context(nc.psum_tensor("bank%d" % i, [128, 512], F32)) for i in range(8)]
    bankR = [Reg("bank%d" % i) for i in range(8)]
    bctr = [0]

    reserved = set()

    def nb():
        while True:
            i = bctr[0] % 8
            bctr[0] += 1
            if i not in reserved:
                return banks[i], bankR[i]

    regs = {}

    special = {"ac": ["acT", "achi", "aclo"], "rope": ["cosT", "sinT"], "vca1": ["vca"]}

    def R(name):
        if name not in regs:
            rng = None
            if name in bufrange:
                rng = [bufrange[name]]
            elif name in special:
                rng = [bufrange[b] for b in special[name]]
            elif name.startswith("xres") and name[4:].isdigit():
                lo = bufrange["xres"][0] + int(name[4:]) * D * 4
                rng = [(lo, lo + D * 4)]
            regs[name] = Reg(name, rng)
            if rng:
                S.ranged.append(regs[name])
        return regs[name]

    def C(n, pack=cf, off=CF_OFF):
        a0, a1 = off[n]
        return pack[:, a0:a1]

    def CB(n):
        a0, a1 = CB_OFF[n]
        return cb[:, a0:a1]

    def dma_load(eng, out_ap, in_ap, key, writes, reads=()):
        def fn(e, sem):
            e.dma_start(out=out_ap, in_=in_ap).then_inc(sem, 16)
        S.task(eng, fn, reads=reads, writes=writes, dma=key)

    def dma_multi(eng, pairs, key, writes, reads=(), slow=False):
        def fn(e, sem):
            for o, i_ in pairs:
                if slow:
                    e.dma_start(out=o, in_=i_, allow_slow_non_contiguous=True).then_inc(sem, 16)
                else:
                    e.dma_start(out=o, in_=i_).then_inc(sem, 16)
        S.task(eng, fn, reads=reads, writes=writes, dma=key, ndma=len(pairs))

    Rc = R("consts")
    dma_load("sp", cf[:, :], cf_d[:, :], "c0", [R("c0")])
    dma_load("sp", gt[:, :], gtiles[:, :], "c1", [R("c1")])
    dma_load("sp", vc[:, :], vecs[:, :], "c2", [R("c2")])
    dma_load("pool", cb[:, :], cb_d[:, :], "c3", [R("c3")])
    S.task("dve", lambda dve: dve.memset(stat[:, 60:64], 0.0), reads=[R("c0"), R("c1"), R("c2"), R("c3")], writes=[Rc])
    gpre, gpost, gffn, gpff = (gt[:, i * D:(i + 1) * D] for i in range(4))
    dtb, alog, sinks = vc[:, 0:16], vc[:, 16:32], vc[:, 32:48]
    convw = vc[:, 48:96]
    convb = vc[:, 96:108]
    gssm = vc[:, 108:116]
    dskipc = vc[:, 116:124]

    def t_init(act):
        act.activation(out=abc[:, :], in_=alog, func=AF.Exp)
        return act.activation(out=es[:, :], in_=sinks, func=AF.Exp)
    S.task("act", t_init, reads=[Rc], writes=[R("abc"), R("es")])

    def t_init2(dve):
        dve.tensor_scalar(out=abc[:, :], in0=abc[:, :], scalar1=-1.0, scalar2=None, op0=ALU.mult)
        dve.memset(ccar[:, :, :], 0.0)
        dve.memset(HT[:, :], 0.0)
        dve.memset(HTb[:, :], 0.0)
        dve.memset(stat[:, :], 0.0)
        return dve.tensor_scalar(out=esr[:, :, :], in0=es[:, :].unsqueeze(2).to_broadcast([128, 16, 128]),
                                 scalar1=C("row0"), scalar2=None, op0=ALU.mult)
    S.task("dve", t_init2, reads=[Rc, R("abc"), R("es")], writes=[R("abc"), R("esr"), R("ccar"), R("HT"), R("HTb"), R("stat")])

    wslot = [0]

    fslot = [0]
    bslot = [0]

    def load_big(dram_w, c0):
        j = bslot[0] % 6
        bslot[0] += 1
        t, nm = wpb[j], "wpb%d" % j
        Rw = R(nm)
        src = dram_w.rearrange("(kc p) n -> p kc n", p=128)
        dma_multi("pool", [(t[:, :, :], src[:, :, c0:c0 + 256])], nm, [Rw])
        return t, Rw

    def load_panel(dram_w, c0, ncols=128, dup=False, deep=False):
        if deep:
            j = fslot[0] % 12
            fslot[0] += 1
            if j < 4:
                t, nm = wp[j], "wp%d" % j
            else:
                t, nm = wpx[j - 4], "wpx%d" % (j - 4)
            Rw = R(nm)
            src = dram_w.rearrange("(kc p) n -> p kc n", p=128)
            dma_multi("pool", [(t[:, :, 0:ncols], src[:, :, c0:c0 + ncols])], nm, [Rw])
            return t, Rw
        i = wslot[0] % 4
        wslot[0] += 1
        t = wp[i]
        Rw = R("wp%d" % i)
        src = dram_w.rearrange("(kc p) n -> p kc n", p=128)
        if dup:
            pairs = [(t[:, :, 0:64], src[:, :, c0:c0 + 64]), (t[:, :, 64:128], src[:, :, c0:c0 + 64])]
        else:
            pairs = [(t[:, :, 0:ncols], src[:, :, c0:c0 + ncols])]
        dma_multi("pool", pairs, "wp%d" % i, [Rw])
        return t, Rw

    KSTOP = os.environ.get('KSTOP', '')
    KBARS = set(os.environ.get('KBARS', '').split(','))

    class _Stop(Exception):
        pass

    def chk(tag, g):
        if KSTOP == tag + str(g):
            raise _Stop()
    try:
      for g in range(4):
          has_s = (g == 3)
          ntile = 5 if has_s else 4
          NP = 512
          ranges = [(0, 512)] + ([(512, 640)] if has_s else [])
          Rh = [R("hT%d" % t) for t in range(5)]

          def a_tile(g, lt):
              Rh = [R("hT%d" % t) for t in range(5)]
              if True:
                  xi = xin[lt % 2]
                  Rx = R("xin%d" % (lt % 2))
                  src = xs[:, :] if lt == 4 else xp[(4 * g + lt) * 128:(4 * g + lt + 1) * 128, :]
                  dma_load("sp", xi[:, :], src, "xin%d" % (lt % 2), [Rx])
                  Rst = R("stat")

                  S.task("dve", lambda dve: dve.memset(stat[:, 0:1], 0.0), writes=[Rst])
                  S.chain("act", [lambda act, xi=xi: act.activation(out=hb[:, :], in_=xi[:, :], func=AF.Square, accum_out=stat[:, 0:1]),
                                  lambda act: act.activation(out=stat[:, 1:2], in_=stat[:, 0:1], func=AF.Ln, scale=1.0 / D, bias=EPS),
                                  lambda act: act.activation(out=stat[:, 2:3], in_=stat[:, 1:2], func=AF.Exp, scale=-0.5)],
                          reads=[Rx], writes=[Rst, R("hb0")])
                  S.chain("dve", [lambda dve, xi=xi: dve.scalar_tensor_tensor(out=hb[:, :], in0=xi[:, :], scalar=stat[:, 2:3], in1=gpre,
                                                                             op0=ALU.mult, op1=ALU.mult)],
                          reads=[Rx, Rst, Rc], writes=[Rst, R("hb0")])
                  yield
                  bk, bR = nb()

                  def tA3(pe, bk=bk):
                      for kc in range(8):
                          last = pe.transpose(bk[:, kc * 64:(kc + 1) * 64].bitcast(BF16), hb[:, kc * 128:(kc + 1) * 128], CB("ident"))
                      return last
                  S.task("pe", tA3, reads=[R("hb0"), Rc], writes=[bR])

                  def tA4(act, bk=bk, lt=lt):
                      return act.activation(out=hT[:, :, lt * 128:(lt + 1) * 128],
                                            in_=bk[:, :].bitcast(BF16).rearrange("p (c t) -> p c t", c=8), func=AF.Copy)
                  S.task("act", tA4, reads=[bR], writes=[Rh[lt]])
          def phaseA(g):
              for lt_ in range(5 if g == 3 else 4):
                  for _ in a_tile(g, lt_):
                      pass
          if g == 0:
              phaseA(0)
          S.barrier(force=('A' in KBARS))
          chk('A', g)

          srcw = w_in.rearrange("(kc p) n -> p kc n", p=128)
          dma_multi("pool", [(wv[:, :, 0:256], srcw[:, :, 1280:1536]), (wv[:, :, 256:272], srcw[:, :, 4096:4112])], "wv", [R("wv")])
          if has_s and True:
              dma_load("sp", sctok[:, :], sconv[:, :], "sct", [R("sctok")])
              for c in range(12):
                  bk, bR = nb()

                  def tcs(pe, bk=bk, c=c):
                      return pe.transpose(bk[:, 0:48], sctok[0:48, c * 128:(c + 1) * 128], C("ident")[0:48, 0:48])
                  S.task("pe", tcs, reads=[R("sctok"), Rc], writes=[bR])

                  def tcs2(act, bk=bk, c=c):
                      return act.activation(out=cstT[:, c, :], in_=bk[:, 0:48], func=AF.Copy)
                  S.task("act", tcs2, reads=[bR], writes=[R("cstT")])
          for c in range(12):
              wt, Rw = load_panel(w_in, 1536 + 1024 + c * 128, deep=True)
              for (r0, r1) in ranges:
                  bk, bR = nb()

                  def tm(pe, bk=bk, wt=wt, r0=r0, r1=r1):
                      for kc in range(8):
                          last = pe.matmul(bk[:, 0:r1 - r0], wt[:, kc, :], hT[:, kc, r0:r1], start=(kc == 0), stop=(kc == 7))
                      return last
                  S.task("pe", tm, reads=[Rw] + Rh, writes=[bR])
                  if r0 == 0:
                      sg_, an_ = stgB[c % 2], accB[c % 2]
                      sgn, ann = ("stg", "acc") if c % 2 == 0 else ("stg2", "acc2")
                      S.chain("act", [lambda act, c=c, sg_=sg_: act.activation(out=sg_[:, 0:3], in_=ccar[:, c, :], func=AF.Copy),
                                      lambda act, bk=bk, sg_=sg_: act.activation(out=sg_[:, 3:515], in_=bk[:, 0:512], func=AF.Copy),
                                      lambda act, c=c, sg_=sg_: act.activation(out=ccar[:, c, :], in_=sg_[:, 512:515], func=AF.Copy)],
                              reads=[bR, R("ccar")], writes=[R(sgn), R("ccar")])
                      fl = [lambda dve, c=c, sg_=sg_, an_=an_: dve.tensor_scalar(out=an_[:, :], in0=sg_[:, 0:512], scalar1=convw[:, c * 4:c * 4 + 1], scalar2=None, op0=ALU.mult)]
                      for tap in range(1, 4):
                          fl.append(lambda dve, c=c, tap=tap, sg_=sg_, an_=an_: dve.scalar_tensor_tensor(out=an_[:, :], in0=sg_[:, tap:tap + 512], scalar=convw[:, c * 4 + tap:c * 4 + tap + 1],
                                                                                                       in1=an_[:, :], op0=ALU.mult, op1=ALU.add))
                      S.chain("dve", fl, reads=[R(sgn), Rc], writes=[R(ann)])

                      def tc3(act, c=c, an_=an_):
                          return act.activation(out=xbcT[:, c, 0:512], in_=an_[:, :], func=AF.Silu, bias=convb[:, c:c + 1], scale=1.0)
                      S.task("act", tc3, reads=[R(ann), Rc], writes=[R("xbcT")])
                  else:
                      S.chain("act", [lambda act, c=c: act.activation(out=sstg[:, :, 0:3], in_=cstT[:, c, :].rearrange("p (b t) -> p b t", t=3), func=AF.Copy),
                                      lambda act, bk=bk: act.activation(out=sstg[:, :, 3:11], in_=bk[:, 0:128].rearrange("p (b t) -> p b t", t=8), func=AF.Copy),
                                      lambda act, c=c: act.activation(out=ncs[:, c, :].rearrange("p (b t) -> p b t", t=3), in_=sstg[:, :, 8:11], func=AF.Copy)],
                              reads=[bR, R("cstT")], writes=[R("sstg"), R("ncs")])
                      av = acc[:, 0:128].rearrange("p (b t) -> p b t", t=8)
                      fl = [lambda dve, c=c, av=av: dve.tensor_scalar(out=av, in0=sstg[:, :, 0:8], scalar1=convw[:, c * 4:c * 4 + 1], scalar2=None, op0=ALU.mult)]
                      for tap in range(1, 4):
                          fl.append(lambda dve, c=c, tap=tap, av=av: dve.scalar_tensor_tensor(out=av, in0=sstg[:, :, tap:tap + 8], scalar=convw[:, c * 4 + tap:c * 4 + tap + 1],
                                                                                              in1=av, op0=ALU.mult, op1=ALU.add))
                      S.chain("dve", fl, reads=[R("sstg"), Rc], writes=[R("acc")])

                      def ts3(act, c=c):
                          return act.activation(out=xbcT[:, c, 512:640], in_=acc[:, 0:128], func=AF.Silu, bias=convb[:, c:c + 1], scale=1.0)
                      S.task("act", ts3, reads=[R("acc"), Rc], writes=[R("xbcT")])
          for c in range(8):
              wt, Rw = load_panel(w_in, 1536 + c * 128, deep=True)
              for (r0, r1) in ranges:
                  bk, bR = nb()

                  def tm(pe, bk=bk, wt=wt, r0=r0, r1=r1):
                      for kc in range(8):
                          last = pe.matmul(bk[:, 0:r1 - r0], wt[:, kc, :], hT[:, kc, r0:r1], start=(kc == 0), stop=(kc == 7))
                      return last
                  S.task("pe", tm, reads=[Rw] + Rh, writes=[bR])

                  def tz(act, bk=bk, c=c, r0=r0, r1=r1):
                      return act.activation(out=szT[:, c, r0:r1], in_=bk[:, 0:r1 - r0], func=AF.Silu)
                  S.task("act", tz, reads=[bR], writes=[R("szT")])
          for lt in range(ntile):
              bk, bR = nb()

              def tdt(pe, bk=bk, lt=lt):
                  for kc in range(8):
                      last = pe.matmul(bk[:, 0:16], hT[:, kc, lt * 128:(lt + 1) * 128], wv[:, kc, 256:272], start=(kc == 0), stop=(kc == 7))
                  return last
              S.task("pe", tdt, reads=[R("wv")] + Rh, writes=[bR])

              def tdt2(dve, bk=bk, lt=lt):
                  return dve.tensor_tensor(out=dtt[:, lt, :], in0=bk[:, 0:16], in1=dtb, op=ALU.add)
              S.task("dve", tdt2, reads=[bR, Rc], writes=[R("dtt")])
          S.chain("act", [lambda act, ntile=ntile: act.activation(out=dtt[:, 0:ntile, :], in_=dtt[:, 0:ntile, :], func=AF.Exp),
                          lambda act, ntile=ntile: act.activation(out=dtt[:, 0:ntile, :], in_=dtt[:, 0:ntile, :], func=AF.Ln, bias=1.0, scale=1.0)],
                  reads=[R("dtt")], writes=[R("dtt")])

          def tdt4(dve, ntile=ntile):
              return dve.tensor_tensor(out=dtA[:, 0:ntile, :], in0=dtt[:, 0:ntile, :], in1=abc[:, :].unsqueeze(1).to_broadcast([128, ntile, 16]), op=ALU.mult)
          S.task("dve", tdt4, reads=[R("dtt"), R("abc")], writes=[R("dtA")])

          S.barrier(force=('B1' in KBARS))
          chk('B1', g)
          def ssd_front(lt):
              par = (lt % 2) if lt < 4 else 0
              dec, CdT, xdt, xtail, Btok, eal = decB[par], CdTB[par], xdtB[par], xtailB[par], BtokB[par], ealB[par]
              samp = (lt == 4)
              ci = 4 * g + lt
              cs = slice(lt * 128, (lt + 1) * 128)
              tri = C("tri_s") if samp else C("tri_p")
              RS = R("ssdtmp")
              bk, bR = nb()

              def tac(pe, bk=bk, lt=lt, tri=tri):
                  pe.matmul(bk[0:16, 0:128], dtA[:, lt, :], tri, start=True, stop=True)
                  return pe.matmul(bk[:, 128:144], C("ones"), dtA[:, lt, :], start=True, stop=True)
              S.task("pe", tac, reads=[R("dtA"), Rc], writes=[bR])

              S.chain("dve", [lambda dve, bk=bk: dve.tensor_copy(out=acT[0:16, :], in_=bk[0:16, 0:128]),
                              lambda dve: dve.tensor_copy(out=achi[0:16, :], in_=acT[0:16, :]),
                              lambda dve: dve.tensor_tensor(out=aclo[0:16, :], in0=acT[0:16, :], in1=achi[0:16, :], op=ALU.subtract)],
                      reads=[bR], writes=[R("ac"), bR])

              def tac3(act, bk=bk):
                  return act.activation(out=eal[:, :], in_=bk[:, 128:144], func=AF.Exp)
              S.task("act", tac3, reads=[bR], writes=[R("eal%d" % par), bR])
              def tX(dve, lt=lt, tri=tri):
                  return dve.tensor_tensor(out=Xb[:, :, :], in0=tri.unsqueeze(1).to_broadcast([128, 16, 128]),
                                           in1=dtA[:, lt, :].unsqueeze(2).to_broadcast([128, 16, 128]), op=ALU.mult)
              S.task("dve", tX, reads=[R("dtA"), Rc], writes=[R("X")])
              yield
              sb_ = [nb() for _ in range(4)]

              def tseg(pe, sb_=sb_):
                  for q4 in range(4):
                      last = pe.matmul(sb_[q4][0][:, :], CB("U"), Xb[:, q4 * 4:(q4 + 1) * 4, :], start=True, stop=True)
                  return last
              S.task("pe", tseg, reads=[R("X"), Rc], writes=[b[1] for b in sb_])

              def tdec(act, sb_=sb_):
                  for q4 in range(4):
                      last = act.activation(out=dec[:, q4 * 4:(q4 + 1) * 4, :], in_=sb_[q4][0][:, :].rearrange("p (h t) -> p h t", h=4), func=AF.Exp)
                  return last
              S.task("act", tdec, reads=[b[1] for b in sb_], writes=[R("dec%d" % par)])
              yield
              bk, bR = nb()

              def tcb(pe, bk=bk, cs=cs):
                  for gg in range(2):
                      last = pe.matmul(bk[:, gg * 128:(gg + 1) * 128], xbcT[:, 8 + gg, cs], xbcT[:, 10 + gg, cs], start=True, stop=True)
                  return last
              S.task("pe", tcb, reads=[R("xbcT")], writes=[bR])

              def tcb2(dve, bk=bk, tri=tri):
                  return dve.tensor_tensor(out=cbm[:, :, :], in0=bk[:, 0:256].rearrange("p (g t) -> p g t", g=2),
                                           in1=tri.unsqueeze(1).to_broadcast([128, 2, 128]), op=ALU.mult)
              S.task("dve", tcb2, reads=[bR, Rc], writes=[R("cbm")])
              if samp:
                  S.chain("dve", [lambda dve: dve.tensor_tensor(out=tmpd[:, :, :], in0=dec[:, :, :], in1=C("lastmask").unsqueeze(1).to_broadcast([128, 16, 128]), op=ALU.mult),
                                  lambda dve: dve.tensor_reduce(out=tailc[:, :], in_=tmpd[:, :, :], axis=AX.X, op=ALU.add)],
                          reads=[R("dec%d" % par), Rc], writes=[R("tailc"), R("tmpd")])
              else:
                  def ttl(dve):
                      return dve.tensor_copy(out=tailc[:, :], in_=dec[:, :, 127])
                  S.task("dve", ttl, reads=[R("dec%d" % par)], writes=[R("tailc")])
              bk, bR = nb()
              bk2, bR2 = nb()

              def ttr(pe, bk=bk, bk2=bk2, cs=cs):
                  for c in range(8):
                      pe.transpose(bk[:, c * 64:(c + 1) * 64].bitcast(BF16), xbcT[:, c, cs], CB("ident"))
                  for gg in range(2):
                      last = pe.transpose(bk2[:, gg * 64:(gg + 1) * 64].bitcast(BF16), xbcT[:, 8 + gg, cs], CB("ident"))
                  return last
              S.task("pe", ttr, reads=[R("xbcT"), Rc], writes=[bR, bR2])

              S.chain("dve", [lambda dve, bk=bk, lt=lt: dve.tensor_tensor(out=xdt[:, :, :], in0=bk[:, :].bitcast(BF16).rearrange("p (h d) -> p h d", h=16),
                                                                          in1=dtt[:, lt, :].unsqueeze(2).to_broadcast([128, 16, 64]), op=ALU.mult),
                              lambda dve: dve.tensor_tensor(out=xtail[:, :, :], in0=xdt[:, :, :], in1=tailc[:, :].unsqueeze(2).to_broadcast([128, 16, 64]), op=ALU.mult),
                              lambda dve, bk2=bk2: dve.tensor_copy(out=Btok[:, :, :], in_=bk2[:, 0:128].bitcast(BF16).rearrange("p (g n) -> p g n", g=2))],
                      reads=[bR, bR2, R("dtt"), R("tailc")], writes=[R("xdt%d" % par), R("xtail%d" % par), R("Btok%d" % par)])
              yield
              eb = [nb() for _ in range(4)]

              def teac(pe, eb=eb):
                  for h in range(16):
                      o = eb[h // 4][0][:, (h % 4) * 128:(h % 4 + 1) * 128]
                      a0 = CB_OFF["sel16"][0] + h * 128
                      pe.matmul(o, cb[0:16, a0:a0 + 128], achi[0:16, :], start=True, stop=False)
                      last = pe.matmul(o, cb[0:16, a0:a0 + 128], aclo[0:16, :], start=False, stop=True)
                  return last
              S.task("pe", teac, reads=[R("ac"), Rc], writes=[b[1] for b in eb])

              def teac2(act, eb=eb):
                  for q4 in range(4):
                      last = act.activation(out=eac[:, q4 * 4:(q4 + 1) * 4, :], in_=eb[q4][0][:, :].rearrange("p (h t) -> p h t", h=4), func=AF.Exp)
                  return last
              S.task("act", teac2, reads=[b[1] for b in eb], writes=[R("eac")])
              def twt(dve, cs=cs):
                  return dve.tensor_tensor(out=dec[:, :, :].rearrange("p (g e) t -> p g e t", g=2), in0=dec[:, :, :].rearrange("p (g e) t -> p g e t", g=2),
                                           in1=cbm[:, :, :].unsqueeze(2).to_broadcast([128, 2, 8, 128]), op=ALU.mult)
              S.task("dve", twt, reads=[R("dec%d" % par), R("cbm"), R("tailc")], writes=[R("dec%d" % par)])

              def twt2(dve, cs=cs):
                  return dve.tensor_tensor(out=CdT[:, :, :].rearrange("p (g e) t -> p g e t", g=2), in0=eac[:, :, :].rearrange("p (g e) t -> p g e t", g=2),
                                            in1=xbcT[:, 10:12, cs].unsqueeze(2).to_broadcast([128, 2, 8, 128]), op=ALU.mult)
              S.task("dve", twt2, reads=[R("eac"), R("xbcT")], writes=[R("CdT%d" % par)])
          def ssd_back(lt):
              par = (lt % 2) if lt < 4 else 0
              dec, CdT, xdt, xtail, Btok, eal = decB[par], CdTB[par], xdtB[par], xtailB[par], BtokB[par], ealB[par]
              samp = (lt == 4)
              ci = 4 * g + lt
              cs = slice(lt * 128, (lt + 1) * 128)
              tri = C("tri_s") if samp else C("tri_p")
              yb = [nb(), nb()]
              if samp:
                  reserved.update(banks.index(yb[0][0]), ) if False else None
                  for _b in yb:
                      reserved.add([id(x) for x in banks].index(id(_b[0])))
              first_chunk = (ci == 0 and not samp)
              if samp:
                  def tBm(dve):
                      return dve.tensor_tensor(out=Bm[:, :, :], in0=Btok[:, :, :].rearrange("p g n -> p (g n)").unsqueeze(1).to_broadcast([128, 16, 256]),
                                               in1=C("oh").unsqueeze(2).to_broadcast([128, 16, 256]), op=ALU.mult)
                  S.task("dve", tBm, reads=[R("Btok%d" % par), Rc], writes=[R("Bm")])
                  bk3, bR3 = nb()

                  def tds(pe, bk3=bk3):
                      for pr in range(8):
                          a0 = CB_OFF["selpair"][0] + pr * 128
                          o = bk3[:, pr * 16:(pr + 1) * 16]
                          pe.matmul(o, cb[0:16, a0:a0 + 128], achi[0:16, 7:128:8], start=True, stop=False)
                          last = pe.matmul(o, cb[0:16, a0:a0 + 128], aclo[0:16, 7:128:8], start=False, stop=True)
                      return last
                  S.task("pe", tds, reads=[R("ac"), Rc], writes=[bR3])

                  def tds2(act, bk3=bk3):
                      return act.activation(out=decs[:, :, :], in_=bk3[:, 0:128].rearrange("p (r b) -> p r b", r=8), func=AF.Exp)
                  S.task("act", tds2, reads=[bR3], writes=[R("decs")])

              def tyi(pe, yb=yb, first_chunk=first_chunk, samp=samp):
                  for h in range(16):
                      pr = h // 2
                      o = yb[pr // 4][0][64 * (h % 2):64 * (h % 2) + 64, (pr % 4) * 128:(pr % 4 + 1) * 128]
                      last = pe.matmul(o, xdt[:, h, :], dec[:, h, :], start=((pr % 4 == 0) if samp else True), stop=first_chunk, tile_position=(0, 64 * (h % 2)), skip_group_check=samp)
                      if not first_chunk and not samp:
                          last = pe.matmul(o, HTb[:, h * 64:(h + 1) * 64], CdT[:, h, :], start=False, stop=True, tile_position=(0, 64 * (h % 2)))
                  return last
              S.task("pe", tyi, reads=[R("xdt%d" % par), R("dec%d" % par), R("CdT%d" % par), R("HTb")], writes=[yb[0][1], yb[1][1]])
              if samp:
                  seqctx = {}
                  def seq_s1(b):
                      h0n_, hn_ = [(h0n, "h0n"), (h0n2, "h0n2"), (h0n3, "h0n3")][b % 3]
                      hout_, ho_ = (hout, "hout") if b % 2 == 0 else (hout2, "hout2")
                      dma_load("pool", h0n_[:, :, :], sssm[b].rearrange("(r q) n -> q r n", q=128), hn_, [R(hn_)])
                      tb = [nb(), nb()]

                      def th0(pe, tb=tb, h0n_=h0n_):
                          for pr in range(8):
                              last = pe.transpose(tb[pr // 4][0][:, (pr % 4) * 128:(pr % 4 + 1) * 128], h0n_[:, pr, :], C("ident"))
                          return last
                      S.task("pe", th0, reads=[R(hn_), Rc], writes=[tb[0][1], tb[1][1]])

                      def th1(act, tb=tb, h0T=h0TB[b % 2]):
                          act.activation(out=h0T[:, 0:512], in_=tb[0][0][:, :], func=AF.Copy)
                          return act.activation(out=h0T[:, 512:1024], in_=tb[1][0][:, :], func=AF.Copy)
                      S.task("act", th1, reads=[tb[0][1], tb[1][1]], writes=[R("h0T%d" % (b % 2))])

                      seqctx[b] = tb
                  def seq_s2(b):
                      h0n_, hn_ = [(h0n, "h0n"), (h0n2, "h0n2"), (h0n3, "h0n3")][b % 3]
                      hout_, ho_ = (hout, "hout") if b % 2 == 0 else (hout2, "hout2")
                      tb = seqctx.pop(b)
                      def th2(pe, yb=yb, b=b, h0T=h0TB[b % 2]):
                          for h in range(16):
                              pr = h // 2
                              o = yb[pr // 4][0][64 * (h % 2):64 * (h % 2) + 64, (pr % 4) * 128 + 8 * b:(pr % 4) * 128 + 8 * b + 8]
                              last = pe.matmul(o, h0T[:, h * 64:(h + 1) * 64], CdT[:, h, 8 * b:8 * b + 8], start=False, stop=(b == 15), skip_group_check=True,
                                               tile_position=(0, 64 * (h % 2)))
                          return last
                      S.task("pe", th2, reads=[R("h0T%d" % (b % 2)), R("CdT%d" % par)], writes=[yb[0][1], yb[1][1]])
                      hb2 = [nb(), nb()]

                      def th3(pe, hb2=hb2, b=b):
                          for pr in range(8):
                              gg = pr // 4
                              last = pe.matmul(hb2[pr // 4][0][:, (pr % 4) * 128:(pr % 4 + 1) * 128], xtail[:, :, :].rearrange("p h d -> p (h d)")[:, pr * 128:(pr + 1) * 128],
                                               Bm[:, b, gg * 128:(gg + 1) * 128], start=True, stop=True)
                          return last
                      S.task("pe", th3, reads=[R("xtail%d" % par), R("Bm")], writes=[hb2[0][1], hb2[1][1]])

                      def th4(dve, hb2=hb2, b=b, h0n_=h0n_, hout_=hout_):
                          for pr in range(8):
                              last = dve.scalar_tensor_tensor(out=hout_[:, pr, :], in0=h0n_[:, pr, :], scalar=decs[:, pr, b:b + 1],
                                                              in1=hb2[pr // 4][0][:, (pr % 4) * 128:(pr % 4 + 1) * 128], op0=ALU.mult, op1=ALU.add)
                          return last
                      S.task("dve", th4, reads=[hb2[0][1], hb2[1][1], R(hn_), R("decs")], writes=[R(ho_)])
                      dma_load("sp", nssm_s[b].rearrange("(r q) n -> q r n", q=128), hout_[:, :, :], ho_, [], reads=[R(ho_)])

                  for b in range(16):
                      seq_s1(b)
                      seq_s2(b)
              else:
                  hb2 = [nb(), nb()]

                  def tst(pe, hb2=hb2):
                      for gg in range(2):
                          last = pe.matmul(hb2[gg][0][:, :], Btok[:, gg, :], xtail[:, :, :].rearrange("p h d -> p (h d)")[:, gg * 512:(gg + 1) * 512], start=True, stop=True)
                      return last
                  S.task("pe", tst, reads=[R("Btok%d" % par), R("xtail%d" % par)], writes=[hb2[0][1], hb2[1][1]])

                  hv = HT[:, :].rearrange("p (h d) -> p h d", h=16)
                  S.chain("dve", [lambda dve, hv=hv: dve.tensor_tensor(out=hv, in0=hv, in1=eal[:, :].unsqueeze(2).to_broadcast([128, 16, 64]), op=ALU.mult),
                                  lambda dve, hb2=hb2: dve.tensor_tensor(out=HT[:, 0:512], in0=HT[:, 0:512], in1=hb2[0][0][:, :], op=ALU.add),
                                  lambda dve, hb2=hb2: dve.tensor_tensor(out=HT[:, 512:1024], in0=HT[:, 512:1024], in1=hb2[1][0][:, :], op=ALU.add)],
                          reads=[hb2[0][1], hb2[1][1], R("eal%d" % par)], writes=[R("HT")])
              reserved.clear()
              def tg1(dve, yb=yb, cs=cs):
                  for pr in range(8):
                      yv = yb[pr // 4][0][:, (pr % 4) * 128:(pr % 4 + 1) * 128]
                      last = dve.scalar_tensor_tensor(out=gated[:, pr, :], in0=xbcT[:, pr, cs], scalar=dskipc[:, pr:pr + 1], in1=yv, op0=ALU.mult, op1=ALU.add)
                  return last
              S.chain("dve", [tg1, lambda dve, cs=cs: dve.tensor_tensor(out=gated[:, :, :], in0=gated[:, :, :], in1=szT[:, :, cs], op=ALU.mult)],
                      reads=[yb[0][1], yb[1][1], R("xbcT"), R("szT"), Rc], writes=[R("gated")])
              yield
              if not samp:
                  def tst3(act):
                      return act.activation(out=HTb[:, :], in_=HT[:, :], func=AF.Copy)
                  S.task("act", tst3, reads=[R("HT")], writes=[R("HTb")])

              def tg2(act):
                  return act.activation(out=gsq[:, :, :], in_=gated[:, :, :], func=AF.Square)
              S.task("act", tg2, reads=[R("gated")], writes=[R("gsq")])
              bk, bR = nb()

              def tg3(pe, bk=bk):
                  for pr in range(8):
                      gg = pr // 4
                      last = pe.matmul(bk[:, gg * 128:(gg + 1) * 128], CB("ones"), gsq[:, pr, :], start=(pr % 4 == 0), stop=(pr % 4 == 3))
                  return last
              S.task("pe", tg3, reads=[R("gsq"), Rc], writes=[bR])

              S.chain("act", [lambda act, bk=bk: act.activation(out=rs[:, :, :], in_=bk[:, 0:256].rearrange("p (g t) -> p g t", g=2), func=AF.Ln, scale=1.0 / 512, bias=EPS),
                              lambda act: act.activation(out=rs[:, :, :], in_=rs[:, :, :], func=AF.Exp, scale=-0.5)],
                      reads=[bR], writes=[R("rs")])
              yield

              def tg5(dve, cs=cs):
                  for pr in range(8):
                      last = dve.scalar_tensor_tensor(out=catT[:, 8 + pr, cs], in0=gated[:, pr, :], scalar=gssm[:, pr:pr + 1],
                                                      in1=rs[:, pr // 4, :], op0=ALU.mult, op1=ALU.mult)
                  return last
              S.task("dve", tg5, reads=[R("rs"), R("gated"), Rc], writes=[R("catT")])
              if ci == 15 and not samp:
                  tb = [nb(), nb()]

                  def tfin(pe, tb=tb):
                      for pr in range(8):
                          last = pe.transpose(tb[pr // 4][0][:, (pr % 4) * 128:(pr % 4 + 1) * 128], HT[:, pr * 128:(pr + 1) * 128], C("ident"))
                      return last
                  S.task("pe", tfin, reads=[R("HT"), Rc], writes=[tb[0][1], tb[1][1]])

                  def tfin2(act, tb=tb):
                      act.activation(out=hout[:, 0:4, :], in_=tb[0][0][:, :].rearrange("p (r n) -> p r n", r=4), func=AF.Copy)
                      return act.activation(out=hout[:, 4:8, :], in_=tb[1][0][:, :].rearrange("p (r n) -> p r n", r=4), func=AF.Copy)
                  S.task("act", tfin2, reads=[tb[0][1], tb[1][1]], writes=[R("hout")])
                  dma_load("sp", nssm_p.rearrange("(r q) n -> q r n", q=128), hout[:, :, :], "hout", [], reads=[R("hout")])
          def run_il(gens):
              gens = list(gens)
              while gens:
                  for g_ in list(gens):
                      try:
                          next(g_)
                      except StopIteration:
                          gens.remove(g_)
          run_il([ssd_front(0)])
          for lt in range(4):
              if lt + 1 < 4:
                  run_il([ssd_front(lt + 1), ssd_back(lt)])
              else:
                  run_il([ssd_back(lt)])
          if has_s:
              run_il([ssd_front(4)])
              run_il([ssd_back(4)])
          if has_s:
              for c in range(12):
                  bk, bR = nb()

                  def tco(pe, bk=bk, c=c):
                      pe.transpose(bk[0:48, 0:128], ncs[:, c, :], C("ident"))
                      return pe.transpose(bk[0:3, 128:256], ccar[:, c, :], C("ident"))
                  S.task("pe", tco, reads=[R("ncs"), R("ccar"), Rc], writes=[bR])

                  def tco2(act, bk=bk, c=c):
                      act.activation(out=sctok[0:48, c * 128:(c + 1) * 128], in_=bk[0:48, 0:128], func=AF.Copy)
                      return act.activation(out=stg[0:3, 0:128], in_=bk[0:3, 128:256], func=AF.Copy)
                  S.task("act", tco2, reads=[bR], writes=[R("sctok"), R("stg")])
                  dma_load("sp", ncv_p[:, c * 128:(c + 1) * 128], stg[0:3, 0:128], "ncv", [], reads=[R("stg")])
                  R("stg").r.append(("dq_ncv", S.dsem["ncv"][1]))
              dma_load("sp", ncv_s[:, :], sctok[0:48, :], "ncvs", [], reads=[R("sctok")])
          S.barrier(force=('C' in KBARS))
          chk('C', g)

          dma_multi("sp", [(cosT[:, 0:512], cos_d[:, g * 512:(g + 1) * 512]), (sinT[:, 0:512], sin_d[:, g * 512:(g + 1) * 512])], "rope", [R("rope")])
          if has_s:
              dma_multi("sp", [(cosT[:, 512:640], cos_d[:, 2048:2176]), (sinT[:, 512:640], sin_d[:, 2048:2176])], "rope", [R("rope")])

          def tz0(dve):
              dve.memset(kE[64:128, :, :], 0.0)
              dve.memset(kO[0:64, :, :], 0.0)
              return dve.memset(vaug[:, :, :, 64:128], 1.0)
          S.task("dve", tz0, writes=[R("kE"), R("kO"), R("vaug")])
          if KSTOP == 'B2a%d' % g:
              S.barrier()
              chk('B2a', g)
          for c in range(12):
              isk = c >= 8
              if c == 8 and KSTOP == 'B2b%d' % g:
                  S.barrier()
                  chk('B2b', g)
              if isk:
                  wt, Rw = load_panel(w_in, 1024 + (c - 8) * 64, dup=True)
              else:
                  wt, Rw = load_panel(w_in, c * 128)
              for (r0, r1) in ranges:
                  n = r1 - r0
                  bk, bR = nb()

                  def tm(pe, bk=bk, wt=wt, r0=r0, r1=r1):
                      for kc in range(8):
                          last = pe.matmul(bk[:, 0:r1 - r0], wt[:, kc, :], hT[:, kc, r0:r1], start=(kc == 0), stop=(kc == 7))
                      return last
                  S.task("pe", tm, reads=[Rw] + Rh, writes=[bR])

                  def tq1(act, bk=bk, n=n):
                      return act.activation(out=qb[:, 0:n], in_=bk[:, 0:n], func=AF.Copy)
                  S.task("act", tq1, reads=[bR], writes=[R("qb"), bR])

                  def tq2(dve, bk=bk, r0=r0, r1=r1, n=n):
                      return dve.tensor_tensor(out=t1[:, 0:n], in0=bk[:, 0:n], in1=cosT[:, r0:r1], op=ALU.mult)
                  S.task("dve", tq2, reads=[bR, R("rope")], writes=[R("t1"), bR])
                  bk2, bR2 = nb()

                  def tq3(pe, bk2=bk2, n=n):
                      return pe.matmul(bk2[:, 0:n], CB("prot"), qb[:, 0:n], start=True, stop=True)
                  S.task("pe", tq3, reads=[R("qb"), Rc], writes=[bR2])
                  if not isk:
                      S.chain("dve", [lambda dve, bk2=bk2, r0=r0, r1=r1, n=n: dve.tensor_tensor(out=rden[:, 0:n], in0=bk2[:, 0:n], in1=sinT[:, r0:r1], op=ALU.mult),
                                      lambda dve, c=c, r0=r0, r1=r1, n=n: dve.tensor_tensor(out=qT[:, c, r0:r1], in0=rden[:, 0:n], in1=t1[:, 0:n], op=ALU.add)],
                              reads=[bR2, R("t1"), R("rope")], writes=[R("qT"), R("rden")])
                  else:
                      kv = c - 8

                      def tk4c(dve, kv=kv, r0=r0, r1=r1, n=n, g=g):
                          dve.tensor_copy(out=kE[0:64, kv, r0:r1], in_=rden[0:64, 0:n])
                          last = dve.tensor_copy(out=kO[64:128, kv, r0:r1], in_=rden[64:128, 0:n])
                          if r0 == 512:
                              last = dve.tensor_copy(out=kfp[:, kv, :], in_=rden[:, 0:128])
                          elif g == 3:
                              last = dve.tensor_copy(out=kfp[:, kv, :], in_=rden[:, 384:512])
                          return last
                      S.chain("dve", [lambda dve, bk2=bk2, r0=r0, r1=r1, n=n: dve.tensor_tensor(out=rden[:, 0:n], in0=bk2[:, 0:n], in1=sinT[:, r0:r1], op=ALU.mult),
                                      lambda dve, n=n: dve.tensor_tensor(out=rden[:, 0:n], in0=rden[:, 0:n], in1=t1[:, 0:n], op=ALU.add),
                                      tk4c],
                              reads=[bR2, R("t1"), R("rope")], writes=[R("kE"), R("kO"), R("rden"), R("kfp")])
                      if g == 3:
                          bk3, bR3 = nb()

                          def tko(pe, bk3=bk3, kv=kv):
                              return pe.transpose(bk3[:, 0:128], kfp[:, kv, :], C("ident"))
                          S.task("pe", tko, reads=[R("kfp"), Rc], writes=[bR3])

                          ot = otok if r0 == 0 else otok2
                          otn = "otok" if r0 == 0 else "otok2"

                          def tko2(act, bk3=bk3, kv=kv, ot=ot):
                              return act.activation(out=ot[:, kv * 64:(kv + 1) * 64], in_=bk3[:, 0:64], func=AF.Copy)
                          S.task("act", tko2, reads=[bR3], writes=[R(otn)])
                          if kv == 3:
                              if r0 == 0:
                                  dma_load("sp", nk_p[:, :], ot[:, :], otn, [], reads=[R(otn)])
                              else:
                                  dma_multi("sp", [(nk_s[b, 120:128, :], ot[8 * b:8 * b + 8, :]) for b in range(16)], otn, [], reads=[R(otn)])
          if KSTOP == 'B2c%d' % g:
              S.barrier()
              chk('B2c', g)
          for lt in range(ntile):
              bk, bR = nb()

              def tv(pe, bk=bk, lt=lt):
                  for kc in range(8):
                      last = pe.matmul(bk[:, 0:256], hT[:, kc, lt * 128:(lt + 1) * 128], wv[:, kc, 0:256], start=(kc == 0), stop=(kc == 7))
                  return last
              S.task("pe", tv, reads=[R("wv")] + Rh, writes=[bR])

              def tv2(act, bk=bk, lt=lt):
                  vv = bk[:, 0:256].rearrange("p (k d) -> p k d", k=4)
                  act.activation(out=vaug[:, lt, :, 0:64], in_=vv, func=AF.Copy)
                  return act.activation(out=vaug[:, lt, :, 128:192], in_=vv, func=AF.Copy)
              S.task("act", tv2, reads=[bR], writes=[R("vaug"), bR])
              if g == 3 and lt >= 3:
                  def tv3(dve, bk=bk):
                      return dve.tensor_copy(out=vfp[:, :], in_=bk[:, 0:256])
                  S.task("dve", tv3, reads=[bR], writes=[R("vfp"), bR])
                  if lt == 3:
                      dma_load("sp", nv_p[:, :], vfp[:, :], "vfp", [], reads=[R("vfp")])
                  else:
                      dma_multi("sp", [(nv_s[b, 120:128, :], vfp[8 * b:8 * b + 8, :]) for b in range(16)], "vfp", [], reads=[R("vfp")])
                  R("vfp").r.append(("dq_vfp", S.dsem["vfp"][1]))
          if has_s:
              dma_multi("sp", [(nk_s[:, 0:120, :], ck[:, 8:128, :, :].rearrange("b s k d -> b s (k d)")),
                               (nv_s[:, 0:120, :], cv[:, 8:128, :, :].rearrange("b s k d -> b s (k d)"))], "cshift", [])

          S.barrier(force=('B2' in KBARS))
          chk('B2', g)
          srco = w_out.rearrange("(kc p) n -> p kc n", p=128)
          dma_multi("pool", [(wo[:, 4 * i:4 * i + 4, :], srco[:, 4 * i:4 * i + 4, :]) for i in range(4)], "wo", [R("wo")])
          attn_ctx = {}
          def att_s1(u, lt, kv):
              samp = (lt == 4)
              ci = 4 * g + lt
              cs = slice(lt * 128, (lt + 1) * 128)
              has_prev = (not samp) and ci > 0
              pp = u % 2
              pT = pTB[pp]
              if samp:
                  kcn_ = kcnB[kv % 2]
                  kcnn = "kcn" if kv % 2 == 0 else "kcn2"
                  dma_multi("pool", [(kcn_[:, :, 0:64], ck[:, :, kv, :].rearrange("b s d -> s b d")),
                                     (kcn_[:, :, 64:128], ck[:, :, kv, :].rearrange("b s d -> s b d"))], kcnn, [R(kcnn)])
                  dma_multi("pool", [(vca[:, :, 0:64], cv[:, :, kv, :].rearrange("b s d -> s b d")),
                                     (vca[:, :, 128:192], cv[:, :, kv, :].rearrange("b s d -> s b d"))], "vcaL", [R("vca")])

                  if kv == 0:
                      def tkc0(dve):
                          dve.memset(kcE[64:128, :, :], 0.0)
                          return dve.memset(kcO[0:64, :, :], 0.0)
                      S.task("dve", tkc0, reads=[], writes=[R("kcE"), R("kcO")])
                  S.task("dve", lambda dve: dve.memset(vca[:, :, 64:128], 1.0), reads=[], writes=[R("vca"), R("vca1")])
                  for b4 in range(4):
                      bk, bR = nb()

                      def tkc(pe, bk=bk, b4=b4, kcn_=kcn_):
                          for j in range(4):
                              last = pe.transpose(bk[:, j * 64:(j + 1) * 64].bitcast(BF16), kcn_[:, b4 * 4 + j, :], CB("ident"))
                          return last
                      S.task("pe", tkc, reads=[R(kcnn), Rc], writes=[bR])

                      def tkc2(act, bk=bk, b4=b4):
                          vv = bk[:, 0:256].bitcast(BF16).rearrange("p (j s) -> p j s", j=4)
                          act.activation(out=kcE[0:64, b4 * 4:b4 * 4 + 4, :], in_=vv[0:64], func=AF.Copy)
                          return act.activation(out=kcO[64:128, b4 * 4:b4 * 4 + 4, :], in_=vv[64:128], func=AF.Copy)
                      S.task("act", tkc2, reads=[bR], writes=[R("kcE"), R("kcO")])
                  bkc, bRc = nb()

                  def tsc(pe, bkc=bkc, kv=kv):
                      pe.matmul(bkc[:, :], CB("ident"), CB("mc"), start=True, stop=False)
                      for b in range(16):
                          for gq in range(4):
                              c = 2 * kv + gq // 2
                              kk = kcE if gq % 2 == 0 else kcO
                              last = pe.matmul(bkc[:, b * 32 + gq * 8:b * 32 + gq * 8 + 8], kk[:, b, :], qT[:, c, 512 + 8 * b:512 + 8 * b + 8],
                                               start=False, stop=(b == 15 and gq == 3))
                      return last
                  S.task("pe", tsc, reads=[R("kcE"), R("kcO"), R("qT"), Rc], writes=[bRc])

                  def tsc2(act, bkc=bkc):
                      return act.activation(out=pTc[:, :], in_=bkc[:, :], func=AF.Exp, scale=0.125)
                  S.task("act", tsc2, reads=[bRc], writes=[R("pTc")])
              sbk = []
              for j in ([0, 1] if has_prev else [1]):
                  bk, bR = nb()
                  sbk.append((j, bk, bR))

              def tsc_(pe, sbk=sbk, kv=kv, lt=lt, cs=cs, samp=samp):
                  for j, bk, bR in sbk:
                      mk = CB("mcur_s") if samp else (CB("mcur") if j == 1 else CB("mprev"))
                      pe.matmul(bk[:, :], CB("ident"), mk, start=True, stop=False)
                      for gq in range(4):
                          c = 2 * kv + gq // 2
                          if j == 1:
                              kk = (kE if gq % 2 == 0 else kO)[:, kv, cs]
                          elif lt == 0:
                              kk = (kE0 if gq % 2 == 0 else kO0)[:, kv, :]
                          else:
                              kk = (kE if gq % 2 == 0 else kO)[:, kv, (lt - 1) * 128:lt * 128]
                          last = pe.matmul(bk[:, gq * 128:(gq + 1) * 128], kk, qT[:, c, cs], start=False, stop=(gq == 3))
                  return last
              S.task("pe", tsc_, reads=[R("kE"), R("kO"), R("qT"), R("k0"), Rc], writes=[x[2] for x in sbk])

              def tex(act, sbk=sbk):
                  for j, bk, bR in sbk:
                      last = act.activation(out=pT[:, j, :], in_=bk[:, :], func=AF.Exp, scale=0.125)
                  return last
              S.task("act", tex, reads=[x[2] for x in sbk], writes=[R("pT%d" % pp)])
              attn_ctx[u] = sbk
          def att_s2(u, lt, kv):
              samp = (lt == 4)
              ci = 4 * g + lt
              cs = slice(lt * 128, (lt + 1) * 128)
              has_prev = (not samp) and ci > 0
              pp = u % 2
              pT = pTB[pp]
              sbk = attn_ctx.pop(u)
              pv, pvR = nb()

              def tpv(pe, pv=pv, sbk=sbk, kv=kv, lt=lt, samp=samp):
                  o = pv[:, 0:512]
                  first = True
                  for j, bk, bR in sbk:
                      if j == 1:
                          va = vaug[:, lt, kv, 0:128]
                      elif lt == 0:
                          va = va0[:, kv, 0:128]
                      else:
                          va = vaug[:, lt - 1, kv, 0:128]
                      pe.matmul(o, va, pT[:, j, :], start=first, stop=False)
                      first = False
                  if samp:
                      for b in range(16):
                          for gq in range(4):
                              pe.matmul(o[:, gq * 128 + 8 * b:gq * 128 + 8 * b + 8], vca[:, b, 0:128],
                                        pTc[:, b * 32 + gq * 8:b * 32 + gq * 8 + 8], start=False, stop=False)
                  return pe.matmul(o, CB("sinkE"), esr[:, 4 * kv:4 * kv + 4, :], start=False, stop=True)
              S.task("pe", tpv, reads=[R("pT%d" % pp), R("pTc"), R("vaug"), R("vca"), R("vca1"), R("k0"), R("esr"), Rc], writes=[pvR])

              def tno0(act, pv=pv):
                  return act.activation(out=rden[64:128, 0:512], in_=pv[64:128, 0:512], func=AF.Ln)

              def tno1(act):
                  return act.activation(out=rden[64:128, 0:512], in_=rden[64:128, 0:512], func=AF.Exp, scale=-1.0)

              def tno(dve, pv=pv, kv=kv, cs=cs):
                  pv3 = pv[0:64, 0:512].rearrange("p (g q) -> p g q", g=4)
                  rd3 = rden[64:128, 0:512].rearrange("p (g q) -> p g q", g=4)
                  dve.tensor_tensor(out=catT[0:64, 2 * kv:2 * kv + 2, cs], in0=pv3[:, 0::2, :], in1=rd3[:, 0::2, :], op=ALU.mult)
                  return dve.tensor_tensor(out=catT[64:128, 2 * kv:2 * kv + 2, cs], in0=pv3[:, 1::2, :], in1=rd3[:, 1::2, :], op=ALU.mult)
              S.chain("act", [tno0, tno1], reads=[pvR], writes=[R("rden"), pvR])
              S.task("dve", tno, reads=[pvR, R("rden")], writes=[R("catT"), pvR])
          units = [(lt, kv) for lt in range(4) for kv in range(4)]
          att_s1(0, *units[0])
          for u in range(len(units)):
              if u + 1 < len(units):
                  att_s1(u + 1, *units[u + 1])
              att_s2(u, *units[u])
          if has_s:
              for kv in range(4):
                  att_s1(16 + kv, 4, kv)
                  att_s2(16 + kv, 4, kv)
          def tcar(act):
              act.activation(out=kE0[:, :, :], in_=kE[:, :, 384:512], func=AF.Copy)
              act.activation(out=kO0[:, :, :], in_=kO[:, :, 384:512], func=AF.Copy)
              return act.activation(out=va0[:, :, :], in_=vaug[:, 3, :, :], func=AF.Copy)
          S.task("act", tcar, reads=[R("kE"), R("kO"), R("vaug")], writes=[R("k0")])
          S.barrier(force=('D' in KBARS))
          if KDBG and g == 3:
              dma_load("sp", dbg[:, :, :], catT[:, :, 512:640], "dbg", [], reads=[R("catT")])
              S.barrier()
          chk('D', g)

          srcd = w_down.rearrange("(kc p) n -> p kc n", p=128)
          dma_multi("pool", [(wd[:, 0:11, :], srcd[:, 0:11, :]), (wd[:, 11:22, :], srcd[:, 11:22, :])], "wd", [R("wd")])
          def e_s1(lt):
              hb = hbB[lt % 2]
              src = xs[:, :] if lt == 4 else xp[(4 * g + lt) * 128:(4 * g + lt + 1) * 128, :]
              dma_load("sp", xres[:, lt, :], src, "xres%d" % lt, [R("xres%d" % lt)])
              mb = [nb(), nb()]

              def tmo(pe, mb=mb, lt=lt):
                  for hf in range(2):
                      for kc in range(16):
                          last = pe.matmul(mb[hf][0][:, :], catT[:, kc, lt * 128:(lt + 1) * 128], wo[:, kc, hf * 512:(hf + 1) * 512],
                                           start=(kc == 0), stop=(kc == 15))
                  return last
              S.task("pe", tmo, reads=[R("catT"), R("wo")], writes=[mb[0][1], mb[1][1]])

              S.task("dve", lambda dve: dve.memset(stat[:, 4:12], 0.0), writes=[R("stat")])

              def tmo2(act, mb=mb):
                  act.activation(out=mixs[:, 0:512], in_=mb[0][0][:, :], func=AF.Copy)
                  return act.activation(out=mixs[:, 512:1024], in_=mb[1][0][:, :], func=AF.Copy)
              S.chain("act", [tmo2, lambda act: act.activation(out=hb[:, :], in_=mixs[:, :], func=AF.Square, accum_out=stat[:, 4:5]),
                              lambda act: act.activation(out=stat[:, 5:6], in_=stat[:, 4:5], func=AF.Ln, scale=1.0 / D, bias=EPS),
                              lambda act: act.activation(out=stat[:, 6:7], in_=stat[:, 5:6], func=AF.Exp, scale=-0.5)],
                      reads=[mb[0][1], mb[1][1]], writes=[R("mixs"), R("hb%d" % (lt % 2)), R("stat")])
              S.chain("dve", [lambda dve: dve.scalar_tensor_tensor(out=mixs[:, :], in0=mixs[:, :], scalar=stat[:, 6:7], in1=gpost, op0=ALU.mult, op1=ALU.mult),
                              lambda dve, lt=lt: dve.tensor_tensor(out=xres[:, lt, :], in0=xres[:, lt, :], in1=mixs[:, :], op=ALU.add)],
                      reads=[R("mixs"), R("stat"), Rc], writes=[R("mixs"), R("stat"), R("xres%d" % lt)])
              S.chain("act", [lambda act, lt=lt: act.activation(out=hb[:, :], in_=xres[:, lt, :], func=AF.Square, accum_out=stat[:, 8:9]),
                              lambda act: act.activation(out=stat[:, 9:10], in_=stat[:, 8:9], func=AF.Ln, scale=1.0 / D, bias=EPS),
                              lambda act: act.activation(out=stat[:, 10:11], in_=stat[:, 9:10], func=AF.Exp, scale=-0.5)],
                      reads=[R("xres%d" % lt)], writes=[R("hb%d" % (lt % 2)), R("stat")])
              S.chain("dve", [lambda dve, lt=lt: dve.scalar_tensor_tensor(out=hb[:, :], in0=xres[:, lt, :], scalar=stat[:, 10:11], in1=gffn, op0=ALU.mult, op1=ALU.mult)],
                      reads=[R("xres%d" % lt), R("stat"), Rc], writes=[R("stat"), R("hb%d" % (lt % 2))])
          def e_s2(lt):
              hb = hbB[lt % 2]
              bk, bR = nb()

              def tmo6(pe, bk=bk):
                  for kc in range(8):
                      last = pe.transpose(bk[:, kc * 64:(kc + 1) * 64].bitcast(BF16), hb[:, kc * 128:(kc + 1) * 128], CB("ident"))
                  return last
              S.task("pe", tmo6, reads=[R("hb%d" % (lt % 2)), Rc], writes=[bR])

              def tmo7(act, bk=bk, lt=lt):
                  return act.activation(out=hT[:, :, lt * 128:(lt + 1) * 128],
                                        in_=bk[:, :].bitcast(BF16).rearrange("p (c t) -> p c t", c=8), func=AF.Copy)
              S.task("act", tmo7, reads=[bR], writes=[Rh[lt]])
          e_s1(0)
          for lt in range(ntile):
              if lt + 1 < ntile:
                  e_s1(lt + 1)
              e_s2(lt)
          S.barrier(force=('E' in KBARS))
          chk('E', g)

          for m in range(22):
              if m % 2 == 0:
                  wgb_, Rg = load_big(w_gate, m * 128)
                  wub_, Ru = load_big(w_up, m * 128)
              wg_ = wgb_[:, :, (m % 2) * 128:(m % 2) * 128 + 128]
              wu_ = wub_[:, :, (m % 2) * 128:(m % 2) * 128 + 128]
              for (r0, r1) in ranges:
                  n = r1 - r0
                  bg, bgR = nb()
                  bu, buR = nb()

                  def tf(pe, bg=bg, bu=bu, wg_=wg_, wu_=wu_, r0=r0, r1=r1, n=n):
                      for kc in range(8):
                          pe.matmul(bg[:, 0:n], wg_[:, kc, :], hT[:, kc, r0:r1], start=(kc == 0), stop=(kc == 7))
                      for kc in range(8):
                          last = pe.matmul(bu[:, 0:n], wu_[:, kc, :], hT[:, kc, r0:r1], start=(kc == 0), stop=(kc == 7))
                      return last
                  S.task("pe", tf, reads=[Rg, Ru] + Rh, writes=[bgR, buR])

                  def tf2(act, bg=bg, n=n):
                      return act.activation(out=sg[:, 0:n], in_=bg[:, 0:n], func=AF.Silu)
                  S.task("act", tf2, reads=[bgR], writes=[R("sg")])

                  def tf3(dve, bu=bu, m=m, r0=r0, r1=r1, n=n):
                      return dve.tensor_tensor(out=hff[:, m, r0:r1], in0=sg[:, 0:n], in1=bu[:, 0:n], op=ALU.mult)
                  S.task("dve", tf3, reads=[buR, R("sg")], writes=[R("hff")])
          ga = {}
          ntn = 0 if g == 3 else (5 if g + 1 == 3 else 4)
          for lt in range(ntile):
              if lt < ntn:
                  ga[lt] = a_tile(g + 1, lt)
                  next(ga[lt])
              mb = [nb(), nb()]

              def td(pe, mb=mb, lt=lt):
                  for hf in range(2):
                      for kc in range(22):
                          last = pe.matmul(mb[hf][0][:, :], hff[:, kc, lt * 128:(lt + 1) * 128], wd[:, kc, hf * 512:(hf + 1) * 512],
                                           start=(kc == 0), stop=(kc == 21))
                  return last
              S.task("pe", td, reads=[R("hff"), R("wd")], writes=[mb[0][1], mb[1][1]])

              S.task("dve", lambda dve: dve.memset(stat[:, 12:13], 0.0), writes=[R("stat")])

              def td2(act, mb=mb):
                  act.activation(out=dns[:, 0:512], in_=mb[0][0][:, :], func=AF.Copy)
                  return act.activation(out=dns[:, 512:1024], in_=mb[1][0][:, :], func=AF.Copy)
              S.chain("act", [td2, lambda act, lt=lt: act.activation(out=hbB[1][:, :], in_=dns[:, :], func=AF.Square, accum_out=stat[:, 12:13]),
                              lambda act: act.activation(out=stat[:, 13:14], in_=stat[:, 12:13], func=AF.Ln, scale=1.0 / D, bias=EPS),
                              lambda act: act.activation(out=stat[:, 14:15], in_=stat[:, 13:14], func=AF.Exp, scale=-0.5)],
                      reads=[mb[0][1], mb[1][1]], writes=[R("mixs"), R("hb1"), R("stat")])
              S.chain("dve", [lambda dve: dve.scalar_tensor_tensor(out=dns[:, :], in0=dns[:, :], scalar=stat[:, 14:15], in1=gpff, op0=ALU.mult, op1=ALU.mult),
                              lambda dve, lt=lt: dve.tensor_tensor(out=xres[:, lt, :], in0=xres[:, lt, :], in1=dns[:, :], op=ALU.add)],
                      reads=[R("mixs"), R("stat"), Rc], writes=[R("mixs"), R("stat"), R("xres%d" % lt)])
              dst = y_s[:, :] if lt == 4 else y_p[(4 * g + lt) * 128:(4 * g + lt + 1) * 128, :]
              dma_load("sp", dst, xres[:, lt, :], "yout%d" % lt, [], reads=[R("xres%d" % lt)])
              if lt in ga:
                  for _ in ga[lt]:
                      pass
          for lt_ in range(ntile, ntn):
              for _ in a_tile(g + 1, lt_):
                  pass
          S.barrier(force=('F' in KBARS))

    except _Stop:
        pass
    S.barrier(force=True)
    S.emit()
    return nc


def _vecs(inp):
    v = np.zeros((128, 140), np.float32)
    v[:, 0:16] = inp["dt_bias"][0][None, :]
    v[:, 16:32] = inp["a_log"][0][None, :]
    v[:, 32:48] = inp["attn_sinks"][0][None, :]
    cw = inp["conv_w"][0]
    v[:, 48:96] = cw.reshape(4, 12, 128).transpose(2, 1, 0).reshape(128, 48)
    v[:, 96:108] = inp["conv_b"][0].reshape(12, 128).T
    v[:, 108:116] = inp["g_ssm_out"][0].reshape(8, 128).T
    v[:, 116:124] = np.repeat(inp["d_skip"][0].reshape(8, 2, 1), 64, axis=2).reshape(8, 128).T
    return v


_NC_CACHE = {}


def kernel(**inp):
    inp = {k: np.asarray(v) for k, v in inp.items()}
    if "nc" not in _NC_CACHE:
        _NC_CACHE["nc"] = build_nc()
    nc = _NC_CACHE["nc"]
    cf, cb, cos_t, sin_t = _host_consts()
    gt = np.concatenate([np.broadcast_to(inp[k][0][None, :], (128, D)) for k in ("g_pre_mix", "g_post_mix", "g_pre_ffn", "g_post_ffn")], axis=1)
    gt = np.ascontiguousarray(gt, dtype=np.float32)
    vecs = _vecs(inp)
    shared = {"w_in": np.ascontiguousarray(inp["w_in"][0]), "w_out": np.ascontiguousarray(inp["w_out"][0]),
              "w_gate": np.ascontiguousarray(inp["w_gate"][0]), "w_up": np.ascontiguousarray(inp["w_up"][0]),
              "w_down": np.ascontiguousarray(inp["w_down"][0]), "gtiles": gt, "vecs": vecs, "cf": cf, "cb": cb,
              "cos_t": cos_t, "sin_t": sin_t}
    in_maps = []
    for c in range(8):
        sl = slice(16 * c, 16 * c + 16)
        m = dict(shared)
        m["xp"] = np.ascontiguousarray(inp["x_prompt"][c])
        m["xs"] = np.ascontiguousarray(inp["x_sample"][sl].reshape(128, D))
        m["ck"] = np.ascontiguousarray(inp["cache_k_win"][0, sl])
        m["cv"] = np.ascontiguousarray(inp["cache_v_win"][0, sl])
        m["sconv"] = np.ascontiguousarray(inp["state_conv"][0, sl].reshape(48, 1536))
        m["sssm"] = np.ascontiguousarray(inp["state_ssm"][0, sl].reshape(16, 1024, 128))
        in_maps.append(m)
    res = run_bass_kernel_spmd(nc, in_maps, core_ids=list(range(8)))
    r = res.results

    def cat(name, shape):
        return np.stack([np.asarray(r[c][name]) for c in range(8)], 0).reshape(shape).astype(np.float32)
    y_p = cat("y_p", (8, 2048, D))
    y_s = cat("y_s", (128, 8, D))
    nk_p = cat("nk_p", (1, 8, 128, 4, 64))
    nv_p = cat("nv_p", (1, 8, 128, 4, 64))
    ncv_p = cat("ncv_p", (1, 8, 3, 1536))
    nssm_p = cat("nssm_p", (1, 8, 16, 64, 128))
    nk_s = cat("nk_s", (1, 128, 128, 4, 64))
    nv_s = cat("nv_s", (1, 128, 128, 4, 64))
    ncv_s = cat("ncv_s", (1, 128, 3, 1536))
    nssm_s = cat("nssm_s", (1, 128, 16, 64, 128))
    return (y_p, y_s, nk_p, nv_p, ncv_p, nssm_p, nk_s, nv_s, ncv_s, nssm_s)
```

```python
import numpy as np
import contextlib
import os
import concourse.bass as bass
import concourse.mybir as mybir
from concourse.bass_utils import run_bass_kernel_spmd

F32 = mybir.dt.float32
BF16 = mybir.dt.bfloat16
AF = mybir.ActivationFunctionType
ALU = mybir.AluOpType
AX = mybir.AxisListType

NEG = -30000.0
EPS = 1e-6
D = 1024
DFF = 2816
NIN = 4112
SB_BASE = 16512
SB_LIMIT = 229344


class Reg:
    __slots__ = ("w", "r", "name", "rng")

    def __init__(self, name, rng=None):
        self.w = None
        self.r = []
        self.name = name
        self.rng = rng


class Sched:
    ENG = ("pe", "act", "dve", "pool", "sp")

    def __init__(self, nc, stack):
        self.nc = nc
        self.h = {"pe": nc.tensor, "act": nc.scalar, "dve": nc.vector, "pool": nc.gpsimd, "sp": nc.sync}
        self.esem = {e: stack.enter_context(nc.semaphore("sem_" + e)) for e in self.ENG}
        self.ecnt = {e: 0 for e in self.ENG}
        self.known = {e: {} for e in self.ENG}
        self.prog = {e: [] for e in self.ENG}
        self.dsem = {}
        self.stack = stack
        self.semobj = {}
        self.ranged = []
        self.use_barriers = bool(os.environ.get("KBAR"))
        for e in self.ENG:
            self.semobj["sem_" + e] = self.esem[e]

    def dma_sem(self, key):
        if key not in self.dsem:
            s = self.stack.enter_context(self.nc.semaphore("dq_" + key))
            self.dsem[key] = [s, 0]
            self.semobj["dq_" + key] = s
        return self.dsem[key]

    def _waits(self, e, deps):
        waits = []
        kn = self.known[e]
        for k, v in deps.items():
            if kn.get(k, 0) < v:
                kn[k] = v
                waits.append((self.semobj[k], v))
        return waits

    def task(self, e, fn, reads=(), writes=(), dma=None, ndma=1):
        deps = {}

        def add(d):
            if d is not None and deps.get(d[0], 0) < d[1]:
                deps[d[0]] = d[1]
        for R in reads:
            add(R.w)
        for R in writes:
            add(R.w)
            for d in R.r:
                add(d)
            if R.rng:
                for Q in self.ranged:
                    if Q is R or (Q.w is None and not Q.r):
                        continue
                    hit = False
                    for (a0, a1) in R.rng:
                        for (b0, b1) in Q.rng:
                            if a0 < b1 and b0 < a1:
                                hit = True
                    if hit:
                        add(Q.w)
                        for d in Q.r:
                            add(d)
        waits = self._waits(e, deps)
        if dma is not None:
            ds = self.dma_sem(dma)
            ds[1] += 16 * ndma
            my = ("dq_" + dma, ds[1])
            sem = ds[0]
        else:
            self.ecnt[e] += 1
            my = ("sem_" + e, self.ecnt[e])
            sem = self.esem[e]
        for R in reads:
            R.r.append(my)
        for R in writes:
            R.w = my
            R.r = []
        self.prog[e].append((waits, fn, sem, dma is not None))

    def chain(self, e, fns, reads=(), writes=()):
        ch = Reg("chain")
        for f in fns:
            self.task(e, f, reads=list(reads), writes=list(writes) + [ch])

    def barrier(self, force=False):
        if not (force or self.use_barriers):
            return
        deps = {}
        for e in self.ENG:
            if self.ecnt[e]:
                deps["sem_" + e] = self.ecnt[e]
        for k, (s, c) in self.dsem.items():
            if c:
                deps["dq_" + k] = c
        for e in self.ENG:
            w = self._waits(e, dict(deps))
            if w:
                self.prog[e].append((w, None, None, False))

    def emit(self):
        nc = self.nc
        with nc.Block() as block:
            def run(e):
                def body(eng):
                    for waits, fn, sem, isdma in self.prog[e]:
                        for s, v in waits:
                            eng.wait_ge(s, v)
                        if fn is None:
                            continue
                        if isdma:
                            fn(eng, sem)
                        else:
                            last = fn(eng)
                            last.then_inc(sem, 1)
                return body
            block.tensor(run("pe"))
            block.scalar(run("act"))
            block.vector(run("dve"))
            block.gpsimd(run("pool"))
            block.sync(run("sp"))


def _host_consts():
    i = np.arange(128)
    seq = i // 8
    tri_p = (i[:, None] <= i[None, :]).astype(np.float32)
    same = (seq[:, None] == seq[None, :])
    tri_s = tri_p * same
    U = (i[:, None] > i[None, :]).astype(np.float32)
    ident = np.eye(128, dtype=np.float32)
    lastmask = (i[None, :] == (8 * seq[:, None] + 7)).astype(np.float32)
    oh = (seq[:, None] == np.arange(16)[None, :]).astype(np.float32)
    ones = np.ones((128, 128), np.float32)
    row0 = np.zeros((128, 1), np.float32)
    row0[0, 0] = 1.0
    cf = np.concatenate([ident, tri_p, tri_s, U, lastmask, oh, ones, row0], axis=1)
    mprev = np.where(i[:, None] >= i[None, :], 0.0, NEG)
    mcur = np.where(i[:, None] <= i[None, :], 0.0, NEG)
    mcur_s = np.where(same & (i[:, None] <= i[None, :]), 0.0, NEG)
    t8 = np.arange(8)
    mc = np.where(i[:, None] >= t8[None, :], 0.0, NEG)
    prot = np.zeros((128, 128), np.float32)
    for m in range(128):
        d = m % 64
        base = m - d
        if d < 8:
            prot[base + d + 8, m] = 1.0
        elif d < 16:
            prot[base + d - 8, m] = 1.0
    sel16 = np.zeros((128, 16, 128), np.float32)
    for h in range(16):
        sel16[h, h, :] = 1.0
    selpair = np.zeros((128, 8, 128), np.float32)
    for pr in range(8):
        selpair[2 * pr, pr, 0:64] = 1.0
        selpair[2 * pr + 1, pr, 64:128] = 1.0
    sinkE = np.zeros((128, 128), np.float32)
    sinkE[0, 64:128] = 1.0
    sinkO = np.zeros((128, 128), np.float32)
    sinkO[0, 0:64] = 1.0
    cb = np.concatenate([ident, prot, np.tile(mprev, (1, 4)), np.tile(mcur, (1, 4)), np.tile(mcur_s, (1, 4)),
                         np.tile(mc, (1, 64)), sel16.reshape(128, -1), selpair.reshape(128, -1), sinkE, sinkO,
                         ones, U], axis=1).astype(np.float32)
    pos = np.concatenate([np.arange(2048), 8192 + (np.arange(128) % 8)]).astype(np.float32)
    half = 8
    inv = (500000.0 ** (-np.arange(half, dtype=np.float32) * 2.0 / 16)).astype(np.float32)
    ang = pos[None, :] * inv[:, None]
    cosv = np.cos(ang).astype(np.float32)
    sinv = np.sin(ang).astype(np.float32)
    cos_t = np.ones((128, pos.size), np.float32)
    sin_t = np.zeros((128, pos.size), np.float32)
    for p in range(128):
        d = p % 64
        if d < 8:
            cos_t[p] = cosv[d]
            sin_t[p] = -sinv[d]
        elif d < 16:
            cos_t[p] = cosv[d - 8]
            sin_t[p] = sinv[d - 8]
    return cf, cb, cos_t, sin_t


CF_OFF = {}
_o = 0
for _n, _w in [("ident", 128), ("tri_p", 128), ("tri_s", 128), ("U", 128), ("lastmask", 128), ("oh", 16),
               ("ones", 128), ("row0", 1)]:
    CF_OFF[_n] = (_o, _o + _w)
    _o += _w
CF_W = _o
CB_OFF = {}
_o = 0
for _n, _w in [("ident", 128), ("prot", 128), ("mprev", 512), ("mcur", 512), ("mcur_s", 512), ("mc", 512),
               ("sel16", 2048), ("selpair", 1024), ("sinkE", 128), ("sinkO", 128), ("ones", 128), ("U", 128)]:
    CB_OFF[_n] = (_o, _o + _w)
    _o += _w
CB_W = _o


def build_nc():
    nc = bass.Bass("TRN2", target_bir_lowering=False)
    stack = contextlib.ExitStack()
    S = Sched(nc, stack)

    def din(name, shape):
        return nc.dram_tensor(name, list(shape), F32, kind="ExternalInput").ap()

    def dout(name, shape):
        return nc.dram_tensor(name, list(shape), F32, kind="ExternalOutput").ap()

    xp = din("xp", [2048, D]); xs = din("xs", [128, D])
    ck = din("ck", [16, 128, 4, 64]); cv = din("cv", [16, 128, 4, 64])
    sconv = din("sconv", [48, 1536]); sssm = din("sssm", [16, 1024, 128])
    w_in = din("w_in", [D, NIN]); w_out = din("w_out", [2048, D])
    w_gate = din("w_gate", [D, DFF]); w_up = din("w_up", [D, DFF]); w_down = din("w_down", [DFF, D])
    gtiles = din("gtiles", [128, 4 * D])
    vecs = din("vecs", [128, 48 + 60 + 16 + 16])
    cf_d = din("cf", [128, CF_W]); cb_d = din("cb", [128, CB_W])
    cos_d = din("cos_t", [128, 2176]); sin_d = din("sin_t", [128, 2176])

    y_p = dout("y_p", [2048, D]); y_s = dout("y_s", [128, D])
    nk_p = dout("nk_p", [128, 256]); nv_p = dout("nv_p", [128, 256])
    ncv_p = dout("ncv_p", [3, 1536]); nssm_p = dout("nssm_p", [1024, 128])
    nk_s = dout("nk_s", [16, 128, 256]); nv_s = dout("nv_s", [16, 128, 256])
    ncv_s = dout("ncv_s", [48, 1536]); nssm_s = dout("nssm_s", [16, 1024, 128])
    KDBG = os.environ.get('KDBG', '')
    if KDBG:
        dbg = nc.dram_tensor("dbg", [128, 16, 128], BF16, kind="ExternalOutput").ap()

    cur = [SB_BASE]
    bufrange = {}

    def sb(name, shape, dt, at=None):
        nbytes = int(np.prod(shape[1:])) * (4 if dt == F32 else 2)
        nbytes = (nbytes + 31) // 32 * 32
        if at is None:
            off = cur[0]
            cur[0] += nbytes
        else:
            off = at[0]
            at[0] += nbytes
        assert off + nbytes <= SB_LIMIT, (name, off, nbytes)
        if at is not None:
            bufrange[name] = (off, off + nbytes)
        return nc.alloc_sbuf_tensor_at(name, list(shape), dt, offset=off)

    G = 640
    cf = sb("cf", [128, CF_W], F32)
    cb = sb("cb", [128, CB_W], BF16)
    gt = sb("gt", [128, 4 * D], F32)
    vc = sb("vc", [128, 140], F32)
    hT = sb("hT", [128, 8, G], BF16)
    catT = sb("catT", [128, 16, G], BF16, cur)
    ovc = [bufrange["catT"][0]]
    wpx = [sb("wpx%d" % i, [128, 8, 128], BF16, ovc) for i in range(8)]
    assert ovc[0] <= bufrange["catT"][1]
    wp = [sb("wp%d" % i, [128, 8, 128], BF16, cur) for i in range(4)]
    wpb = []
    for i_, nm_ in enumerate(["wp0", "wp2", "wpx0", "wpx2", "wpx4", "wpx6"]):
        wpb.append(sb("wpb%d" % i_, [128, 8, 256], BF16, [bufrange[nm_][0]]))
    wv = sb("wv", [128, 8, 272], BF16)
    xin = [sb("xin%d" % i, [128, D], F32) for i in range(2)]
    hb = sb("hb", [128, D], BF16)
    stat = sb("stat", [128, 64], F32)
    ccar = sb("ccar", [128, 12, 3], F32)
    HT = sb("HT", [128, 1024], F32)
    HTb = sb("HTb", [128, 1024], BF16)
    kE0 = sb("kE0", [128, 4, 128], BF16); kO0 = sb("kO0", [128, 4, 128], BF16)
    va0 = sb("va0", [128, 4, 192], BF16)
    dtt = sb("dtt", [128, 5, 16], F32)
    dtA = sb("dtA", [128, 5, 16], F32)
    abc = sb("abc", [128, 16], F32)
    esr = sb("esr", [128, 16, 128], BF16)
    es = sb("es", [128, 16], F32)
    kfp = sb("kfp", [128, 4, 128], F32)
    vfp = sb("vfp", [128, 256], F32)
    otok = sb("otok", [128, 256], F32)
    otok2 = sb("otok2", [128, 256], F32)
    ARENA = cur[0]
    a = [ARENA]
    szT = sb("szT", [128, 8, G], BF16, a)
    xbcT = sb("xbcT", [128, 12, G], BF16, a)
    stg = sb("stg", [128, 3 + 512], F32, a)
    sstg = sb("sstg", [128, 16, 11], F32, a)
    acc = sb("acc", [128, 512], F32, a)
    cstT = sb("cstT", [128, 12, 48], F32, a)
    ncs = sb("ncs", [128, 12, 48], F32, a)
    sctok = sb("sctok", [48, 1536], F32, a)
    X = sb("X", [128, 16, 128], F32, a)
    Xb = nc.alloc_sbuf_tensor_at("Xb", [128, 16, 128], BF16, offset=bufrange["X"][0])
    dec0 = sb("dec0", [128, 16, 128], BF16, a)
    eac = sb("eac", [128, 16, 128], BF16, a)
    CdT0 = sb("CdT0", [128, 16, 128], BF16, a)
    xdt0 = sb("xdt0", [128, 16, 64], BF16, a)
    xtail0 = sb("xtail0", [128, 16, 64], BF16, a)
    Btok0 = sb("Btok0", [128, 2, 128], BF16, a)
    cbm = sb("cbm", [128, 2, 128], BF16, a)
    acT = sb("acT", [128, 128], F32, a)
    achi = sb("achi", [128, 128], BF16, a)
    aclo = sb("aclo", [128, 128], BF16, a)
    eal0 = sb("eal0", [128, 16], F32, a)
    tailc = sb("tailc", [128, 16], F32, a)
    gated = sb("gated", [128, 8, 128], F32, a)
    gsq = sb("gsq", [128, 8, 128], BF16, a)
    rs = sb("rs", [128, 2, 128], F32, a)
    h0n = sb("h0n", [128, 8, 128], F32, a)
    h0T0 = sb("h0T0", [128, 1024], BF16, a)
    h0T1 = sb("h0T1", [128, 1024], BF16, a)
    h0TB = [h0T0, h0T1]
    Bm = sb("Bm", [128, 16, 256], BF16, a)
    decs = sb("decs", [128, 8, 16], F32, a)
    hout = sb("hout", [128, 8, 128], F32, a)
    tmpd = sb("tmpd", [128, 16, 128], BF16, a)
    h0n2 = sb("h0n2", [128, 8, 128], F32, a)
    h0n3 = sb("h0n3", [128, 8, 128], F32, [bufrange["tmpd"][0]])
    hout2 = sb("hout2", [128, 8, 128], F32, [bufrange["X"][0]])
    stg2 = sb("stg2", [128, 3 + 512], F32, [bufrange["X"][0] + 4096])
    acc2 = sb("acc2", [128, 512], F32, [bufrange["dec0"][0]])
    stgB, accB = [stg, stg2], [acc, acc2]
    SSD_END = a[0]
    ov = [bufrange["Bm"][0]]
    dec1 = sb("dec1", [128, 16, 128], BF16, ov)
    CdT1 = sb("CdT1", [128, 16, 128], BF16, ov)
    assert ov[0] <= bufrange["Bm"][1]
    ov = [bufrange["tmpd"][0]]
    xdt1 = sb("xdt1", [128, 16, 64], BF16, ov)
    xtail1 = sb("xtail1", [128, 16, 64], BF16, ov)
    assert ov[0] <= bufrange["tmpd"][1]
    ov = [bufrange["h0n"][0]]
    Btok1 = sb("Btok1", [128, 2, 128], BF16, ov)
    eal1 = sb("eal1", [128, 16], F32, ov)
    assert ov[0] <= bufrange["h0n"][1]
    decB, CdTB, xdtB, xtailB, BtokB, ealB = [dec0, dec1], [CdT0, CdT1], [xdt0, xdt1], [xtail0, xtail1], [Btok0, Btok1], [eal0, eal1]
    a = [ARENA]
    qT = sb("qT", [128, 8, G], BF16, a)
    kE = sb("kE", [128, 4, G], BF16, a)
    kO = sb("kO", [128, 4, G], BF16, a)
    vaug = sb("vaug", [128, 5, 4, 192], BF16, a)
    cosT = sb("cosT", [128, G], F32, a)
    sinT = sb("sinT", [128, G], F32, a)
    qb = sb("qb", [128, 512], BF16, a)
    t1 = sb("t1", [128, 512], F32, a)
    pT0 = sb("pT0", [128, 2, 512], BF16, a)
    pT1 = sb("pT1", [128, 2, 512], BF16, a)
    pT2 = sb("pT2", [128, 2, 512], BF16, a)
    pTB = [pT0, pT1, pT2]
    rden = sb("rden", [128, 512], F32, a)
    kcn = sb("kcn", [128, 16, 128], BF16, a)
    kcE = sb("kcE", [128, 16, 128], BF16, a)
    kcO = sb("kcO", [128, 16, 128], BF16, a)
    vca = sb("vca", [128, 16, 192], BF16, a)
    pTc = sb("pTc", [128, 512], BF16, a)
    kcn2 = sb("kcn2", [128, 16, 128], BF16, a)
    kcnB = [kcn, kcn2]
    ATT_END = a[0]
    a = [ARENA]
    wd = sb("wd", [128, 22, D], BF16, a)
    xres = sb("xres", [128, 5, D], F32, a)
    mixs = sb("mixs", [128, D], F32, a)
    dns = mixs
    assert a[0] >= ATT_END, (a[0], ATT_END)
    a2 = [a[0]]
    wo = sb("wo", [128, 16, D], BF16, a)
    hff = sb("hff", [128, 22, G], BF16, a2)
    sg = sb("sg", [128, 512], BF16, a)
    hb1 = sb("hb1", [128, D], BF16, a)
    hbB = [hb, hb1]
    E_END = a[0]
    F_END = a2[0]
    assert max(SSD_END, ATT_END, E_END, F_END) <= SB_LIMIT, (SSD_END, ATT_END, E_END, F_END)

    banks = [stack.enter_context(nc.psum_tensor("bank%d" % i, [128, 512], F32)) for i in range(8)]
    bankR = [Reg("bank%d" % i) for i in range(8)]
    bctr = [0]

    reserved = set()

    def nb():
        while True:
            i = bctr[0] % 8
            bctr[0] += 1
            if i not in reserved:
                return banks[i], bankR[i]

    regs = {}

    special = {"ac": ["acT", "achi", "aclo"], "rope": ["cosT", "sinT"], "vca1": ["vca"]}

    def R(name):
        if name not in regs:
            rng = None
            if name in bufrange:
                rng = [bufrange[name]]
            elif name in special:
                rng = [bufrange[b] for b in special[name]]
            elif name.startswith("xres") and name[4:].isdigit():
                lo = bufrange["xres"][0] + int(name[4:]) * D * 4
                rng = [(lo, lo + D * 4)]
            regs[name] = Reg(name, rng)
            if rng:
                S.ranged.append(regs[name])
        return regs[name]

    def C(n, pack=cf, off=CF_OFF):
        a0, a1 = off[n]
        return pack[:, a0:a1]

    def CB(n):
        a0, a1 = CB_OFF[n]
        return cb[:, a0:a1]

    def dma_load(eng, out_ap, in_ap, key, writes, reads=()):
        def fn(e, sem):
            e.dma_start(out=out_ap, in_=in_ap).then_inc(sem, 16)
        S.task(eng, fn, reads=reads, writes=writes, dma=key)

    def dma_multi(eng, pairs, key, writes, reads=(), slow=False):
        def fn(e, sem):
            for o, i_ in pairs:
                if slow:
                    e.dma_start(out=o, in_=i_, allow_slow_non_contiguous=True).then_inc(sem, 16)
                else:
                    e.dma_start(out=o, in_=i_).then_inc(sem, 16)
        S.task(eng, fn, reads=reads, writes=writes, dma=key, ndma=len(pairs))

    Rc = R("consts")
    dma_load("sp", cf[:, :], cf_d[:, :], "c0", [R("c0")])
    dma_load("sp", gt[:, :], gtiles[:, :], "c1", [R("c1")])
    dma_load("sp", vc[:, :], vecs[:, :], "c2", [R("c2")])
    dma_load("pool", cb[:, :], cb_d[:, :], "c3", [R("c3")])
    S.task("dve", lambda dve: dve.memset(stat[:, 60:64], 0.0), reads=[R("c0"), R("c1"), R("c2"), R("c3")], writes=[Rc])
    gpre, gpost, gffn, gpff = (gt[:, i * D:(i + 1) * D] for i in range(4))
    dtb, alog, sinks = vc[:, 0:16], vc[:, 16:32], vc[:, 32:48]
    convw = vc[:, 48:96]
    convb = vc[:, 96:108]
    gssm = vc[:, 108:116]
    dskipc = vc[:, 116:124]

    def t_init(act):
        act.activation(out=abc[:, :], in_=alog, func=AF.Exp)
        return act.activation(out=es[:, :], in_=sinks, func=AF.Exp)
    S.task("act", t_init, reads=[Rc], writes=[R("abc"), R("es")])

    def t_init2(dve):
        dve.tensor_scalar(out=abc[:, :], in0=abc[:, :], scalar1=-1.0, scalar2=None, op0=ALU.mult)
        dve.memset(ccar[:, :, :], 0.0)
        dve.memset(HT[:, :], 0.0)
        dve.memset(HTb[:, :], 0.0)
        dve.memset(stat[:, :], 0.0)
        return dve.tensor_scalar(out=esr[:, :, :], in0=es[:, :].unsqueeze(2).to_broadcast([128, 16, 128]),
                                 scalar1=C("row0"), scalar2=None, op0=ALU.mult)
    S.task("dve", t_init2, reads=[Rc, R("abc"), R("es")], writes=[R("abc"), R("esr"), R("ccar"), R("HT"), R("HTb"), R("stat")])

    wslot = [0]

    fslot = [0]
    bslot = [0]

    def load_big(dram_w, c0):
        j = bslot[0] % 6
        bslot[0] += 1
        t, nm = wpb[j], "wpb%d" % j
        Rw = R(nm)
        src = dram_w.rearrange("(kc p) n -> p kc n", p=128)
        dma_multi("pool", [(t[:, :, :], src[:, :, c0:c0 + 256])], nm, [Rw])
        return t, Rw

    def load_panel(dram_w, c0, ncols=128, dup=False, deep=False):
        if deep:
            j = fslot[0] % 12
            fslot[0] += 1
            if j < 4:
                t, nm = wp[j], "wp%d" % j
            else:
                t, nm = wpx[j - 4], "wpx%d" % (j - 4)
            Rw = R(nm)
            src = dram_w.rearrange("(kc p) n -> p kc n", p=128)
            dma_multi("pool", [(t[:, :, 0:ncols], src[:, :, c0:c0 + ncols])], nm, [Rw])
            return t, Rw
        i = wslot[0] % 4
        wslot[0] += 1
        t = wp[i]
        Rw = R("wp%d" % i)
        src = dram_w.rearrange("(kc p) n -> p kc n", p=128)
        if dup:
            pairs = [(t[:, :, 0:64], src[:, :, c0:c0 + 64]), (t[:, :, 64:128], src[:, :, c0:c0 + 64])]
        else:
            pairs = [(t[:, :, 0:ncols], src[:, :, c0:c0 + ncols])]
        dma_multi("pool", pairs, "wp%d" % i, [Rw])
        return t, Rw

    KSTOP = os.environ.get('KSTOP', '')
    KBARS = set(os.environ.get('KBARS', '').split(','))

    class _Stop(Exception):
        pass

    def chk(tag, g):
        if KSTOP == tag + str(g):
            raise _Stop()
    try:
      for g in range(4):
          has_s = (g == 3)
          ntile = 5 if has_s else 4
          NP = 512
          ranges = [(0, 512)] + ([(512, 640)] if has_s else [])
          Rh = [R("hT%d" % t) for t in range(5)]

          def a_tile(g, lt):
              Rh = [R("hT%d" % t) for t in range(5)]
              if True:
                  xi = xin[lt % 2]
                  Rx = R("xin%d" % (lt % 2))
                  src = xs[:, :] if lt == 4 else xp[(4 * g + lt) * 128:(4 * g + lt + 1) * 128, :]
                  dma_load("sp", xi[:, :], src, "xin%d" % (lt % 2), [Rx])
                  Rst = R("stat")

                  S.task("dve", lambda dve: dve.memset(stat[:, 0:1], 0.0), writes=[Rst])
                  S.chain("act", [lambda act, xi=xi: act.activation(out=hb[:, :], in_=xi[:, :], func=AF.Square, accum_out=stat[:, 0:1]),
                                  lambda act: act.activation(out=stat[:, 1:2], in_=stat[:, 0:1], func=AF.Ln, scale=1.0 / D, bias=EPS),
                                  lambda act: act.activation(out=stat[:, 2:3], in_=stat[:, 1:2], func=AF.Exp, scale=-0.5)],
                          reads=[Rx], writes=[Rst, R("hb0")])
                  S.chain("dve", [lambda dve, xi=xi: dve.scalar_tensor_tensor(out=hb[:, :], in0=xi[:, :], scalar=stat[:, 2:3], in1=gpre,
                                                                             op0=ALU.mult, op1=ALU.mult)],
                          reads=[Rx, Rst, Rc], writes=[Rst, R("hb0")])
                  yield
                  bk, bR = nb()

                  def tA3(pe, bk=bk):
                      for kc in range(8):
                          last = pe.transpose(bk[:, kc * 64:(kc + 1) * 64].bitcast(BF16), hb[:, kc * 128:(kc + 1) * 128], CB("ident"))
                      return last
                  S.task("pe", tA3, reads=[R("hb0"), Rc], writes=[bR])

                  def tA4(act, bk=bk, lt=lt):
                      return act.activation(out=hT[:, :, lt * 128:(lt + 1) * 128],
                                            in_=bk[:, :].bitcast(BF16).rearrange("p (c t) -> p c t", c=8), func=AF.Copy)
                  S.task("act", tA4, reads=[bR], writes=[Rh[lt]])
          def phaseA(g):
              for lt_ in range(5 if g == 3 else 4):
                  for _ in a_tile(g, lt_):
                      pass
          if g == 0:
              phaseA(0)
          S.barrier(force=('A' in KBARS))
          chk('A', g)

          srcw = w_in.rearrange("(kc p) n -> p kc n", p=128)
          dma_multi("pool", [(wv[:, :, 0:256], srcw[:, :, 1280:1536]), (wv[:, :, 256:272], srcw[:, :, 4096:4112])], "wv", [R("wv")])
          if has_s and True:
              dma_load("sp", sctok[:, :], sconv[:, :], "sct", [R("sctok")])
              for c in range(12):
                  bk, bR = nb()

                  def tcs(pe, bk=bk, c=c):
                      return pe.transpose(bk[:, 0:48], sctok[0:48, c * 128:(c + 1) * 128], C("ident")[0:48, 0:48])
                  S.task("pe", tcs, reads=[R("sctok"), Rc], writes=[bR])

                  def tcs2(act, bk=bk, c=c):
                      return act.activation(out=cstT[:, c, :], in_=bk[:, 0:48], func=AF.Copy)
                  S.task("act", tcs2, reads=[bR], writes=[R("cstT")])
          for c in range(12):
              wt, Rw = load_panel(w_in, 1536 + 1024 + c * 128, deep=True)
              for (r0, r1) in ranges:
                  bk, bR = nb()

                  def tm(pe, bk=bk, wt=wt, r0=r0, r1=r1):
                      for kc in range(8):
                          last = pe.matmul(bk[:, 0:r1 - r0], wt[:, kc, :], hT[:, kc, r0:r1], start=(kc == 0), stop=(kc == 7))
                      return last
                  S.task("pe", tm, reads=[Rw] + Rh, writes=[bR])
                  if r0 == 0:
                      sg_, an_ = stgB[c % 2], accB[c % 2]
                      sgn, ann = ("stg", "acc") if c % 2 == 0 else ("stg2", "acc2")
                      S.chain("act", [lambda act, c=c, sg_=sg_: act.activation(out=sg_[:, 0:3], in_=ccar[:, c, :], func=AF.Copy),
                                      lambda act, bk=bk, sg_=sg_: act.activation(out=sg_[:, 3:515], in_=bk[:, 0:512], func=AF.Copy),
                                      lambda act, c=c, sg_=sg_: act.activation(out=ccar[:, c, :], in_=sg_[:, 512:515], func=AF.Copy)],
                              reads=[bR, R("ccar")], writes=[R(sgn), R("ccar")])
                      fl = [lambda dve, c=c, sg_=sg_, an_=an_: dve.tensor_scalar(out=an_[:, :], in0=sg_[:, 0:512], scalar1=convw[:, c * 4:c * 4 + 1], scalar2=None, op0=ALU.mult)]
                      for tap in range(1, 4):
                          fl.append(lambda dve, c=c, tap=tap, sg_=sg_, an_=an_: dve.scalar_tensor_tensor(out=an_[:, :], in0=sg_[:, tap:tap + 512], scalar=convw[:, c * 4 + tap:c * 4 + tap + 1],
                                                                                                       in1=an_[:, :], op0=ALU.mult, op1=ALU.add))
                      S.chain("dve", fl, reads=[R(sgn), Rc], writes=[R(ann)])

                      def tc3(act, c=c, an_=an_):
                          return act.activation(out=xbcT[:, c, 0:512], in_=an_[:, :], func=AF.Silu, bias=convb[:, c:c + 1], scale=1.0)
                      S.task("act", tc3, reads=[R(ann), Rc], writes=[R("xbcT")])
                  else:
                      S.chain("act", [lambda act, c=c: act.activation(out=sstg[:, :, 0:3], in_=cstT[:, c, :].rearrange("p (b t) -> p b t", t=3), func=AF.Copy),
                                      lambda act, bk=bk: act.activation(out=sstg[:, :, 3:11], in_=bk[:, 0:128].rearrange("p (b t) -> p b t", t=8), func=AF.Copy),
                                      lambda act, c=c: act.activation(out=ncs[:, c, :].rearrange("p (b t) -> p b t", t=3), in_=sstg[:, :, 8:11], func=AF.Copy)],
                              reads=[bR, R("cstT")], writes=[R("sstg"), R("ncs")])
                      av = acc[:, 0:128].rearrange("p (b t) -> p b t", t=8)
                      fl = [lambda dve, c=c, av=av: dve.tensor_scalar(out=av, in0=sstg[:, :, 0:8], scalar1=convw[:, c * 4:c * 4 + 1], scalar2=None, op0=ALU.mult)]
                      for tap in range(1, 4):
                          fl.append(lambda dve, c=c, tap=tap, av=av: dve.scalar_tensor_tensor(out=av, in0=sstg[:, :, tap:tap + 8], scalar=convw[:, c * 4 + tap:c * 4 + tap + 1],
                                                                                              in1=av, op0=ALU.mult, op1=ALU.add))
                      S.chain("dve", fl, reads=[R("sstg"), Rc], writes=[R("acc")])

                      def ts3(act, c=c):
                          return act.activation(out=xbcT[:, c, 512:640], in_=acc[:, 0:128], func=AF.Silu, bias=convb[:, c:c + 1], scale=1.0)
                      S.task("act", ts3, reads=[R("acc"), Rc], writes=[R("xbcT")])
          for c in range(8):
              wt, Rw = load_panel(w_in, 1536 + c * 128, deep=True)
              for (r0, r1) in ranges:
                  bk, bR = nb()

                  def tm(pe, bk=bk, wt=wt, r0=r0, r1=r1):
                      for kc in range(8):
                          last = pe.matmul(bk[:, 0:r1 - r0], wt[:, kc, :], hT[:, kc, r0:r1], start=(kc == 0), stop=(kc == 7))
                      return last
                  S.task("pe", tm, reads=[Rw] + Rh, writes=[bR])

                  def tz(act, bk=bk, c=c, r0=r0, r1=r1):
                      return act.activation(out=szT[:, c, r0:r1], in_=bk[:, 0:r1 - r0], func=AF.Silu)
                  S.task("act", tz, reads=[bR], writes=[R("szT")])
          for lt in range(ntile):
              bk, bR = nb()

              def tdt(pe, bk=bk, lt=lt):
                  for kc in range(8):
                      last = pe.matmul(bk[:, 0:16], hT[:, kc, lt * 128:(lt + 1) * 128], wv[:, kc, 256:272], start=(kc == 0), stop=(kc == 7))
                  return last
              S.task("pe", tdt, reads=[R("wv")] + Rh, writes=[bR])

              def tdt2(dve, bk=bk, lt=lt):
                  return dve.tensor_tensor(out=dtt[:, lt, :], in0=bk[:, 0:16], in1=dtb, op=ALU.add)
              S.task("dve", tdt2, reads=[bR, Rc], writes=[R("dtt")])
          S.chain("act", [lambda act, ntile=ntile: act.activation(out=dtt[:, 0:ntile, :], in_=dtt[:, 0:ntile, :], func=AF.Exp),
                          lambda act, ntile=ntile: act.activation(out=dtt[:, 0:ntile, :], in_=dtt[:, 0:ntile, :], func=AF.Ln, bias=1.0, scale=1.0)],
                  reads=[R("dtt")], writes=[R("dtt")])

          def tdt4(dve, ntile=ntile):
              return dve.tensor_tensor(out=dtA[:, 0:ntile, :], in0=dtt[:, 0:ntile, :], in1=abc[:, :].unsqueeze(1).to_broadcast([128, ntile, 16]), op=ALU.mult)
          S.task("dve", tdt4, reads=[R("dtt"), R("abc")], writes=[R("dtA")])

          S.barrier(force=('B1' in KBARS))
          chk('B1', g)
          def ssd_front(lt):
              par = (lt % 2) if lt < 4 else 0
              dec, CdT, xdt, xtail, Btok, eal = decB[par], CdTB[par], xdtB[par], xtailB[par], BtokB[par], ealB[par]
              samp = (lt == 4)
              ci = 4 * g + lt
              cs = slice(lt * 128, (lt + 1) * 128)
              tri = C("tri_s") if samp else C("tri_p")
              RS = R("ssdtmp")
              bk, bR = nb()

              def tac(pe, bk=bk, lt=lt, tri=tri):
                  pe.matmul(bk[0:16, 0:128], dtA[:, lt, :], tri, start=True, stop=True)
                  return pe.matmul(bk[:, 128:144], C("ones"), dtA[:, lt, :], start=True, stop=True)
              S.task("pe", tac, reads=[R("dtA"), Rc], writes=[bR])

              S.chain("dve", [lambda dve, bk=bk: dve.tensor_copy(out=acT[0:16, :], in_=bk[0:16, 0:128]),
                              lambda dve: dve.tensor_copy(out=achi[0:16, :], in_=acT[0:16, :]),
                              lambda dve: dve.tensor_tensor(out=aclo[0:16, :], in0=acT[0:16, :], in1=achi[0:16, :], op=ALU.subtract)],
                      reads=[bR], writes=[R("ac"), bR])

              def tac3(act, bk=bk):
                  return act.activation(out=eal[:, :], in_=bk[:, 128:144], func=AF.Exp)
              S.task("act", tac3, reads=[bR], writes=[R("eal%d" % par), bR])
              def tX(dve, lt=lt, tri=tri):
                  return dve.tensor_tensor(out=Xb[:, :, :], in0=tri.unsqueeze(1).to_broadcast([128, 16, 128]),
                                           in1=dtA[:, lt, :].unsqueeze(2).to_broadcast([128, 16, 128]), op=ALU.mult)
              S.task("dve", tX, reads=[R("dtA"), Rc], writes=[R("X")])
              yield
              sb_ = [nb() for _ in range(4)]

              def tseg(pe, sb_=sb_):
                  for q4 in range(4):
                      last = pe.matmul(sb_[q4][0][:, :], CB("U"), Xb[:, q4 * 4:(q4 + 1) * 4, :], start=True, stop=True)
                  return last
              S.task("pe", tseg, reads=[R("X"), Rc], writes=[b[1] for b in sb_])

              def tdec(act, sb_=sb_):
                  for q4 in range(4):
                      last = act.activation(out=dec[:, q4 * 4:(q4 + 1) * 4, :], in_=sb_[q4][0][:, :].rearrange("p (h t) -> p h t", h=4), func=AF.Exp)
                  return last
              S.task("act", tdec, reads=[b[1] for b in sb_], writes=[R("dec%d" % par)])
              yield
              bk, bR = nb()

              def tcb(pe, bk=bk, cs=cs):
                  for gg in range(2):
                      last = pe.matmul(bk[:, gg * 128:(gg + 1) * 128], xbcT[:, 8 + gg, cs], xbcT[:, 10 + gg, cs], start=True, stop=True)
                  return last
              S.task("pe", tcb, reads=[R("xbcT")], writes=[bR])

              def tcb2(dve, bk=bk, tri=tri):
                  return dve.tensor_tensor(out=cbm[:, :, :], in0=bk[:, 0:256].rearrange("p (g t) -> p g t", g=2),
                                           in1=tri.unsqueeze(1).to_broadcast([128, 2, 128]), op=ALU.mult)
              S.task("dve", tcb2, reads=[bR, Rc], writes=[R("cbm")])
              if samp:
                  S.chain("dve", [lambda dve: dve.tensor_tensor(out=tmpd[:, :, :], in0=dec[:, :, :], in1=C("lastmask").unsqueeze(1).to_broadcast([128, 16, 128]), op=ALU.mult),
                                  lambda dve: dve.tensor_reduce(out=tailc[:, :], in_=tmpd[:, :, :], axis=AX.X, op=ALU.add)],
                          reads=[R("dec%d" % par), Rc], writes=[R("tailc"), R("tmpd")])
              else:
                  def ttl(dve):
                      return dve.tensor_copy(out=tailc[:, :], in_=dec[:, :, 127])
                  S.task("dve", ttl, reads=[R("dec%d" % par)], writes=[R("tailc")])
              bk, bR = nb()
              bk2, bR2 = nb()

              def ttr(pe, bk=bk, bk2=bk2, cs=cs):
                  for c in range(8):
                      pe.transpose(bk[:, c * 64:(c + 1) * 64].bitcast(BF16), xbcT[:, c, cs], CB("ident"))
                  for gg in range(2):
                      last = pe.transpose(bk2[:, gg * 64:(gg + 1) * 64].bitcast(BF16), xbcT[:, 8 + gg, cs], CB("ident"))
                  return last
              S.task("pe", ttr, reads=[R("xbcT"), Rc], writes=[bR, bR2])

              S.chain("dve", [lambda dve, bk=bk, lt=lt: dve.tensor_tensor(out=xdt[:, :, :], in0=bk[:, :].bitcast(BF16).rearrange("p (h d) -> p h d", h=16),
                                                                          in1=dtt[:, lt, :].unsqueeze(2).to_broadcast([128, 16, 64]), op=ALU.mult),
                              lambda dve: dve.tensor_tensor(out=xtail[:, :, :], in0=xdt[:, :, :], in1=tailc[:, :].unsqueeze(2).to_broadcast([128, 16, 64]), op=ALU.mult),
                              lambda dve, bk2=bk2: dve.tensor_copy(out=Btok[:, :, :], in_=bk2[:, 0:128].bitcast(BF16).rearrange("p (g n) -> p g n", g=2))],
                      reads=[bR, bR2, R("dtt"), R("tailc")], writes=[R("xdt%d" % par), R("xtail%d" % par), R("Btok%d" % par)])
              yield
              eb = [nb() for _ in range(4)]

              def teac(pe, eb=eb):
                  for h in range(16):
                      o = eb[h // 4][0][:, (h % 4) * 128:(h % 4 + 1) * 128]
                      a0 = CB_OFF["sel16"][0] + h * 128
                      pe.matmul(o, cb[0:16, a0:a0 + 128], achi[0:16, :], start=True, stop=False)
                      last = pe.matmul(o, cb[0:16, a0:a0 + 128], aclo[0:16, :], start=False, stop=True)
                  return last
              S.task("pe", teac, reads=[R("ac"), Rc], writes=[b[1] for b in eb])

              def teac2(act, eb=eb):
                  for q4 in range(4):
                      last = act.activation(out=eac[:, q4 * 4:(q4 + 1) * 4, :], in_=eb[q4][0][:, :].rearrange("p (h t) -> p h t", h=4), func=AF.Exp)
                  return last
              S.task("act", teac2, reads=[b[1] for b in eb], writes=[R("eac")])
              def twt(dve, cs=cs):
                  return dve.tensor_tensor(out=dec[:, :, :].rearrange("p (g e) t -> p g e t", g=2), in0=dec[:, :, :].rearrange("p (g e) t -> p g e t", g=2),
                                           in1=cbm[:, :, :].unsqueeze(2).to_broadcast([128, 2, 8, 128]), op=ALU.mult)
              S.task("dve", twt, reads=[R("dec%d" % par), R("cbm"), R("tailc")], writes=[R("dec%d" % par)])

              def twt2(dve, cs=cs):
                  return dve.tensor_tensor(out=CdT[:, :, :].rearrange("p (g e) t -> p g e t", g=2), in0=eac[:, :, :].rearrange("p (g e) t -> p g e t", g=2),
                                            in1=xbcT[:, 10:12, cs].unsqueeze(2).to_broadcast([128, 2, 8, 128]), op=ALU.mult)
              S.task("dve", twt2, reads=[R("eac"), R("xbcT")], writes=[R("CdT%d" % par)])
          def ssd_back(lt):
              par = (lt % 2) if lt < 4 else 0
              dec, CdT, xdt, xtail, Btok, eal = decB[par], CdTB[par], xdtB[par], xtailB[par], BtokB[par], ealB[par]
              samp = (lt == 4)
              ci = 4 * g + lt
              cs = slice(lt * 128, (lt + 1) * 128)
              tri = C("tri_s") if samp else C("tri_p")
              yb = [nb(), nb()]
              if samp:
                  reserved.update(banks.index(yb[0][0]), ) if False else None
                  for _b in yb:
                      reserved.add([id(x) for x in banks].index(id(_b[0])))
              first_chunk = (ci == 0 and not samp)
              if samp:
                  def tBm(dve):
                      return dve.tensor_tensor(out=Bm[:, :, :], in0=Btok[:, :, :].rearrange("p g n -> p (g n)").unsqueeze(1).to_broadcast([128, 16, 256]),
                                               in1=C("oh").unsqueeze(2).to_broadcast([128, 16, 256]), op=ALU.mult)
                  S.task("dve", tBm, reads=[R("Btok%d" % par), Rc], writes=[R("Bm")])
                  bk3, bR3 = nb()

                  def tds(pe, bk3=bk3):
                      for pr in range(8):
                          a0 = CB_OFF["selpair"][0] + pr * 128
                          o = bk3[:, pr * 16:(pr + 1) * 16]
                          pe.matmul(o, cb[0:16, a0:a0 + 128], achi[0:16, 7:128:8], start=True, stop=False)
                          last = pe.matmul(o, cb[0:16, a0:a0 + 128], aclo[0:16, 7:128:8], start=False, stop=True)
                      return last
                  S.task("pe", tds, reads=[R("ac"), Rc], writes=[bR3])

                  def tds2(act, bk3=bk3):
                      return act.activation(out=decs[:, :, :], in_=bk3[:, 0:128].rearrange("p (r b) -> p r b", r=8), func=AF.Exp)
                  S.task("act", tds2, reads=[bR3], writes=[R("decs")])

              def tyi(pe, yb=yb, first_chunk=first_chunk, samp=samp):
                  for h in range(16):
                      pr = h // 2
                      o = yb[pr // 4][0][64 * (h % 2):64 * (h % 2) + 64, (pr % 4) * 128:(pr % 4 + 1) * 128]
                      last = pe.matmul(o, xdt[:, h, :], dec[:, h, :], start=((pr % 4 == 0) if samp else True), stop=first_chunk, tile_position=(0, 64 * (h % 2)), skip_group_check=samp)
                      if not first_chunk and not samp:
                          last = pe.matmul(o, HTb[:, h * 64:(h + 1) * 64], CdT[:, h, :], start=False, stop=True, tile_position=(0, 64 * (h % 2)))
                  return last
              S.task("pe", tyi, reads=[R("xdt%d" % par), R("dec%d" % par), R("CdT%d" % par), R("HTb")], writes=[yb[0][1], yb[1][1]])
              if samp:
                  seqctx = {}
                  def seq_s1(b):
                      h0n_, hn_ = [(h0n, "h0n"), (h0n2, "h0n2"), (h0n3, "h0n3")][b % 3]
                      hout_, ho_ = (hout, "hout") if b % 2 == 0 else (hout2, "hout2")
                      dma_load("pool", h0n_[:, :, :], sssm[b].rearrange("(r q) n -> q r n", q=128), hn_, [R(hn_)])
                      tb = [nb(), nb()]

                      def th0(pe, tb=tb, h0n_=h0n_):
                          for pr in range(8):
                              last = pe.transpose(tb[pr // 4][0][:, (pr % 4) * 128:(pr % 4 + 1) * 128], h0n_[:, pr, :], C("ident"))
                          return last
                      S.task("pe", th0, reads=[R(hn_), Rc], writes=[tb[0][1], tb[1][1]])

                      def th1(act, tb=tb, h0T=h0TB[b % 2]):
                          act.activation(out=h0T[:, 0:512], in_=tb[0][0][:, :], func=AF.Copy)
                          return act.activation(out=h0T[:, 512:1024], in_=tb[1][0][:, :], func=AF.Copy)
                      S.task("act", th1, reads=[tb[0][1], tb[1][1]], writes=[R("h0T%d" % (b % 2))])

                      seqctx[b] = tb
                  def seq_s2(b):
                      h0n_, hn_ = [(h0n, "h0n"), (h0n2, "h0n2"), (h0n3, "h0n3")][b % 3]
                      hout_, ho_ = (hout, "hout") if b % 2 == 0 else (hout2, "hout2")
                      tb = seqctx.pop(b)
                      def th2(pe, yb=yb, b=b, h0T=h0TB[b % 2]):
                          for h in range(16):
                              pr = h // 2
                              o = yb[pr // 4][0][64 * (h % 2):64 * (h % 2) + 64, (pr % 4) * 128 + 8 * b:(pr % 4) * 128 + 8 * b + 8]
                              last = pe.matmul(o, h0T[:, h * 64:(h + 1) * 64], CdT[:, h, 8 * b:8 * b + 8], start=False, stop=(b == 15), skip_group_check=True,
                                               tile_position=(0, 64 * (h % 2)))
                          return last
                      S.task("pe", th2, reads=[R("h0T%d" % (b % 2)), R("CdT%d" % par)], writes=[yb[0][1], yb[1][1]])
                      hb2 = [nb(), nb()]

                      def th3(pe, hb2=hb2, b=b):
                          for pr in range(8):
                              gg = pr // 4
                              last = pe.matmul(hb2[pr // 4][0][:, (pr % 4) * 128:(pr % 4 + 1) * 128], xtail[:, :, :].rearrange("p h d -> p (h d)")[:, pr * 128:(pr + 1) * 128],
                                               Bm[:, b, gg * 128:(gg + 1) * 128], start=True, stop=True)
                          return last
                      S.task("pe", th3, reads=[R("xtail%d" % par), R("Bm")], writes=[hb2[0][1], hb2[1][1]])

                      def th4(dve, hb2=hb2, b=b, h0n_=h0n_, hout_=hout_):
                          for pr in range(8):
                              last = dve.scalar_tensor_tensor(out=hout_[:, pr, :], in0=h0n_[:, pr, :], scalar=decs[:, pr, b:b + 1],
                                                              in1=hb2[pr // 4][0][:, (pr % 4) * 128:(pr % 4 + 1) * 128], op0=ALU.mult, op1=ALU.add)
                          return last
                      S.task("dve", th4, reads=[hb2[0][1], hb2[1][1], R(hn_), R("decs")], writes=[R(ho_)])
                      dma_load("sp", nssm_s[b].rearrange("(r q) n -> q r n", q=128), hout_[:, :, :], ho_, [], reads=[R(ho_)])

                  for b in range(16):
                      seq_s1(b)
                      seq_s2(b)
              else:
                  hb2 = [nb(), nb()]

                  def tst(pe, hb2=hb2):
                      for gg in range(2):
                          last = pe.matmul(hb2[gg][0][:, :], Btok[:, gg, :], xtail[:, :, :].rearrange("p h d -> p (h d)")[:, gg * 512:(gg + 1) * 512], start=True, stop=True)
                      return last
                  S.task("pe", tst, reads=[R("Btok%d" % par), R("xtail%d" % par)], writes=[hb2[0][1], hb2[1][1]])

                  hv = HT[:, :].rearrange("p (h d) -> p h d", h=16)
                  S.chain("dve", [lambda dve, hv=hv: dve.tensor_tensor(out=hv, in0=hv, in1=eal[:, :].unsqueeze(2).to_broadcast([128, 16, 64]), op=ALU.mult),
                                  lambda dve, hb2=hb2: dve.tensor_tensor(out=HT[:, 0:512], in0=HT[:, 0:512], in1=hb2[0][0][:, :], op=ALU.add),
                                  lambda dve, hb2=hb2: dve.tensor_tensor(out=HT[:, 512:1024], in0=HT[:, 512:1024], in1=hb2[1][0][:, :], op=ALU.add)],
                          reads=[hb2[0][1], hb2[1][1], R("eal%d" % par)], writes=[R("HT")])
              reserved.clear()
              def tg1(dve, yb=yb, cs=cs):
                  for pr in range(8):
                      yv = yb[pr // 4][0][:, (pr % 4) * 128:(pr % 4 + 1) * 128]
                      last = dve.scalar_tensor_tensor(out=gated[:, pr, :], in0=xbcT[:, pr, cs], scalar=dskipc[:, pr:pr + 1], in1=yv, op0=ALU.mult, op1=ALU.add)
                  return last
              S.chain("dve", [tg1, lambda dve, cs=cs: dve.tensor_tensor(out=gated[:, :, :], in0=gated[:, :, :], in1=szT[:, :, cs], op=ALU.mult)],
                      reads=[yb[0][1], yb[1][1], R("xbcT"), R("szT"), Rc], writes=[R("gated")])
              yield
              if not samp:
                  def tst3(act):
                      return act.activation(out=HTb[:, :], in_=HT[:, :], func=AF.Copy)
                  S.task("act", tst3, reads=[R("HT")], writes=[R("HTb")])

              def tg2(act):
                  return act.activation(out=gsq[:, :, :], in_=gated[:, :, :], func=AF.Square)
              S.task("act", tg2, reads=[R("gated")], writes=[R("gsq")])
              bk, bR = nb()

              def tg3(pe, bk=bk):
                  for pr in range(8):
                      gg = pr // 4
                      last = pe.matmul(bk[:, gg * 128:(gg + 1) * 128], CB("ones"), gsq[:, pr, :], start=(pr % 4 == 0), stop=(pr % 4 == 3))
                  return last
              S.task("pe", tg3, reads=[R("gsq"), Rc], writes=[bR])

              S.chain("act", [lambda act, bk=bk: act.activation(out=rs[:, :, :], in_=bk[:, 0:256].rearrange("p (g t) -> p g t", g=2), func=AF.Ln, scale=1.0 / 512, bias=EPS),
                              lambda act: act.activation(out=rs[:, :, :], in_=rs[:, :, :], func=AF.Exp, scale=-0.5)],
                      reads=[bR], writes=[R("rs")])
              yield

              def tg5(dve, cs=cs):
                  for pr in range(8):
                      last = dve.scalar_tensor_tensor(out=catT[:, 8 + pr, cs], in0=gated[:, pr, :], scalar=gssm[:, pr:pr + 1],
                                                      in1=rs[:, pr // 4, :], op0=ALU.mult, op1=ALU.mult)
                  return last
              S.task("dve", tg5, reads=[R("rs"), R("gated"), Rc], writes=[R("catT")])
              if ci == 15 and not samp:
                  tb = [nb(), nb()]

                  def tfin(pe, tb=tb):
                      for pr in range(8):
                          last = pe.transpose(tb[pr // 4][0][:, (pr % 4) * 128:(pr % 4 + 1) * 128], HT[:, pr * 128:(pr + 1) * 128], C("ident"))
                      return last
                  S.task("pe", tfin, reads=[R("HT"), Rc], writes=[tb[0][1], tb[1][1]])

                  def tfin2(act, tb=tb):
                      act.activation(out=hout[:, 0:4, :], in_=tb[0][0][:, :].rearrange("p (r n) -> p r n", r=4), func=AF.Copy)
                      return act.activation(out=hout[:, 4:8, :], in_=tb[1][0][:, :].rearrange("p (r n) -> p r n", r=4), func=AF.Copy)
                  S.task("act", tfin2, reads=[tb[0][1], tb[1][1]], writes=[R("hout")])
                  dma_load("sp", nssm_p.rearrange("(r q) n -> q r n", q=128), hout[:, :, :], "hout", [], reads=[R("hout")])
          def run_il(gens):
              gens = list(gens)
              while gens:
                  for g_ in list(gens):
                      try:
                          next(g_)
                      except StopIteration:
                          gens.remove(g_)
          run_il([ssd_front(0)])
          for lt in range(4):
              if lt + 1 < 4:
                  run_il([ssd_front(lt + 1), ssd_back(lt)])
              else:
                  run_il([ssd_back(lt)])
          if has_s:
              run_il([ssd_front(4)])
              run_il([ssd_back(4)])
          if has_s:
              for c in range(12):
                  bk, bR = nb()

                  def tco(pe, bk=bk, c=c):
                      pe.transpose(bk[0:48, 0:128], ncs[:, c, :], C("ident"))
                      return pe.transpose(bk[0:3, 128:256], ccar[:, c, :], C("ident"))
                  S.task("pe", tco, reads=[R("ncs"), R("ccar"), Rc], writes=[bR])

                  def tco2(act, bk=bk, c=c):
                      act.activation(out=sctok[0:48, c * 128:(c + 1) * 128], in_=bk[0:48, 0:128], func=AF.Copy)
                      return act.activation(out=stg[0:3, 0:128], in_=bk[0:3, 128:256], func=AF.Copy)
                  S.task("act", tco2, reads=[bR], writes=[R("sctok"), R("stg")])
                  dma_load("sp", ncv_p[:, c * 128:(c + 1) * 128], stg[0:3, 0:128], "ncv", [], reads=[R("stg")])
                  R("stg").r.append(("dq_ncv", S.dsem["ncv"][1]))
              dma_load("sp", ncv_s[:, :], sctok[0:48, :], "ncvs", [], reads=[R("sctok")])
          S.barrier(force=('C' in KBARS))
          chk('C', g)

          dma_multi("sp", [(cosT[:, 0:512], cos_d[:, g * 512:(g + 1) * 512]), (sinT[:, 0:512], sin_d[:, g * 512:(g + 1) * 512])], "rope", [R("rope")])
          if has_s:
              dma_multi("sp", [(cosT[:, 512:640], cos_d[:, 2048:2176]), (sinT[:, 512:640], sin_d[:, 2048:2176])], "rope", [R("rope")])

          def tz0(dve):
              dve.memset(kE[64:128, :, :], 0.0)
              dve.memset(kO[0:64, :, :], 0.0)
              return dve.memset(vaug[:, :, :, 64:128], 1.0)
          S.task("dve", tz0, writes=[R("kE"), R("kO"), R("vaug")])
          if KSTOP == 'B2a%d' % g:
              S.barrier()
              chk('B2a', g)
          for c in range(12):
              isk = c >= 8
              if c == 8 and KSTOP == 'B2b%d' % g:
                  S.barrier()
                  chk('B2b', g)
              if isk:
                  wt, Rw = load_panel(w_in, 1024 + (c - 8) * 64, dup=True)
              else:
                  wt, Rw = load_panel(w_in, c * 128)
              for (r0, r1) in ranges:
                  n = r1 - r0
                  bk, bR = nb()

                  def tm(pe, bk=bk, wt=wt, r0=r0, r1=r1):
                      for kc in range(8):
                          last = pe.matmul(bk[:, 0:r1 - r0], wt[:, kc, :], hT[:, kc, r0:r1], start=(kc == 0), stop=(kc == 7))
                      return last
                  S.task("pe", tm, reads=[Rw] + Rh, writes=[bR])

                  def tq1(act, bk=bk, n=n):
                      return act.activation(out=qb[:, 0:n], in_=bk[:, 0:n], func=AF.Copy)
                  S.task("act", tq1, reads=[bR], writes=[R("qb"), bR])

                  def tq2(dve, bk=bk, r0=r0, r1=r1, n=n):
                      return dve.tensor_tensor(out=t1[:, 0:n], in0=bk[:, 0:n], in1=cosT[:, r0:r1], op=ALU.mult)
                  S.task("dve", tq2, reads=[bR, R("rope")], writes=[R("t1"), bR])
                  bk2, bR2 = nb()

                  def tq3(pe, bk2=bk2, n=n):
                      return pe.matmul(bk2[:, 0:n], CB("prot"), qb[:, 0:n], start=True, stop=True)
                  S.task("pe", tq3, reads=[R("qb"), Rc], writes=[bR2])
                  if not isk:
                      S.chain("dve", [lambda dve, bk2=bk2, r0=r0, r1=r1, n=n: dve.tensor_tensor(out=rden[:, 0:n], in0=bk2[:, 0:n], in1=sinT[:, r0:r1], op=ALU.mult),
                                      lambda dve, c=c, r0=r0, r1=r1, n=n: dve.tensor_tensor(out=qT[:, c, r0:r1], in0=rden[:, 0:n], in1=t1[:, 0:n], op=ALU.add)],
                              reads=[bR2, R("t1"), R("rope")], writes=[R("qT"), R("rden")])
                  else:
                      kv = c - 8

                      def tk4c(dve, kv=kv, r0=r0, r1=r1, n=n, g=g):
                          dve.tensor_copy(out=kE[0:64, kv, r0:r1], in_=rden[0:64, 0:n])
                          last = dve.tensor_copy(out=kO[64:128, kv, r0:r1], in_=rden[64:128, 0:n])
                          if r0 == 512:
                              last = dve.tensor_copy(out=kfp[:, kv, :], in_=rden[:, 0:128])
                          elif g == 3:
                              last = dve.tensor_copy(out=kfp[:, kv, :], in_=rden[:, 384:512])
                          return last
                      S.chain("dve", [lambda dve, bk2=bk2, r0=r0, r1=r1, n=n: dve.tensor_tensor(out=rden[:, 0:n], in0=bk2[:, 0:n], in1=sinT[:, r0:r1], op=ALU.mult),
                                      lambda dve, n=n: dve.tensor_tensor(out=rden[:, 0:n], in0=rden[:, 0:n], in1=t1[:, 0:n], op=ALU.add),
                                      tk4c],
                              reads=[bR2, R("t1"), R("rope")], writes=[R("kE"), R("kO"), R("rden"), R("kfp")])
                      if g == 3:
                          bk3, bR3 = nb()

                          def tko(pe, bk3=bk3, kv=kv):
                              return pe.transpose(bk3[:, 0:128], kfp[:, kv, :], C("ident"))
                          S.task("pe", tko, reads=[R("kfp"), Rc], writes=[bR3])

                          ot = otok if r0 == 0 else otok2
                          otn = "otok" if r0 == 0 else "otok2"

                          def tko2(act, bk3=bk3, kv=kv, ot=ot):
                              return act.activation(out=ot[:, kv * 64:(kv + 1) * 64], in_=bk3[:, 0:64], func=AF.Copy)
                          S.task("act", tko2, reads=[bR3], writes=[R(otn)])
                          if kv == 3:
                              if r0 == 0:
                                  dma_load("sp", nk_p[:, :], ot[:, :], otn, [], reads=[R(otn)])
                              else:
                                  dma_multi("sp", [(nk_s[b, 120:128, :], ot[8 * b:8 * b + 8, :]) for b in range(16)], otn, [], reads=[R(otn)])
          if KSTOP == 'B2c%d' % g:
              S.barrier()
              chk('B2c', g)
          for lt in range(ntile):
              bk, bR = nb()

              def tv(pe, bk=bk, lt=lt):
                  for kc in range(8):
                      last = pe.matmul(bk[:, 0:256], hT[:, kc, lt * 128:(lt + 1) * 128], wv[:, kc, 0:256], start=(kc == 0), stop=(kc == 7))
                  return last
              S.task("pe", tv, reads=[R("wv")] + Rh, writes=[bR])

              def tv2(act, bk=bk, lt=lt):
                  vv = bk[:, 0:256].rearrange("p (k d) -> p k d", k=4)
                  act.activation(out=vaug[:, lt, :, 0:64], in_=vv, func=AF.Copy)
                  return act.activation(out=vaug[:, lt, :, 128:192], in_=vv, func=AF.Copy)
              S.task("act", tv2, reads=[bR], writes=[R("vaug"), bR])
              if g == 3 and lt >= 3:
                  def tv3(dve, bk=bk):
                      return dve.tensor_copy(out=vfp[:, :], in_=bk[:, 0:256])
                  S.task("dve", tv3, reads=[bR], writes=[R("vfp"), bR])
                  if lt == 3:
                      dma_load("sp", nv_p[:, :], vfp[:, :], "vfp", [], reads=[R("vfp")])
                  else:
                      dma_multi("sp", [(nv_s[b, 120:128, :], vfp[8 * b:8 * b + 8, :]) for b in range(16)], "vfp", [], reads=[R("vfp")])
                  R("vfp").r.append(("dq_vfp", S.dsem["vfp"][1]))
          if has_s:
              dma_multi("sp", [(nk_s[:, 0:120, :], ck[:, 8:128, :, :].rearrange("b s k d -> b s (k d)")),
                               (nv_s[:, 0:120, :], cv[:, 8:128, :, :].rearrange("b s k d -> b s (k d)"))], "cshift", [])

          S.barrier(force=('B2' in KBARS))
          chk('B2', g)
          srco = w_out.rearrange("(kc p) n -> p kc n", p=128)
          dma_multi("pool", [(wo[:, 4 * i:4 * i + 4, :], srco[:, 4 * i:4 * i + 4, :]) for i in range(4)], "wo", [R("wo")])
          attn_ctx = {}
          def att_s1(u, lt, kv):
              samp = (lt == 4)
              ci = 4 * g + lt
              cs = slice(lt * 128, (lt + 1) * 128)
              has_prev = (not samp) and ci > 0
              pp = u % 3
              pT = pTB[pp]
              if samp:
                  kcn_ = kcnB[kv % 2]
                  kcnn = "kcn" if kv % 2 == 0 else "kcn2"
                  dma_multi("pool", [(kcn_[:, :, 0:64], ck[:, :, kv, :].rearrange("b s d -> s b d")),
                                     (kcn_[:, :, 64:128], ck[:, :, kv, :].rearrange("b s d -> s b d"))], kcnn, [R(kcnn)])
                  dma_multi("pool", [(vca[:, :, 0:64], cv[:, :, kv, :].rearrange("b s d -> s b d")),
                                     (vca[:, :, 128:192], cv[:, :, kv, :].rearrange("b s d -> s b d"))], "vcaL", [R("vca")])

                  if kv == 0:
                      def tkc0(dve):
                          dve.memset(kcE[64:128, :, :], 0.0)
                          return dve.memset(kcO[0:64, :, :], 0.0)
                      S.task("dve", tkc0, reads=[], writes=[R("kcE"), R("kcO")])
                  S.task("dve", lambda dve: dve.memset(vca[:, :, 64:128], 1.0), reads=[], writes=[R("vca"), R("vca1")])
                  for b4 in range(4):
                      bk, bR = nb()

                      def tkc(pe, bk=bk, b4=b4, kcn_=kcn_):
                          for j in range(4):
                              last = pe.transpose(bk[:, j * 64:(j + 1) * 64].bitcast(BF16), kcn_[:, b4 * 4 + j, :], CB("ident"))
                          return last
                      S.task("pe", tkc, reads=[R(kcnn), Rc], writes=[bR])

                      def tkc2(act, bk=bk, b4=b4):
                          vv = bk[:, 0:256].bitcast(BF16).rearrange("p (j s) -> p j s", j=4)
                          act.activation(out=kcE[0:64, b4 * 4:b4 * 4 + 4, :], in_=vv[0:64], func=AF.Copy)
                          return act.activation(out=kcO[64:128, b4 * 4:b4 * 4 + 4, :], in_=vv[64:128], func=AF.Copy)
                      S.task("act", tkc2, reads=[bR], writes=[R("kcE"), R("kcO")])
                  bkc, bRc = nb()

                  def tsc(pe, bkc=bkc, kv=kv):
                      pe.matmul(bkc[:, :], CB("ident"), CB("mc"), start=True, stop=False)
                      for b in range(16):
                          for gq in range(4):
                              c = 2 * kv + gq // 2
                              kk = kcE if gq % 2 == 0 else kcO
                              last = pe.matmul(bkc[:, b * 32 + gq * 8:b * 32 + gq * 8 + 8], kk[:, b, :], qT[:, c, 512 + 8 * b:512 + 8 * b + 8],
                                               start=False, stop=(b == 15 and gq == 3))
                      return last
                  S.task("pe", tsc, reads=[R("kcE"), R("kcO"), R("qT"), Rc], writes=[bRc])

                  def tsc2(act, bkc=bkc):
                      return act.activation(out=pTc[:, :], in_=bkc[:, :], func=AF.Exp, scale=0.125)
                  S.task("act", tsc2, reads=[bRc], writes=[R("pTc")])
              sbk = []
              for j in ([0, 1] if has_prev else [1]):
                  bk, bR = nb()
                  sbk.append((j, bk, bR))

              def tsc_(pe, sbk=sbk, kv=kv, lt=lt, cs=cs, samp=samp):
                  for j, bk, bR in sbk:
                      mk = CB("mcur_s") if samp else (CB("mcur") if j == 1 else CB("mprev"))
                      pe.matmul(bk[:, :], CB("ident"), mk, start=True, stop=False)
                      for gq in range(4):
                          c = 2 * kv + gq // 2
                          if j == 1:
                              kk = (kE if gq % 2 == 0 else kO)[:, kv, cs]
                          elif lt == 0:
                              kk = (kE0 if gq % 2 == 0 else kO0)[:, kv, :]
                          else:
                              kk = (kE if gq % 2 == 0 else kO)[:, kv, (lt - 1) * 128:lt * 128]
                          last = pe.matmul(bk[:, gq * 128:(gq + 1) * 128], kk, qT[:, c, cs], start=False, stop=(gq == 3))
                  return last
              S.task("pe", tsc_, reads=[R("kE"), R("kO"), R("qT"), R("k0"), Rc], writes=[x[2] for x in sbk])

              def tex(act, sbk=sbk):
                  for j, bk, bR in sbk:
                      last = act.activation(out=pT[:, j, :], in_=bk[:, :], func=AF.Exp, scale=0.125)
                  return last
              S.task("act", tex, reads=[x[2] for x in sbk], writes=[R("pT%d" % pp)])
              attn_ctx[u] = sbk
          def att_s2(u, lt, kv):
              samp = (lt == 4)
              ci = 4 * g + lt
              cs = slice(lt * 128, (lt + 1) * 128)
              has_prev = (not samp) and ci > 0
              pp = u % 3
              pT = pTB[pp]
              sbk = attn_ctx.pop(u)
              pv, pvR = nb()

              def tpv(pe, pv=pv, sbk=sbk, kv=kv, lt=lt, samp=samp):
                  o = pv[:, 0:512]
                  first = True
                  for j, bk, bR in sbk:
                      if j == 1:
                          va = vaug[:, lt, kv, 0:128]
                      elif lt == 0:
                          va = va0[:, kv, 0:128]
                      else:
                          va = vaug[:, lt - 1, kv, 0:128]
                      pe.matmul(o, va, pT[:, j, :], start=first, stop=False)
                      first = False
                  if samp:
                      for b in range(16):
                          for gq in range(4):
                              pe.matmul(o[:, gq * 128 + 8 * b:gq * 128 + 8 * b + 8], vca[:, b, 0:128],
                                        pTc[:, b * 32 + gq * 8:b * 32 + gq * 8 + 8], start=False, stop=False)
                  return pe.matmul(o, CB("sinkE"), esr[:, 4 * kv:4 * kv + 4, :], start=False, stop=True)
              S.task("pe", tpv, reads=[R("pT%d" % pp), R("pTc"), R("vaug"), R("vca"), R("vca1"), R("k0"), R("esr"), Rc], writes=[pvR])

              def tno0(act, pv=pv):
                  return act.activation(out=rden[64:128, 0:512], in_=pv[64:128, 0:512], func=AF.Ln)

              def tno1(act):
                  return act.activation(out=rden[64:128, 0:512], in_=rden[64:128, 0:512], func=AF.Exp, scale=-1.0)

              def tno(dve, pv=pv, kv=kv, cs=cs):
                  pv3 = pv[0:64, 0:512].rearrange("p (g q) -> p g q", g=4)
                  rd3 = rden[64:128, 0:512].rearrange("p (g q) -> p g q", g=4)
                  dve.tensor_tensor(out=catT[0:64, 2 * kv:2 * kv + 2, cs], in0=pv3[:, 0::2, :], in1=rd3[:, 0::2, :], op=ALU.mult)
                  return dve.tensor_tensor(out=catT[64:128, 2 * kv:2 * kv + 2, cs], in0=pv3[:, 1::2, :], in1=rd3[:, 1::2, :], op=ALU.mult)
              S.chain("act", [tno0, tno1], reads=[pvR], writes=[R("rden"), pvR])
              S.task("dve", tno, reads=[pvR, R("rden")], writes=[R("catT"), pvR])
          units = [(lt, kv) for lt in range(4) for kv in range(4)]
          att_s1(0, *units[0])
          att_s1(1, *units[1])
          for u in range(len(units)):
              if u + 2 < len(units):
                  att_s1(u + 2, *units[u + 2])
              att_s2(u, *units[u])
          if has_s:
              for kv in range(4):
                  att_s1(16 + kv, 4, kv)
                  att_s2(16 + kv, 4, kv)
          def tcar(act):
              act.activation(out=kE0[:, :, :], in_=kE[:, :, 384:512], func=AF.Copy)
              act.activation(out=kO0[:, :, :], in_=kO[:, :, 384:512], func=AF.Copy)
              return act.activation(out=va0[:, :, :], in_=vaug[:, 3, :, :], func=AF.Copy)
          S.task("act", tcar, reads=[R("kE"), R("kO"), R("vaug")], writes=[R("k0")])
          S.barrier(force=('D' in KBARS))
          if KDBG and g == 3:
              dma_load("sp", dbg[:, :, :], catT[:, :, 512:640], "dbg", [], reads=[R("catT")])
              S.barrier()
          chk('D', g)

          srcd = w_down.rearrange("(kc p) n -> p kc n", p=128)
          dma_multi("pool", [(wd[:, 0:11, :], srcd[:, 0:11, :]), (wd[:, 11:22, :], srcd[:, 11:22, :])], "wd", [R("wd")])
          def e_s1(lt):
              hb = hbB[lt % 2]
              src = xs[:, :] if lt == 4 else xp[(4 * g + lt) * 128:(4 * g + lt + 1) * 128, :]
              dma_load("sp", xres[:, lt, :], src, "xres%d" % lt, [R("xres%d" % lt)])
              mb = [nb(), nb()]

              def tmo(pe, mb=mb, lt=lt):
                  for hf in range(2):
                      for kc in range(16):
                          last = pe.matmul(mb[hf][0][:, :], catT[:, kc, lt * 128:(lt + 1) * 128], wo[:, kc, hf * 512:(hf + 1) * 512],
                                           start=(kc == 0), stop=(kc == 15))
                  return last
              S.task("pe", tmo, reads=[R("catT"), R("wo")], writes=[mb[0][1], mb[1][1]])

              S.task("dve", lambda dve: dve.memset(stat[:, 4:12], 0.0), writes=[R("stat")])

              def tmo2(act, mb=mb):
                  act.activation(out=mixs[:, 0:512], in_=mb[0][0][:, :], func=AF.Copy)
                  return act.activation(out=mixs[:, 512:1024], in_=mb[1][0][:, :], func=AF.Copy)
              S.chain("act", [tmo2, lambda act: act.activation(out=hb[:, :], in_=mixs[:, :], func=AF.Square, accum_out=stat[:, 4:5]),
                              lambda act: act.activation(out=stat[:, 5:6], in_=stat[:, 4:5], func=AF.Ln, scale=1.0 / D, bias=EPS),
                              lambda act: act.activation(out=stat[:, 6:7], in_=stat[:, 5:6], func=AF.Exp, scale=-0.5)],
                      reads=[mb[0][1], mb[1][1]], writes=[R("mixs"), R("hb%d" % (lt % 2)), R("stat")])
              S.chain("dve", [lambda dve: dve.scalar_tensor_tensor(out=mixs[:, :], in0=mixs[:, :], scalar=stat[:, 6:7], in1=gpost, op0=ALU.mult, op1=ALU.mult),
                              lambda dve, lt=lt: dve.tensor_tensor(out=xres[:, lt, :], in0=xres[:, lt, :], in1=mixs[:, :], op=ALU.add)],
                      reads=[R("mixs"), R("stat"), Rc], writes=[R("mixs"), R("stat"), R("xres%d" % lt)])
              S.chain("act", [lambda act, lt=lt: act.activation(out=hb[:, :], in_=xres[:, lt, :], func=AF.Square, accum_out=stat[:, 8:9]),
                              lambda act: act.activation(out=stat[:, 9:10], in_=stat[:, 8:9], func=AF.Ln, scale=1.0 / D, bias=EPS),
                              lambda act: act.activation(out=stat[:, 10:11], in_=stat[:, 9:10], func=AF.Exp, scale=-0.5)],
                      reads=[R("xres%d" % lt)], writes=[R("hb%d" % (lt % 2)), R("stat")])
              S.chain("dve", [lambda dve, lt=lt: dve.scalar_tensor_tensor(out=hb[:, :], in0=xres[:, lt, :], scalar=stat[:, 10:11], in1=gffn, op0=ALU.mult, op1=ALU.mult)],
                      reads=[R("xres%d" % lt), R("stat"), Rc], writes=[R("stat"), R("hb%d" % (lt % 2))])
          def e_s2(lt):
              hb = hbB[lt % 2]
              bk, bR = nb()

              def tmo6(pe, bk=bk):
                  for kc in range(8):
                      last = pe.transpose(bk[:, kc * 64:(kc + 1) * 64].bitcast(BF16), hb[:, kc * 128:(kc + 1) * 128], CB("ident"))
                  return last
              S.task("pe", tmo6, reads=[R("hb%d" % (lt % 2)), Rc], writes=[bR])

              def tmo7(act, bk=bk, lt=lt):
                  return act.activation(out=hT[:, :, lt * 128:(lt + 1) * 128],
                                        in_=bk[:, :].bitcast(BF16).rearrange("p (c t) -> p c t", c=8), func=AF.Copy)
              S.task("act", tmo7, reads=[bR], writes=[Rh[lt]])
          e_s1(0)
          for lt in range(ntile):
              if lt + 1 < ntile:
                  e_s1(lt + 1)
              e_s2(lt)
          S.barrier(force=('E' in KBARS))
          chk('E', g)

          for m in range(22):
              if m % 2 == 0:
                  wgb_, Rg = load_big(w_gate, m * 128)
                  wub_, Ru = load_big(w_up, m * 128)
              wg_ = wgb_[:, :, (m % 2) * 128:(m % 2) * 128 + 128]
              wu_ = wub_[:, :, (m % 2) * 128:(m % 2) * 128 + 128]
              for (r0, r1) in ranges:
                  n = r1 - r0
                  bg, bgR = nb()
                  bu, buR = nb()

                  def tf(pe, bg=bg, bu=bu, wg_=wg_, wu_=wu_, r0=r0, r1=r1, n=n):
                      for kc in range(8):
                          pe.matmul(bg[:, 0:n], wg_[:, kc, :], hT[:, kc, r0:r1], start=(kc == 0), stop=(kc == 7))
                      for kc in range(8):
                          last = pe.matmul(bu[:, 0:n], wu_[:, kc, :], hT[:, kc, r0:r1], start=(kc == 0), stop=(kc == 7))
                      return last
                  S.task("pe", tf, reads=[Rg, Ru] + Rh, writes=[bgR, buR])

                  def tf2(act, bg=bg, n=n):
                      return act.activation(out=sg[:, 0:n], in_=bg[:, 0:n], func=AF.Silu)
                  S.task("act", tf2, reads=[bgR], writes=[R("sg")])

                  def tf3(dve, bu=bu, m=m, r0=r0, r1=r1, n=n):
                      return dve.tensor_tensor(out=hff[:, m, r0:r1], in0=sg[:, 0:n], in1=bu[:, 0:n], op=ALU.mult)
                  S.task("dve", tf3, reads=[buR, R("sg")], writes=[R("hff")])
          ga = {}
          ntn = 0 if g == 3 else (5 if g + 1 == 3 else 4)
          for lt in range(ntile):
              if lt < ntn:
                  ga[lt] = a_tile(g + 1, lt)
                  next(ga[lt])
              mb = [nb(), nb()]

              def td(pe, mb=mb, lt=lt):
                  for hf in range(2):
                      for kc in range(22):
                          last = pe.matmul(mb[hf][0][:, :], hff[:, kc, lt * 128:(lt + 1) * 128], wd[:, kc, hf * 512:(hf + 1) * 512],
                                           start=(kc == 0), stop=(kc == 21))
                  return last
              S.task("pe", td, reads=[R("hff"), R("wd")], writes=[mb[0][1], mb[1][1]])

              S.task("dve", lambda dve: dve.memset(stat[:, 12:13], 0.0), writes=[R("stat")])

              def td2(act, mb=mb):
                  act.activation(out=dns[:, 0:512], in_=mb[0][0][:, :], func=AF.Copy)
                  return act.activation(out=dns[:, 512:1024], in_=mb[1][0][:, :], func=AF.Copy)
              S.chain("act", [td2, lambda act, lt=lt: act.activation(out=hbB[1][:, :], in_=dns[:, :], func=AF.Square, accum_out=stat[:, 12:13]),
                              lambda act: act.activation(out=stat[:, 13:14], in_=stat[:, 12:13], func=AF.Ln, scale=1.0 / D, bias=EPS),
                              lambda act: act.activation(out=stat[:, 14:15], in_=stat[:, 13:14], func=AF.Exp, scale=-0.5)],
                      reads=[mb[0][1], mb[1][1]], writes=[R("mixs"), R("hb1"), R("stat")])
              S.chain("dve", [lambda dve: dve.scalar_tensor_tensor(out=dns[:, :], in0=dns[:, :], scalar=stat[:, 14:15], in1=gpff, op0=ALU.mult, op1=ALU.mult),
                              lambda dve, lt=lt: dve.tensor_tensor(out=xres[:, lt, :], in0=xres[:, lt, :], in1=dns[:, :], op=ALU.add)],
                      reads=[R("mixs"), R("stat"), Rc], writes=[R("mixs"), R("stat"), R("xres%d" % lt)])
              dst = y_s[:, :] if lt == 4 else y_p[(4 * g + lt) * 128:(4 * g + lt + 1) * 128, :]
              dma_load("sp", dst, xres[:, lt, :], "yout%d" % lt, [], reads=[R("xres%d" % lt)])
              if lt in ga:
                  for _ in ga[lt]:
                      pass
          for lt_ in range(ntile, ntn):
              for _ in a_tile(g + 1, lt_):
                  pass
          S.barrier(force=('F' in KBARS))

    except _Stop:
        pass
    S.barrier(force=True)
    S.emit()
    return nc


def _vecs(inp):
    v = np.zeros((128, 140), np.float32)
    v[:, 0:16] = inp["dt_bias"][0][None, :]
    v[:, 16:32] = inp["a_log"][0][None, :]
    v[:, 32:48] = inp["attn_sinks"][0][None, :]
    cw = inp["conv_w"][0]
    v[:, 48:96] = cw.reshape(4, 12, 128).transpose(2, 1, 0).reshape(128, 48)
    v[:, 96:108] = inp["conv_b"][0].reshape(12, 128).T
    v[:, 108:116] = inp["g_ssm_out"][0].reshape(8, 128).T
    v[:, 116:124] = np.repeat(inp["d_skip"][0].reshape(8, 2, 1), 64, axis=2).reshape(8, 128).T
    return v


_NC_CACHE = {}


def kernel(**inp):
    inp = {k: np.asarray(v) for k, v in inp.items()}
    if "nc" not in _NC_CACHE:
        _NC_CACHE["nc"] = build_nc()
    nc = _NC_CACHE["nc"]
    cf, cb, cos_t, sin_t = _host_consts()
    gt = np.concatenate([np.broadcast_to(inp[k][0][None, :], (128, D)) for k in ("g_pre_mix", "g_post_mix", "g_pre_ffn", "g_post_ffn")], axis=1)
    gt = np.ascontiguousarray(gt, dtype=np.float32)
    vecs = _vecs(inp)
    shared = {"w_in": np.ascontiguousarray(inp["w_in"][0]), "w_out": np.ascontiguousarray(inp["w_out"][0]),
              "w_gate": np.ascontiguousarray(inp["w_gate"][0]), "w_up": np.ascontiguousarray(inp["w_up"][0]),
              "w_down": np.ascontiguousarray(inp["w_down"][0]), "gtiles": gt, "vecs": vecs, "cf": cf, "cb": cb,
              "cos_t": cos_t, "sin_t": sin_t}
    in_maps = []
    for c in range(8):
        sl = slice(16 * c, 16 * c + 16)
        m = dict(shared)
        m["xp"] = np.ascontiguousarray(inp["x_prompt"][c])
        m["xs"] = np.ascontiguousarray(inp["x_sample"][sl].reshape(128, D))
        m["ck"] = np.ascontiguousarray(inp["cache_k_win"][0, sl])
        m["cv"] = np.ascontiguousarray(inp["cache_v_win"][0, sl])
        m["sconv"] = np.ascontiguousarray(inp["state_conv"][0, sl].reshape(48, 1536))
        m["sssm"] = np.ascontiguousarray(inp["state_ssm"][0, sl].reshape(16, 1024, 128))
        in_maps.append(m)
    res = run_bass_kernel_spmd(nc, in_maps, core_ids=list(range(8)))
    r = res.results

    def cat(name, shape):
        return np.stack([np.asarray(r[c][name]) for c in range(8)], 0).reshape(shape).astype(np.float32)
    y_p = cat("y_p", (8, 2048, D))
    y_s = cat("y_s", (128, 8, D))
    nk_p = cat("nk_p", (1, 8, 128, 4, 64))
    nv_p = cat("nv_p", (1, 8, 128, 4, 64))
    ncv_p = cat("ncv_p", (1, 8, 3, 1536))
    nssm_p = cat("nssm_p", (1, 8, 16, 64, 128))
    nk_s = cat("nk_s", (1, 128, 128, 4, 64))
    nv_s = cat("nv_s", (1, 128, 128, 4, 64))
    ncv_s = cat("ncv_s", (1, 128, 3, 1536))
    nssm_s = cat("nssm_s", (1, 128, 16, 64, 128))
    return (y_p, y_s, nk_p, nv_p, ncv_p, nssm_p, nk_s, nv_s, ncv_s, nssm_s)
```
